# Optimizing a Trainium2 kernel written in Bass

```python
import jax, jax.numpy as jnp
from jax import lax
import numpy as np

D_MODEL = 1024
BATCH = 2
SEQ = 8192
DEPTH = 2

MIX_WIDTH = D_MODEL
N_MIXERS = 4
GROUP_WIDTH = MIX_WIDTH // N_MIXERS
HEADS_PER_GROUP = 4
HEAD_DIM = GROUP_WIDTH // HEADS_PER_GROUP
SGU_CHUNK = 128
SC_WIDTH = 3
DN_CONV_WIDTH = 4
DN_CHUNK = 64
GLA_CHUNK = 64
GLA_GATE_RANK = 16
GLA_GATE_TEMP = 16.0
D_FF = ((8 * D_MODEL // 3 + 255) // 256) * 256
EPS = 1e-6

_G = GROUP_WIDTH
_H = HEADS_PER_GROUP
IN_SPLITS = (_G, _G,
             _G, _G, _G,
             _G, _G, _G, _H, _H, _G,
             _G, _G, _G, GLA_GATE_RANK, _G)
IN_COLS = sum(IN_SPLITS)

kernel_name = "hybrid_sgu_shortconv_gdn_gla"


def rmsnorm(x, w):
    xf = x.astype(jnp.float32)
    y = xf * lax.rsqrt(jnp.mean(xf * xf, axis=-1, keepdims=True) + EPS)
    return (y * w.astype(jnp.float32)).astype(x.dtype)


def layernorm(x, w, b):
    xf = x.astype(jnp.float32)
    mu = jnp.mean(xf, axis=-1, keepdims=True)
    var = jnp.mean(jnp.square(xf - mu), axis=-1, keepdims=True)
    y = (xf - mu) * lax.rsqrt(var + EPS)
    return (y * w.astype(jnp.float32) + b.astype(jnp.float32)).astype(x.dtype)


def l2norm(t):
    return t * lax.rsqrt(jnp.sum(t * t, axis=-1, keepdims=True) + EPS)


def causal_dwconv(x, w):
    k_width, ch = w.shape
    return lax.conv_general_dilated(
        x, w[:, None, :].astype(x.dtype), window_strides=(1,),
        padding=[(k_width - 1, 0)], dimension_numbers=('NWC', 'WIO', 'NWC'),
        feature_group_count=ch)


def to_heads(t):
    b, s, _ = t.shape
    return t.reshape(b, s, HEADS_PER_GROUP, HEAD_DIM).astype(jnp.float32)


def to_chunks(t, c):
    b, s = t.shape[:2]
    t = t.reshape(b, s // c, c, *t.shape[2:])
    return jnp.moveaxis(t, 3, 1)


def from_chunks(t):
    b, h, n, c, d = t.shape
    return jnp.moveaxis(t, 1, 3).reshape(b, n * c, h, d)


def sgu_mixer(u, v, ln_w, ln_b, w_s, b_s):
    bsz, s, g = u.shape
    n = s // SGU_CHUNK
    u = jax.nn.gelu(u)
    v = layernorm(jax.nn.gelu(v), ln_w, ln_b)
    vc = v.reshape(bsz, n, SGU_CHUNK, HEADS_PER_GROUP, HEAD_DIM)
    mask = jnp.tril(jnp.ones((SGU_CHUNK, SGU_CHUNK), dtype=bool))
    ws = jnp.where(mask, w_s, 0.0).astype(v.dtype)
    mixed = jnp.einsum('hts,bnshd->bnthd', ws, vc) + b_s.T.astype(v.dtype)[None, None, :, :, None]
    return u * mixed.reshape(bsz, s, g)


def short_conv_mixer(gate_b, gate_c, h, w_conv):
    return gate_b * causal_dwconv(gate_c * h, w_conv)


def chunk_gated_delta_rule(q, k, v, g, beta):
    bsz, s, h, dk = q.shape
    dv = v.shape[-1]
    c = DN_CHUNK
    q, k, v = to_chunks(q, c), to_chunks(k, c), to_chunks(v, c)
    g, beta = to_chunks(g, c), to_chunks(beta, c)
    q = q * dk ** -0.5
    gc = jnp.cumsum(g, axis=-1)
    causal = jnp.tril(jnp.ones((c, c), dtype=bool))
    strict = jnp.tril(jnp.ones((c, c), dtype=bool), k=-1)
    decay = jnp.exp(jnp.where(causal, gc[..., :, None] - gc[..., None, :], -jnp.inf))
    kb = k * beta[..., None]
    low = jnp.where(strict, jnp.einsum('bhncd,bhnsd->bhncs', kb, k) * decay, 0.0)
    rhs = jnp.concatenate([v * beta[..., None], kb * jnp.exp(gc)[..., None]], axis=-1)
    sol = lax.linalg.triangular_solve(low, rhs, left_side=True, lower=True, unit_diagonal=True)
    u, w = sol[..., :dv], sol[..., dv:]
    attn = jnp.einsum('bhncd,bhnsd->bhncs', q, k) * decay
    qg = q * jnp.exp(gc)[..., None]
    k_dec = k * jnp.exp(gc[..., -1:] - gc)[..., None]
    chunk_dec = jnp.exp(gc[..., -1])

    def step(state, inp):
        qg_n, w_n, u_n, attn_n, kd_n, dec_n = inp
        v_new = u_n - jnp.einsum('bhcd,bhde->bhce', w_n, state)
        o = jnp.einsum('bhcd,bhde->bhce', qg_n, state) + jnp.einsum('bhcs,bhse->bhce', attn_n, v_new)
        state = state * dec_n[..., None, None] + jnp.einsum('bhcd,bhce->bhde', kd_n, v_new)
        return state, o

    s0 = jnp.zeros((bsz, h, dk, dv), jnp.float32)
    xs = tuple(jnp.moveaxis(t, 2, 0) for t in (qg, w, u, attn, k_dec, chunk_dec))
    _, o = lax.scan(step, s0, xs)
    return from_chunks(jnp.moveaxis(o, 0, 2))


def deltanet_mixer(q, k, v, a, b, z, conv_w, a_log, dt_bias, norm_w):
    bsz, s, _ = q.shape
    qkv = jax.nn.silu(causal_dwconv(jnp.concatenate([q, k, v], axis=-1), conv_w))
    q, k, v = jnp.split(qkv, 3, axis=-1)
    q, k, v = l2norm(to_heads(q)), l2norm(to_heads(k)), to_heads(v)
    g = -jnp.exp(a_log.astype(jnp.float32)) * jax.nn.softplus(a.astype(jnp.float32) + dt_bias.astype(jnp.float32))
    beta = jax.nn.sigmoid(b.astype(jnp.float32))
    o = chunk_gated_delta_rule(q, k, v, g, beta)
    o = rmsnorm(o, norm_w) * jax.nn.silu(to_heads(z))
    return o.reshape(bsz, s, GROUP_WIDTH).astype(z.dtype)


def chunk_gla(q, k, v, log_a):
    bsz, s, h, dk = q.shape
    dv = v.shape[-1]
    c = GLA_CHUNK
    q, k, v, log_a = (to_chunks(t, c) for t in (q, k, v, log_a))
    q = q * dk ** -0.5
    gcum = jnp.cumsum(log_a, axis=3)
    g_mid = gcum[:, :, :, c // 2:c // 2 + 1, :]
    qa = q * jnp.exp(gcum - g_mid)
    ka = k * jnp.exp(g_mid - gcum)
    causal = jnp.tril(jnp.ones((c, c), dtype=bool))
    attn = jnp.where(causal, jnp.einsum('bhncd,bhnsd->bhncs', qa, ka), 0.0)
    o_intra = jnp.einsum('bhncs,bhnse->bhnce', attn, v)
    qg = q * jnp.exp(gcum)
    k_last = k * jnp.exp(gcum[:, :, :, -1:, :] - gcum)
    chunk_dec = jnp.exp(gcum[:, :, :, -1, :])

    def step(state, inp):
        qg_n, kl_n, v_n, dec_n = inp
        o = jnp.einsum('bhcd,bhde->bhce', qg_n, state)
        state = state * dec_n[..., :, None] + jnp.einsum('bhcd,bhce->bhde', kl_n, v_n)
        return state, o

    s0 = jnp.zeros((bsz, h, dk, dv), jnp.float32)
    xs = tuple(jnp.moveaxis(t, 2, 0) for t in (qg, k_last, v, chunk_dec))
    _, o_inter = lax.scan(step, s0, xs)
    return from_chunks(o_intra + jnp.moveaxis(o_inter, 0, 2))


def gla_mixer(q, k, v, g_lr, z, w_gate2, gate_bias, norm_w):
    bsz, s, _ = q.shape
    pre = jnp.einsum('btr,rg->btg', g_lr, w_gate2.astype(g_lr.dtype)).astype(jnp.float32) + gate_bias.astype(jnp.float32)
    log_a = jax.nn.log_sigmoid(pre) / GLA_GATE_TEMP
    o = chunk_gla(to_heads(q), to_heads(k), to_heads(v), to_heads(log_a))
    o = rmsnorm(o, norm_w) * jax.nn.silu(to_heads(z))
    return o.reshape(bsz, s, GROUP_WIDTH).astype(z.dtype)


def swiglu(h, w_gate_up, w_down):
    gu = h @ w_gate_up
    gate, up = jnp.split(gu, 2, axis=-1)
    return (jax.nn.silu(gate) * up) @ w_down


def setup_inputs(seed: int = 0) -> dict:
    key = jax.random.key(seed)
    ks = jax.random.split(key, 24)
    f32 = jnp.float32

    def nrm(k, shape, scale):
        return jax.random.normal(k, shape, f32) * scale

    def gain(k, shape):
        return 1.0 + 0.02 * jax.random.normal(k, shape, f32)

    dt = jnp.exp(jax.random.uniform(ks[11], (DEPTH, HEADS_PER_GROUP), f32, np.log(1e-3), np.log(1e-1)))
    return {
        "x": jax.random.normal(ks[0], (BATCH, SEQ, D_MODEL), f32),
        "norm1_w": gain(ks[1], (DEPTH, D_MODEL)),
        "w_in": nrm(ks[2], (DEPTH, D_MODEL, IN_COLS), D_MODEL ** -0.5),
        "sgu_ln_w": gain(ks[3], (DEPTH, GROUP_WIDTH)),
        "sgu_ln_b": nrm(ks[4], (DEPTH, GROUP_WIDTH), 0.02),
        "sgu_w_spatial": nrm(ks[5], (DEPTH, HEADS_PER_GROUP, SGU_CHUNK, SGU_CHUNK), SGU_CHUNK ** -0.5),
        "sgu_b_spatial": gain(ks[6], (DEPTH, HEADS_PER_GROUP, SGU_CHUNK)),
        "sc_conv_w": nrm(ks[7], (DEPTH, SC_WIDTH, GROUP_WIDTH), SC_WIDTH ** -0.5),
        "dn_conv_w": nrm(ks[8], (DEPTH, DN_CONV_WIDTH, 3 * GROUP_WIDTH), DN_CONV_WIDTH ** -0.5),
        "dn_a_log": jnp.log(jax.random.uniform(ks[9], (DEPTH, HEADS_PER_GROUP), f32, 1.0, 16.0)),
        "dn_dt_bias": dt + jnp.log(-jnp.expm1(-dt)),
        "dn_norm_w": gain(ks[10], (DEPTH, HEAD_DIM)),
        "gla_w_gate2": nrm(ks[12], (DEPTH, GLA_GATE_RANK, GROUP_WIDTH), GLA_GATE_RANK ** -0.5),
        "gla_gate_bias": nrm(ks[13], (DEPTH, GROUP_WIDTH), 0.1),
        "gla_norm_w": gain(ks[14], (DEPTH, HEAD_DIM)),
        "w_out": nrm(ks[15], (DEPTH, MIX_WIDTH, D_MODEL), MIX_WIDTH ** -0.5),
        "norm2_w": gain(ks[16], (DEPTH, D_MODEL)),
        "w_gate_up": nrm(ks[17], (DEPTH, D_MODEL, 2 * D_FF), D_MODEL ** -0.5),
        "w_down": nrm(ks[18], (DEPTH, D_FF, D_MODEL), D_FF ** -0.5),
        "final_norm_w": gain(ks[19], (D_MODEL,)),
    }


def reference(x, norm1_w, w_in, sgu_ln_w, sgu_ln_b, sgu_w_spatial, sgu_b_spatial, sc_conv_w,
              dn_conv_w, dn_a_log, dn_dt_bias, dn_norm_w, gla_w_gate2, gla_gate_bias, gla_norm_w,
              w_out, norm2_w, w_gate_up, w_down, final_norm_w):
    split_idx = [int(i) for i in np.cumsum(IN_SPLITS)[:-1]]
    for l in range(DEPTH):
        h = rmsnorm(x, norm1_w[l])
        p = h @ w_in[l]
        (a_u, a_v, b_b, b_c, b_h, c_q, c_k, c_v, c_a, c_b, c_z,
         d_q, d_k, d_v, d_g, d_z) = jnp.split(p, split_idx, axis=-1)
        y_a = sgu_mixer(a_u, a_v, sgu_ln_w[l], sgu_ln_b[l], sgu_w_spatial[l], sgu_b_spatial[l])
        y_b = short_conv_mixer(b_b, b_c, b_h, sc_conv_w[l])
        y_c = deltanet_mixer(c_q, c_k, c_v, c_a, c_b, c_z, dn_conv_w[l], dn_a_log[l], dn_dt_bias[l], dn_norm_w[l])
        y_d = gla_mixer(d_q, d_k, d_v, d_g, d_z, gla_w_gate2[l], gla_gate_bias[l], gla_norm_w[l])
        mix = jnp.concatenate([y_a, y_b.astype(x.dtype), y_c, y_d], axis=-1)
        x = x + (mix @ w_out[l]).astype(x.dtype)
        x = x + swiglu(rmsnorm(x, norm2_w[l]), w_gate_up[l], w_down[l]).astype(x.dtype)
    return rmsnorm(x, final_norm_w)
```

```python
import numpy as np
import concourse.bass as bass
import concourse.mybir as mybir
from concourse.bass_utils import run_bass_kernel_spmd

F32 = mybir.dt.float32
BF16 = mybir.dt.bfloat16
AF = mybir.ActivationFunctionType
ALU = mybir.AluOpType

D_MODEL = 1024
SEQ = 8192
NCORES = 8
NT = 2048
G = 512
NG = NT // G
D_FF = 2816
NJ = D_FF // 128
IN_COLS = 3352
EPS = 1e-6
SB_BASE = 16512
SB_END = 229376
GRAN = 256

C_AU, C_AV = 0, 256
C_BB, C_BC, C_BH = 512, 768, 1024
C_CQ, C_CK, C_CV, C_CA, C_CB, C_CZ = 1280, 1536, 1792, 2048, 2052, 2056
C_DQ, C_DK, C_DV, C_DG, C_DZ = 2312, 2568, 2824, 3080, 3096


DBG = {}


class V:
    __slots__ = ("ap", "keys")

    def __init__(self, ap, keys):
        self.ap = ap
        self.keys = keys

    def b(self, shape):
        return V(self.ap.broadcast_to(list(shape)), self.keys)

    def __getitem__(self, idx):
        return V(self.ap[idx], self.keys)


class T:
    def __init__(self, space, name, ap, off, shape, esz):
        self.space, self.name, self.ap, self.off, self.shape, self.esz = space, name, ap, off, shape, esz
        st = [1] * len(shape)
        for i in range(len(shape) - 2, 0, -1):
            st[i] = st[i + 1] * shape[i + 1]
        self.st = st

    def __getitem__(self, idx):
        if not isinstance(idx, tuple):
            idx = (idx,)
        lo = 0
        hi = 0
        full = list(idx) + [slice(None)] * (len(self.shape) - len(idx))
        for d in range(1, len(self.shape)):
            ix = full[d]
            if isinstance(ix, slice):
                a = 0 if ix.start is None else ix.start
                b_ = self.shape[d] if ix.stop is None else ix.stop
            else:
                a, b_ = ix, ix + 1
            lo += a * self.st[d]
            hi += (b_ - 1) * self.st[d]
        b0 = self.off + lo * self.esz
        b1 = self.off + (hi + 1) * self.esz
        if self.space == "ps":
            keys = (("ps", self.off // 2048),)
        else:
            keys = tuple((self.space, g) for g in range(b0 // GRAN, (b1 - 1) // GRAN + 1))
        return V(self.ap[idx], keys)

    @property
    def all(self):
        return self[(slice(None),)]


class Op:
    __slots__ = ("eng", "fn", "reads", "writes", "dma", "eidx", "deps", "sig", "count", "sem", "waits", "cc")

    def __init__(self, eng, fn, reads, writes, dma):
        self.eng, self.fn, self.reads, self.writes, self.dma = eng, fn, reads, writes, dma
        self.deps = []
        self.sig = False
        self.count = 0
        self.sem = None
        self.waits = []
        self.cc = False


ENG_ATTR = {"pe": "tensor", "act": "scalar", "dve": "vector", "pool": "gpsimd", "sp": "sync"}
OUT_NAMES = ("out", "accum_out", "ap")


class Prog:
    def __init__(self, nc):
        self.nc = nc
        self.ops = []
        self.sb_off = SB_BASE
        self.ps_banks = 0
        self.marks = []

    def sb(self, name, shape, dt, at=None):
        esz = 2 if dt == BF16 else 4
        n = 1
        for s in shape[1:]:
            n *= s
        nbytes = ((n * esz + 63) // 64) * 64
        if at is None:
            off = self.sb_off
            self.sb_off += nbytes
        else:
            off = at
        assert off + nbytes <= SB_END, ("SBUF overflow", name, off, nbytes)
        h = self.nc.alloc_sbuf_tensor_at(name, list(shape), dt, offset=off)
        return T("sb", name, h.ap(), off, list(shape), esz)

    def ps(self, name, shape, dt):
        h = self.nc.alloc_psum_tensor(name, list(shape), dt)
        esz = 2 if dt == BF16 else 4
        off = self.ps_banks * 2048
        self.ps_banks += 1
        return T("ps", name, h.ap(), off, list(shape), esz)

    def I(self, eng, method, **kw):
        reads, writes, args = [], [], {}
        for k, v in kw.items():
            if isinstance(v, V):
                args[k] = v.ap
                if k in OUT_NAMES:
                    writes.extend(v.keys)
                else:
                    reads.extend(v.keys)
                    writes.extend(kk for kk in v.keys if kk[0] == "ps")
            else:
                args[k] = v
        extra_r = args.pop("_r", None)
        dma = method == "dma_start"
        fn = (method, args)
        op = Op(eng, fn, tuple(reads), tuple(writes), dma)
        self.ops.append(op)
        return op

    def custom(self, eng, fn, reads=(), writes=(), dma=False, cc=False):
        op = Op(eng, ("custom", fn), tuple(reads), tuple(writes), dma)
        op.cc = cc
        self.ops.append(op)
        return op

    def finalize(self, nsem_dma=10):
        nc = self.nc
        last_w = {}
        readers = {}
        per_eng = {e: [] for e in ENG_ATTR}
        for i, op in enumerate(self.ops):
            op.eidx = len(per_eng[op.eng])
            per_eng[op.eng].append(op)
            deps = set()
            for k in op.reads:
                w = last_w.get(k)
                if w is not None:
                    deps.add(w)
            for k in op.writes:
                w = last_w.get(k)
                if w is not None:
                    deps.add(w)
                for r in readers.get(k, ()):
                    deps.add(r)
            deps.discard(op)
            op.deps = deps
            for k in op.reads:
                readers.setdefault(k, []).append(op)
            for k in op.writes:
                last_w[k] = op
                readers[k] = []
        for op in self.ops:
            need = []
            for d in op.deps:
                if d.dma:
                    need.append(d)
                    continue
                if d.eng == op.eng and not op.dma:
                    if op.eng == "pe":
                        continue
                    if op.eng in ("act", "dve") and op.eidx - d.eidx > 3:
                        continue
                need.append(d)
                d.sig = True
            op.deps = need
        with_sems = {}
        eng_sem = {}
        dma_sems = {}
        names = []
        for e in ENG_ATTR:
            eng_sem[e] = nc.alloc_semaphore("s_" + e)
            dma_sems[e] = [nc.alloc_semaphore("d_%s_%d" % (e, i)) for i in range(nsem_dma)] if e in ("sp", "pool", "act") else []
        for e, lst in per_eng.items():
            cnt = 0
            ndma = 0
            for op in lst:
                if op.cc:
                    op.sem = nc.alloc_semaphore("cc_%d" % op.eidx)
                    op.count = 1
                elif op.dma:
                    op.sem = dma_sems[e][ndma % nsem_dma]
                    prev = 16 * (ndma // nsem_dma)
                    op.count = prev + 16
                    if prev > 0:
                        op.waits.append((op.sem, prev))
                    ndma += 1
                elif op.sig:
                    cnt += 1
                    op.count = cnt
                    op.sem = eng_sem[e]
        for e, lst in per_eng.items():
            waited = {}
            for op in lst:
                ws = {}
                for (s, c) in op.waits:
                    ws[s] = max(ws.get(s, 0), c)
                for d in op.deps:
                    ws[d.sem] = max(ws.get(d.sem, 0), d.count)
                out = []
                for s, c in ws.items():
                    if waited.get(s, 0) >= c:
                        continue
                    waited[s] = c
                    out.append((s, c))
                op.waits = out
        self.nwaits = sum(len(op.waits) for op in self.ops)
        with nc.Block() as block:
            for e, lst in per_eng.items():
                if not lst:
                    continue

                def body(engobj, lst=lst):
                    for op in lst:
                        for (s, c) in op.waits:
                            engobj.wait_ge(s, c)
                        kind, payload = op.fn
                        if kind == "custom":
                            ins = payload(engobj)
                        elif kind == "nop":
                            ins = None
                        else:
                            ins = getattr(engobj, kind)(**payload)
                        if ins is not None:
                            if op.cc:
                                ins.then_inc(op.sem, 1)
                            elif op.dma:
                                ins.then_inc(op.sem, 16)
                            elif op.sig:
                                ins.then_inc(op.sem, 1)

                getattr(block, ENG_ATTR[e])(body)


def build(layers, local_only, final_norm, use_cc=False, dbg=None, skip=()):
    nc = bass.Bass("TRN2", target_bir_lowering=False, num_devices=NCORES) if use_cc else bass.Bass("TRN2", target_bir_lowering=False)
    P = Prog(nc)

    def din(name, shape):
        return nc.dram_tensor(name, list(shape), F32, kind="ExternalInput").ap()

    x_d = din("x", [NT, D_MODEL])
    xh_d = din("xh", [4, D_MODEL])
    w_in_d = din("w_in", [2, D_MODEL, IN_COLS])
    w_out_d = din("w_out", [2, D_MODEL, D_MODEL])
    w_gu_d = din("w_gate_up", [2, D_MODEL, 2 * D_FF])
    w_dn_d = din("w_down", [2, D_FF, D_MODEL])
    nrm_d = din("nrm", [5, D_MODEL])
    pcol_d = din("pcol", [2, 128, 40])
    p4_d = din("p4", [2, 4, 2])
    lnwb_d = din("lnwb", [2, 2, 256])
    wsp_d = din("wsp", [2, 4, 128, 128])
    bsp_d = din("bsp", [2, 4, 128])
    wg2_d = din("wg2", [2, 16, 256])
    cst_d = din("cst", [128, 11, 128])
    ehp_d = din("ehp", [4, 2, 128])
    rmask_d = din("rmask", [128, 512])
    gath_d = din("gath", [4, 128, 392])
    if local_only:
        pack_o = nc.dram_tensor("pack", [128, 392], F32, kind="ExternalOutput").ap()
    else:
        y_o = nc.dram_tensor("y", [NT, D_MODEL], F32, kind="ExternalOutput").ap()
    packd = nc.dram_tensor("packd", [128, 392], F32, kind="Internal").ap()
    gathd = nc.dram_tensor("gathd", [512, 392], F32, kind="Internal").ap()
    xsend = nc.dram_tensor("xsend", [4, D_MODEL], F32, kind="Internal").ap()
    xgath = nc.dram_tensor("xgath", [16, D_MODEL], F32, kind="Internal").ap()
    dfo_d = nc.dram_tensor("dfo", [NG, 128, 2, 2, G], F32, kind="Internal").ap()
    dfb_d = nc.dram_tensor("dfb", [NG, 128, 4, 2, G], BF16, kind="Internal").ap()
    DK = lambda n: (("dram", n),)

    def dv(ap, name=None):
        return V(ap, DK(name) if name else ())

    xs = P.sb("xs", [128, 16, 1024], F32)
    cst = P.sb("cst", [128, 11, 128], F32)
    identb = P.sb("identb", [128, 128], BF16)
    bonesb = P.sb("bonesb", [128, 128], BF16)
    ehp = P.sb("ehp", [4, 2, 128], F32)
    rmask = P.sb("rmask", [128, 512], F32)
    nwb = P.sb("nwb", [128, 1024], F32)
    pcol = P.sb("pcol", [128, 40], F32)
    p4 = P.sb("p4", [4, 4], F32)
    ss = P.sb("ss", [128, 16], F32)
    rstd = P.sb("rstd", [128, 16], F32)
    hT = P.sb("hT", [128, 8, G], BF16)
    xn = [P.sb("xn%d" % i, [128, 1024], BF16) for i in range(2)]
    junk = P.sb("junk", [128, 1024], BF16)
    wo = P.sb("wo", [128, 8, 1024], BF16)
    SinB = P.sb("SinB", [128, 2, 2, 64], BF16)
    SCR = P.sb_off
    ident = cst[:, 0, :]
    maskL = cst[:, 1, :]
    BDUi = cst[:, 2, :]
    BDUs = cst[:, 3, :]
    BDLs = cst[:, 4, :]
    ones_f = cst[:, 5, :]
    eye2 = cst[:, 7, 0:64]
    sel = cst[:, 8, 0:4]
    nBDUs = cst[:, 9, :]
    nBDLs = cst[:, 10, :]

    pb = [P.ps("pb%d" % i, [128, 512], F32) for i in range(7)]
    pT = P.ps("pT", [128, 8, 128], BF16)
    rot = {"i": 0}

    def rbank():
        b = pb[2 + rot["i"] % 4]
        rot["i"] += 1
        return b

    wide = {"i": 0}

    def wbank():
        b = pb[wide["i"] % 6]
        wide["i"] += 1
        return b

    mmi = {"i": 0}

    def mbank():
        b = pb[mmi["i"] % 2]
        mmi["i"] += 1
        return b

    P.I("sp", "dma_start", out=cst.all, in_=dv(cst_d))
    P.I("sp", "dma_start", out=ehp.all, in_=dv(ehp_d))
    P.I("sp", "dma_start", out=rmask.all, in_=dv(rmask_d))
    P.I("dve", "tensor_copy", out=identb.all, in_=ident)
    P.I("dve", "tensor_copy", out=bonesb.all, in_=cst[:, 6, :])
    for t in range(16):
        P.I("sp", "dma_start", out=xs[:, t, :], in_=dv(x_d[t * 128:(t + 1) * 128, :]))

    def norm_group(g, nidx_loaded):
        for tt in range(4):
            Tt = g * 4 + tt
            P.I("act", "activation", out=junk.all, in_=xs[:, Tt, :], func=AF.Square, accum_out=ss[:, Tt:Tt + 1])
            P.I("act", "activation", out=rstd[:, Tt:Tt + 1], in_=ss[:, Tt:Tt + 1], func=AF.Sqrt, bias=EPS, scale=1.0 / D_MODEL)
            P.I("dve", "reciprocal", out=rstd[:, Tt:Tt + 1], in_=rstd[:, Tt:Tt + 1])
            xb = xn[tt % 2]
            P.I("dve", "scalar_tensor_tensor", out=xb.all, in0=xs[:, Tt, :], scalar=rstd[:, Tt:Tt + 1], in1=nwb.all,
                op0=ALU.mult, op1=ALU.mult)
            for fc in range(8):
                P.I("pe", "transpose", out=pT[:, fc, :], in_=xb[:, fc * 128:(fc + 1) * 128], identity=identb.all)
            P.I("act", "activation", out=hT[:, :, tt * 128:(tt + 1) * 128], in_=pT.all, func=AF.Copy)

    def load_nw(idx):
        P.I("sp", "dma_start", out=nwb.all, in_=dv(nrm_d[idx:idx + 1, :].partition_broadcast(128)[:, 0, :]))

    def ffn_group(l, g, S):
        gT, wg, wd, sg = S["gT"], S["wg"], S["wd"], S["sg"]
        nblk = (NJ + 1) // 2
        for jb in range(nblk):
            j0 = jb * 2
            nj = min(2, NJ - j0)
            w = wg[jb % 3]
            P.I("pool", "dma_start", out=w[:, :, 0, 0:nj * 128],
                in_=dv(w_gu_d[l][:, j0 * 128:(j0 + nj) * 128].rearrange("(kc p) n -> p kc n", p=128)))
            P.I("pool", "dma_start", out=w[:, :, 1, 0:nj * 128],
                in_=dv(w_gu_d[l][:, D_FF + j0 * 128:D_FF + (j0 + nj) * 128].rearrange("(kc p) n -> p kc n", p=128)))
            for jj in range(nj):
                j = j0 + jj
                pa, pu = wbank(), wbank()
                for kc in range(8):
                    P.I("pe", "matmul", out=pa.all, lhsT=w[:, kc, 0, jj * 128:(jj + 1) * 128], rhs=hT[:, kc, :],
                        start=(kc == 0), stop=(kc == 7))
                for kc in range(8):
                    P.I("pe", "matmul", out=pu.all, lhsT=w[:, kc, 1, jj * 128:(jj + 1) * 128], rhs=hT[:, kc, :],
                        start=(kc == 0), stop=(kc == 7))
                s_ = sg[j % 2]
                P.I("act", "activation", out=s_.all, in_=pa.all, func=AF.Silu)
                P.I("dve", "tensor_tensor", out=gT[:, j, :], in0=s_.all, in1=pu.all, op=ALU.mult)
        for half in range(2):
            w = wd[half]
            P.I("pool", "dma_start", out=w.all,
                in_=dv(w_dn_d[l][:, half * 512:(half + 1) * 512].rearrange("(j p) n -> p j n", p=128)))
            for tt in range(4):
                Tt = g * 4 + tt
                pa = wbank()
                for j in range(NJ):
                    P.I("pe", "matmul", out=pa.all, lhsT=gT[:, j, tt * 128:(tt + 1) * 128], rhs=w[:, j, :],
                        start=(j == 0), stop=(j == NJ - 1))
                P.I("dve", "tensor_tensor", out=xs[:, Tt, half * 512:(half + 1) * 512],
                    in0=xs[:, Tt, half * 512:(half + 1) * 512], in1=pa.all, op=ALU.add)

    def alloc_ffn_at(_unused=None):
        o = {"i": SCR}

        def a(name, shape, dt):
            t = P.sb(name, shape, dt, at=o["i"])
            esz = 2 if dt == BF16 else 4
            n = 1
            for s in shape[1:]:
                n *= s
            o["i"] += ((n * esz + 63) // 64) * 64
            return t
        S = {}
        S["gT"] = a("gT", [128, NJ, G], BF16)
        S["wg"] = [a("wg%d" % i, [128, 8, 2, 256], BF16) for i in range(3)]
        S["wd"] = [a("wd%d" % i, [128, NJ, 512], BF16) for i in range(2)]
        S["sg"] = [a("sg%d" % i, [128, G], F32) for i in range(2)]
        return S


    class Arena:
        def __init__(self, base):
            self.o = base

        def a(self, name, shape, dt):
            t = P.sb(name, shape, dt, at=self.o)
            esz = 2 if dt == BF16 else 4
            n = 1
            for s_ in shape[1:]:
                n *= s_
            self.o += ((n * esz + 63) // 64) * 64
            return t

    AR = Arena(SCR)
    wi = [AR.a("wi%d" % i, [128, 8, 528], BF16) for i in range(2)]
    mixAB = AR.a("mixAB", [128, 4, G], BF16)
    hTh = AR.a("hTh", [128, 8, 4], BF16)
    WsT = AR.a("WsT", [128, 4, 128], BF16)
    bsb = AR.a("bsb", [128, 2, 128], F32)
    lnw = AR.a("lnw", [128, 2, 256], F32)
    wg2b = AR.a("wg2b", [16, 256], BF16)
    haloB = AR.a("haloB", [128, 2, 4], F32)
    haloC = AR.a("haloC", [128, 6, 4], F32)
    XC = AR.a("XC", [128, 2, 128], F32)
    SD = AR.a("SD", [128, 2, 64], F32)
    DcD = AR.a("DcD", [128, 2, 33], F32)
    packS = AR.a("packS", [128, 392], F32)
    R1 = AR.o
    S1 = Arena(R1)
    xht = S1.a("xht", [4, 1024], F32)
    xhn = S1.a("xhn", [4, 1024], BF16)
    gxh = S1.a("gxh", [16, 1024], F32)
    wsp = S1.a("wsp", [128, 4, 128], F32)
    A1 = Arena(R1)
    uA = A1.a("uA", [128, 2, G], F32)
    vgA = [A1.a("vgA%d" % i, [128, 256], F32) for i in range(2)]
    vlnA = [A1.a("vlnA%d" % i, [128, 256], BF16) for i in range(2)]
    stA = A1.a("stA", [128, 8], F32)
    tmpA = A1.a("tmpA", [128, 2, 128], F32)
    xgA = A1.a("xgA", [128, G], F32)
    tgA = A1.a("tgA", [128, G], F32)
    B1 = Arena(R1)
    gbB = B1.a("gbB", [128, 2, G], F32)
    gcB = B1.a("gcB", [128, G], F32)
    tB = B1.a("tB", [128, 2, G + 2], F32)
    accB = B1.a("accB", [128, G], F32)
    D1 = Arena(R1)
    qfD = D1.a("qfD", [128, 2, G], F32)
    kfD = D1.a("kfD", [128, 2, G], F32)
    vtD = D1.a("vtD", [128, 4, 256], BF16)
    glrD = D1.a("glrD", [16, G], BF16)
    zsD = D1.a("zsD", [128, 2, G], BF16)
    efD = D1.a("efD", [128, G], F32)
    csD = D1.a("csD", [128, 2, G], F32)
    d1D = D1.a("d1D", [128, G], F32)
    aeD = D1.a("aeD", [128, G], F32)
    a3D = D1.a("a3D", [128, 2, G], F32)
    qaD = D1.a("qaD", [128, 2, G], BF16)
    kaD = D1.a("kaD", [128, 2, G], BF16)
    qgD = D1.a("qgD", [128, 2, G], BF16)
    klD = D1.a("klD", [128, 2, G], BF16)
    qgDD = D1.a("qgDD", [128, 2, G], BF16)
    kltD = D1.a("kltD", [128, 4, 256], BF16)
    atD = [D1.a("atD%d" % i, [128, 4, 128], BF16) for i in range(2)]
    SbD = [D1.a("SbD%d" % i, [128, 2, 2, 64], BF16) for i in range(2)]
    ostD = D1.a("ostD", [128, 2, G], F32)
    R2 = R1
    C1 = Arena(R2)
    cbC = C1.a("cbC", [128, G + 3], F32)
    accC = C1.a("accC", [128, G], F32)
    qsC = accC
    sqC = C1.a("sqC", [128, G], BF16)
    rtC = C1.a("rtC", [128, G], F32)
    qTb = C1.a("qTb", [128, 2, G], BF16)
    kTb = C1.a("kTb", [128, 2, G], BF16)
    kbTb = C1.a("kbTb", [128, 2, G], BF16)
    qgTb = C1.a("qgTb", [128, 2, G], BF16)
    vTb = C1.a("vTb", [128, 2, G], BF16)
    zsC = C1.a("zsC", [128, 2, G], BF16)
    t1C = C1.a("t1C", [4, G], F32)
    gcC = C1.a("gcC", [4, G], F32)
    egC = C1.a("egC", [4, G], F32)
    btC = C1.a("btC", [4, G], F32)
    tokq = C1.a("tokq", [128, 4, 4, 4], F32)
    tokb = C1.a("tokb", [128, 4, 4], F32)
    decb = C1.a("decb", [128, 2, 8], F32)
    vbt = [C1.a("vbt%d" % i, [128, 4, 64], BF16) for i in range(2)]
    kbgt = [C1.a("kbgt%d" % i, [128, 4, 64], BF16) for i in range(2)]
    kdt = [C1.a("kdt%d" % i, [128, 4, 64], BF16) for i in range(2)]
    bufA = [C1.a("bufA%d" % i, [128, 4, 128], F32) for i in range(2)]
    bufB = [C1.a("bufB%d" % i, [128, 4, 128], F32) for i in range(2)]
    bufC = [C1.a("bufC%d" % i, [128, 4, 128], F32) for i in range(2)]
    Nb = [[C1.a("Nb%d_%d" % (j, i), [128, 4, 128], BF16) for i in range(2)] for j in range(2)]
    NTb = [[C1.a("NTb%d_%d" % (j, i), [128, 4, 128], BF16) for i in range(2)] for j in range(2)]
    atC = [C1.a("atC%d" % i, [128, 4, 128], BF16) for i in range(2)]
    TTb = [[C1.a("TTb%d_%d" % (j, i), [128, 4, 128], BF16) for i in range(2)] for j in range(2)]
    uaug = [C1.a("uaug%d" % i, [128, 4, 128], F32) for i in range(2)]
    wTb = [C1.a("wTb%d" % i, [128, 2, 128], BF16) for i in range(2)]
    _vnb = C1.a("vnb", [128, 4, 128], BF16)
    vnb = [_vnb, _vnb]
    _XbC = C1.a("XbC", [128, 2, 2, 128], BF16)
    XbC = [_XbC, _XbC]
    ostC = C1.a("ostC", [128, 2, G], F32)
    PTC = C1.a("PTC", [128, 2, G], BF16)
    END_LOCAL = C1.o
    Q1 = Arena(R1)
    gat = Q1.a("gat", [128, 4, 392], F32)
    curC = Q1.a("curC", [128, 2, 64], F32)
    curD = Q1.a("curD", [128, 2, 64], F32)
    accS = Q1.a("accS", [128, 2, 2, 64], F32)
    MTf = Q1.a("MTf", [64, 2, 128], F32)
    curF = Q1.a("curF", [64, 2, 64], F32)
    ost = Q1.a("ost", [128, 2, 2, G], F32)
    dbf = Q1.a("dbf", [128, 4, 2, G], BF16)
    ofQ = Q1.a("ofQ", [128, G], F32)
    sqQ = Q1.a("sqQ", [128, G], BF16)
    rtQ = Q1.a("rtQ", [128, G], F32)
    t1Q = Q1.a("t1Q", [128, G], F32)
    mixCD = Q1.a("mixCD", [128, 4, G], BF16)
    print("SBUF plan: SCR=%d R1=%d R2=%d endlocal=%d endpost=%d" % (SCR, R1, R2, END_LOCAL, Q1.o))
    SF = alloc_ffn_at(max(Q1.o, 0) if False else None)

    def rows(h):
        return slice((h % 2) * 64, (h % 2) * 64 + 64)

    def pc(col):
        return pcol[:, col:col + 1]

    def layer_setup(l):
        P.I("sp", "dma_start", out=pcol.all, in_=dv(pcol_d[l]))
        P.I("sp", "dma_start", out=p4[:, 0:2], in_=dv(p4_d[l]))
        P.I("dve", "tensor_scalar", out=pcol[:, 34:36], in0=pcol[:, 30:32], scalar1=-1.0, scalar2=None, op0=ALU.mult)
        P.I("act", "activation", out=p4[:, 2:3], in_=p4[:, 0:1], func=AF.Exp)
        P.I("dve", "tensor_scalar", out=p4[:, 2:3], in0=p4[:, 2:3], scalar1=-1.0, scalar2=None, op0=ALU.mult)
        for hf in range(2):
            P.I("pool", "dma_start", out=wo[:, hf * 4:(hf + 1) * 4, :],
                in_=dv(w_out_d[l][hf * 512:(hf + 1) * 512, :].rearrange("(kc p) n -> p kc n", p=128)))
        P.I("sp", "dma_start", out=wsp.all, in_=dv(wsp_d[l].rearrange("h t s -> t h s")))
        P.I("dve", "tensor_tensor", out=wsp.all, in0=wsp.all, in1=maskL[:, None, :].b([128, 4, 128]), op=ALU.mult)
        bk = rbank()
        for h in range(4):
            P.I("pe", "transpose", out=bk[:, h * 128:(h + 1) * 128], in_=wsp[:, h, :], identity=ident)
        P.I("act", "activation", out=WsT.all, in_=bk.all, func=AF.Copy)
        for h in range(4):
            P.I("sp", "dma_start", out=bsb[rows(h), h // 2, :], in_=dv(bsp_d[l][h:h + 1, :].partition_broadcast(64)[:, 0, :]))
        for i in range(2):
            P.I("sp", "dma_start", out=lnw[:, i, :], in_=dv(lnwb_d[l][i:i + 1, :].partition_broadcast(128)[:, 0, :]))
        P.I("pool", "dma_start", out=wg2b.all, in_=dv(wg2_d[l]))
        P.I("dve", "memset", ap=XC.all, constant=0.0)
        for hp in range(2):
            P.I("dve", "tensor_copy", out=XC[:, hp, 64:128], in_=eye2)
        P.I("dve", "memset", ap=SD.all, constant=0.0)
        P.I("dve", "memset", ap=DcD.all, constant=1.0)
        for i in range(2):
            P.I("dve", "memset", ap=uaug[i].all, constant=0.0)
        P.I("dve", "memset", ap=packS.all, constant=0.0)

    def halo_prep(l, from_x_dram):
        if from_x_dram is not None:
            P.I("sp", "dma_start", out=xht.all, in_=from_x_dram)
        else:
            P.I("sp", "dma_start", out=gxh.all, in_=dv(xgath, "xgath"))
            for half in range(2):
                bk = rbank()
                P.I("pe", "matmul", out=bk[0:4, :], lhsT=cst[0:16, 8, 4:8], rhs=gxh[:, half * 512:(half + 1) * 512], start=True, stop=True)
                P.I("act", "activation", out=xht[:, half * 512:(half + 1) * 512], in_=bk[0:4, :], func=AF.Copy)
        P.I("act", "activation", out=xhn.all, in_=xht.all, func=AF.Square, accum_out=p4[:, 3:4])
        P.I("act", "activation", out=p4[:, 3:4], in_=p4[:, 3:4], func=AF.Sqrt, bias=EPS, scale=1.0 / D_MODEL)
        P.I("dve", "reciprocal", out=p4[:, 3:4], in_=p4[:, 3:4])
        P.I("dve", "scalar_tensor_tensor", out=xhn.all, in0=xht.all, scalar=p4[:, 3:4], in1=nwb[0:4, :],
            op0=ALU.mult, op1=ALU.mult)
        for fc in range(8):
            P.I("pe", "transpose", out=pT[:, fc, 0:4], in_=xhn[:, fc * 128:(fc + 1) * 128], identity=identb[0:4, 0:4])
        P.I("act", "activation", out=hTh.all, in_=pT[:, :, 0:4], func=AF.Copy)

    def ip_fm(w, c0, n, rhs_of_kc=None, N=G):
        bk = mbank()
        for kc in range(8):
            P.I("pe", "matmul", out=bk[0:n, 0:N], lhsT=w[:, kc, c0:c0 + n],
                rhs=(hT[:, kc, :] if rhs_of_kc is None else rhs_of_kc(kc)), start=(kc == 0), stop=(kc == 7))
        return bk

    def ip_tm(w, c0, n, tt):
        bk = mbank()
        for kc in range(8):
            P.I("pe", "matmul", out=bk[:, 0:n], lhsT=hT[:, kc, tt * 128:(tt + 1) * 128], rhs=w[:, kc, c0:c0 + n],
                start=(kc == 0), stop=(kc == 7))
        return bk

    wslot = {"i": 0}

    def load_wi(l, c0, n):
        w = wi[wslot["i"] % 2]
        wslot["i"] += 1
        P.I("pool", "dma_start", out=w[:, :, 0:n], in_=dv(w_in_d[l][:, c0:c0 + n].rearrange("(kc p) n -> p kc n", p=128)))
        return w

    def gelu_tanh(out_v, ps_v, xg, tg):
        P.I("act", "activation", out=xg, in_=ps_v, func=AF.Copy)
        P.I("dve", "tensor_tensor", out=tg, in0=xg, in1=xg, op=ALU.mult)
        P.I("dve", "tensor_scalar", out=tg, in0=tg, scalar1=0.044715, scalar2=1.0, op0=ALU.mult, op1=ALU.add)
        P.I("dve", "tensor_tensor", out=tg, in0=tg, in1=xg, op=ALU.mult)
        P.I("act", "activation", out=tg, in_=tg, func=AF.Sigmoid, scale=1.5957691216057308)
        P.I("dve", "tensor_tensor", out=out_v, in0=xg, in1=tg, op=ALU.mult)

    def mixer_A(l, g, first):
        w = load_wi(l, 0, 512)
        for c in range(2):
            bk = ip_fm(w, C_AU + c * 128, 128)
            gelu_tanh(uA[:, c, :], bk.all, xgA.all, tgA.all)
        tmpA2 = V(tmpA.ap.rearrange("p c t -> p (c t)"), tmpA.all.keys)
        for tt in range(4):
            bk = ip_tm(w, C_AV, 256, tt)
            vg, vl = vgA[tt % 2], vlnA[tt % 2]
            gelu_tanh(vg.all, bk[:, 0:256], xgA[:, 0:256], tgA[:, 0:256])
            P.I("dve", "tensor_reduce", out=stA[:, 0:1], in_=vg.all, axis=mybir.AxisListType.X, op=ALU.add)
            P.I("act", "activation", out=stA[:, 4:5], in_=stA[:, 0:1], func=AF.Copy, scale=-1.0 / 256)
            P.I("dve", "tensor_scalar", out=vg.all, in0=vg.all, scalar1=stA[:, 4:5], scalar2=None, op0=ALU.add)
            P.I("act", "activation", out=tmpA2, in_=vg.all, func=AF.Square, accum_out=stA[:, 1:2])
            P.I("act", "activation", out=stA[:, 2:3], in_=stA[:, 1:2], func=AF.Sqrt, bias=EPS, scale=1.0 / 256)
            P.I("dve", "reciprocal", out=stA[:, 2:3], in_=stA[:, 2:3])
            P.I("dve", "scalar_tensor_tensor", out=vg.all, in0=vg.all, scalar=stA[:, 2:3], in1=lnw[:, 0, :], op0=ALU.mult, op1=ALU.mult)
            P.I("dve", "tensor_tensor", out=vl.all, in0=vg.all, in1=lnw[:, 1, :], op=ALU.add)
            bm = rbank()
            for h in range(4):
                P.I("pe", "matmul", out=bm[rows(h), (h // 2) * 128:(h // 2) * 128 + 128], lhsT=vl[:, h * 64:(h + 1) * 64],
                    rhs=WsT[:, h, :], start=True, stop=True)
            P.I("dve", "tensor_tensor", out=tmpA.all, in0=V(bm.ap[:, 0:256].rearrange("p (c t) -> p c t", t=128), bm[:, 0:256].keys), in1=bsb.all, op=ALU.add)
            P.I("dve", "tensor_tensor", out=mixAB[:, 0:2, tt * 128:(tt + 1) * 128], in0=tmpA.all,
                in1=uA[:, :, tt * 128:(tt + 1) * 128], op=ALU.mult)

    def mixer_B(l, g, first):
        w = load_wi(l, 512, 512)
        w2 = load_wi(l, 1024, 512)
        for c in range(2):
            bk = ip_fm(w, c * 128, 128)
            P.I("act", "activation", out=gbB[:, c, :], in_=bk.all, func=AF.Copy)
        for c in range(2):
            bk = ip_fm(w, 256 + c * 128, 128)
            P.I("act", "activation", out=gcB.all, in_=bk.all, func=AF.Copy)
            if first:
                b3 = ip_fm(w, 256 + c * 128, 128, rhs_of_kc=lambda kc: hTh[:, kc, :], N=4)
                P.I("act", "activation", out=haloB[:, c, :], in_=b3[:, 0:4], func=AF.Copy)
                b4 = ip_fm(w2, c * 128, 128, rhs_of_kc=lambda kc: hTh[:, kc, :], N=4)
                P.I("dve", "tensor_tensor", out=haloB[:, c, :], in0=haloB[:, c, :], in1=b4[:, 0:4], op=ALU.mult)
            P.I("dve", "tensor_copy", out=tB[:, c, 0:2], in_=haloB[:, c, 2:4])
            bk2 = ip_fm(w2, c * 128, 128)
            P.I("dve", "tensor_tensor", out=tB[:, c, 2:G + 2], in0=gcB.all, in1=bk2.all, op=ALU.mult)
            k0 = c * 3
            P.I("dve", "tensor_scalar", out=accB.all, in0=tB[:, c, 2:G + 2], scalar1=pc(k0 + 2), scalar2=None, op0=ALU.mult)
            P.I("dve", "scalar_tensor_tensor", out=accB.all, in0=tB[:, c, 1:G + 1], scalar=pc(k0 + 1), in1=accB.all, op0=ALU.mult, op1=ALU.add)
            P.I("dve", "scalar_tensor_tensor", out=accB.all, in0=tB[:, c, 0:G], scalar=pc(k0 + 0), in1=accB.all, op0=ALU.mult, op1=ALU.add)
            P.I("dve", "tensor_tensor", out=mixAB[:, 2 + c, :], in0=accB.all, in1=gbB[:, c, :], op=ALU.mult)
            P.I("dve", "tensor_copy", out=haloB[:, c, 2:4], in_=tB[:, c, G:G + 2])
        return w2

    def wout_part(g, mixbuf, kc0):
        for tt in range(4):
            Tt = g * 4 + tt
            for half in range(2):
                bk = wbank()
                for k in range(4):
                    P.I("pe", "matmul", out=bk.all, lhsT=mixbuf[:, k, tt * 128:(tt + 1) * 128],
                        rhs=wo[:, kc0 + k, half * 512:(half + 1) * 512], start=(k == 0), stop=(k == 3))
                P.I("dve", "tensor_tensor", out=xs[:, Tt, half * 512:(half + 1) * 512],
                    in0=xs[:, Tt, half * 512:(half + 1) * 512], in1=bk.all, op=ALU.add)

    def mixer_D(l, g, first):
        w = load_wi(l, C_DQ, 512)
        for c in range(2):
            bk = ip_fm(w, c * 128, 128)
            P.I("act", "activation", out=qfD[:, c, :], in_=bk.all, func=AF.Copy)
        for c in range(2):
            bk = ip_fm(w, 256 + c * 128, 128)
            P.I("act", "activation", out=kfD[:, c, :], in_=bk.all, func=AF.Copy)
        w = load_wi(l, C_DV, 528)
        for tt in range(4):
            bk = ip_tm(w, 0, 256, tt)
            P.I("act", "activation", out=vtD[:, tt, :], in_=bk[:, 0:256], func=AF.Copy)
        bk = ip_fm(w, 256, 16)
        P.I("act", "activation", out=glrD.all, in_=bk[0:16, :], func=AF.Copy)
        for c in range(2):
            bk = ip_fm(w, 272 + c * 128, 128)
            P.I("act", "activation", out=zsD[:, c, :], in_=bk.all, func=AF.Silu)
        if DBG.get("dcut", 99) <= 1:
            return
        for c in range(2):
            bk = mbank()
            P.I("pe", "matmul", out=bk.all, lhsT=wg2b[:, c * 128:(c + 1) * 128], rhs=glrD.all, start=True, stop=True)
            P.I("act", "activation", out=efD.all, in_=bk.all, func=AF.Exp, bias=pc(34 + c), scale=-1.0)
            P.I("act", "activation", out=efD.all, in_=efD.all, func=AF.Ln, bias=1.0, scale=1.0)
            P.I("dve", "tensor_tensor_scan", out=csD[:, c, :], data0=rmask.all, data1=efD.all, initial=0.0, op0=ALU.mult, op1=ALU.add)
            cs3 = V(csD.ap[:, c, :].rearrange("p (n t) -> p n t", t=64), csD[:, c, :].keys)
            d13 = V(d1D.ap.rearrange("p (n t) -> p n t", t=64), d1D.all.keys)
            P.I("dve", "tensor_tensor", out=d13, in0=cs3, in1=cs3[:, :, 32:33].b([128, 8, 64]), op=ALU.subtract)
            P.I("act", "activation", out=aeD.all, in_=d1D.all, func=AF.Exp, scale=-1.0 / 16)
            P.I("dve", "scalar_tensor_tensor", out=qaD[:, c, :], in0=qfD[:, c, :], scalar=0.125, in1=aeD.all, op0=ALU.mult, op1=ALU.mult)
            P.I("act", "activation", out=aeD.all, in_=d1D.all, func=AF.Exp, scale=1.0 / 16)
            P.I("dve", "tensor_tensor", out=kaD[:, c, :], in0=kfD[:, c, :], in1=aeD.all, op=ALU.mult)
            P.I("act", "activation", out=a3D[:, c, :], in_=csD[:, c, :], func=AF.Exp, scale=-1.0 / 16)
            P.I("dve", "scalar_tensor_tensor", out=qgD[:, c, :], in0=qfD[:, c, :], scalar=0.125, in1=a3D[:, c, :], op0=ALU.mult, op1=ALU.mult)
            P.I("dve", "tensor_tensor", out=d13, in0=cs3, in1=cs3[:, :, 63:64].b([128, 8, 64]), op=ALU.subtract)
            P.I("act", "activation", out=aeD.all, in_=d1D.all, func=AF.Exp, scale=1.0 / 16)
            P.I("dve", "tensor_tensor", out=klD[:, c, :], in0=kfD[:, c, :], in1=aeD.all, op=ALU.mult)
        if DBG.get("dcut", 99) <= 2:
            return
        a34 = V(a3D.ap.rearrange("p c (n t) -> p c n t", t=64), a3D.all.keys)
        for n in range(8):
            ng = g * 8 + n
            P.I("dve", "tensor_tensor", out=DcD[:, :, ng + 1], in0=DcD[:, :, ng], in1=a34[:, :, n, 63], op=ALU.mult)
        for c in range(2):
            q3 = V(qgD.ap[:, c, :].rearrange("p (n t) -> p n t", t=64), qgD[:, c, :].keys)
            o3 = V(qgDD.ap[:, c, :].rearrange("p (n t) -> p n t", t=64), qgDD[:, c, :].keys)
            P.I("dve", "tensor_tensor", out=o3, in0=q3, in1=DcD[:, c, g * 8:g * 8 + 8][:, :, None].b([128, 8, 64]), op=ALU.mult)
        if DBG.get("dcut", 99) <= 3:
            return
        for tt in range(4):
            for c in range(2):
                P.I("pe", "transpose", out=pT[:, c, :], in_=klD[:, c, tt * 128:(tt + 1) * 128], identity=identb.all)
            P.I("act", "activation", out=kltD[:, tt, :], in_=pT[:, 0:2, :], func=AF.Copy)
        if DBG.get("dcut", 99) <= 4:
            return
        for tt in range(4):
            tok = slice(tt * 128, (tt + 1) * 128)
            at = atD[tt % 2]
            bkp = [rbank(), rbank()]
            for h in range(4):
                P.I("pe", "matmul", out=bkp[h % 2][:, (h // 2) * 128:(h // 2) * 128 + 128], lhsT=kaD[rows(h), h // 2, tok], rhs=qaD[rows(h), h // 2, tok],
                    start=True, stop=True)
            at4 = V(at.ap.rearrange("p (c r) s -> p c r s", r=2), at.all.keys)
            for par in range(2):
                P.I("dve", "tensor_tensor", out=at4[:, :, par, :], in0=V(bkp[par].ap[:, 0:256].rearrange("p (c s) -> p c s", s=128), bkp[par][:, 0:256].keys),
                    in1=BDUi[:, None, :].b([128, 2, 128]), op=ALU.mult)
            if DBG.get("dsub", 9) <= 1:
                continue
            bsn = [rbank(), rbank()]
            for n2 in range(2):
                rn = slice(n2 * 64, n2 * 64 + 64)
                for h in range(4):
                    P.I("pe", "matmul", out=bsn[n2][rows(h), (h // 2) * 64:(h // 2) * 64 + 64],
                        lhsT=kltD[rn, tt, h * 64:(h + 1) * 64], rhs=vtD[rn, tt, h * 64:(h + 1) * 64], start=True, stop=True)
            Sb = SbD[tt % 2]
            for n2 in range(2):
                n = tt * 2 + n2
                P.I("act", "activation", out=Sb[:, n2, :, :], in_=SD.all, func=AF.Copy)
                for hp in range(2):
                    P.I("dve", "scalar_tensor_tensor", out=SD[:, hp, :], in0=SD[:, hp, :], scalar=a34[:, hp, n, 63:64],
                        in1=bsn[n2][:, hp * 64:hp * 64 + 64], op0=ALU.mult, op1=ALU.add)
            if DBG.get("dsub", 9) <= 2:
                continue
            pop = [pb[6], rbank()]
            pox = [rbank(), rbank()]
            for h in range(4):
                hp = h // 2
                P.I("pe", "matmul", out=pop[h % 2][rows(h), hp * 128:(hp + 1) * 128], lhsT=vtD[:, tt, h * 64:(h + 1) * 64], rhs=at[:, h, :],
                    start=True, stop=True)
                for n2 in range(2):
                    P.I("pe", "matmul", out=pox[h % 2][rows(h), hp * 128 + n2 * 64:hp * 128 + n2 * 64 + 64], lhsT=Sb[rows(h), n2, hp, :],
                        rhs=qgD[rows(h), hp, tt * 128 + n2 * 64:tt * 128 + n2 * 64 + 64], start=True, stop=True)
            for par in range(2):
                pr = slice(par * 64, par * 64 + 64)
                P.I("act", "activation", out=ostD[pr, :, tok], in_=V(pop[par].ap[pr, 0:256].rearrange("p (c t) -> p c t", t=128), pop[par][:, 0:256].keys), func=AF.Copy)
                P.I("dve", "tensor_tensor", out=ostD[pr, :, tok], in0=ostD[pr, :, tok],
                    in1=V(pox[par].ap[pr, 0:256].rearrange("p (c t) -> p c t", t=128), pox[par][:, 0:256].keys), op=ALU.add)
        if DBG.get("dcut", 99) <= 5:
            return
        P.I("sp", "dma_start", out=dv(dfo_d[g][:, 1], "dfo%d" % g), in_=ostD.all)
        P.I("sp", "dma_start", out=dv(dfb_d[g][:, 1], "dfb%d" % g), in_=qgDD.all)
        P.I("sp", "dma_start", out=dv(dfb_d[g][:, 3], "dfb%d" % g), in_=zsD.all)

    def mixer_C(l, g, first, w2):
        wk = load_wi(l, C_CK, 512)
        srcs = [(w2, 256), (w2, 384), (wk, 0), (wk, 128), (wk, 256), (wk, 384)]
        for ci, (w, c0) in enumerate(srcs):
            bk = ip_fm(w, c0, 128)
            P.I("act", "activation", out=cbC[:, 3:G + 3], in_=bk.all, func=AF.Copy)
            if first:
                b3 = ip_fm(w, c0, 128, rhs_of_kc=lambda kc: hTh[:, kc, :], N=4)
                P.I("act", "activation", out=haloC[:, ci, :], in_=b3[:, 0:4], func=AF.Copy)
            P.I("dve", "tensor_copy", out=cbC[:, 0:3], in_=haloC[:, ci, 1:4])
            k0 = 6 + ci * 4
            P.I("act", "activation", out=accC.all, in_=cbC[:, 3:G + 3], func=AF.Copy, scale=pc(k0 + 3))
            for k in range(3):
                P.I("dve", "scalar_tensor_tensor", out=accC.all, in0=cbC[:, k:G + k], scalar=pc(k0 + k), in1=accC.all, op0=ALU.mult, op1=ALU.add)
            P.I("dve", "tensor_copy", out=haloC[:, ci, 1:4], in_=cbC[:, G:G + 3])
            if ci < 4:
                P.I("act", "activation", out=qsC.all, in_=accC.all, func=AF.Silu)
                P.I("act", "activation", out=sqC.all, in_=qsC.all, func=AF.Square)
                bq = rbank()
                P.I("pe", "matmul", out=bq.all, lhsT=bonesb.all, rhs=sqC.all, start=True, stop=True)
                P.I("act", "activation", out=rtC.all, in_=bq.all, func=AF.Sqrt, bias=EPS, scale=1.0)
                P.I("dve", "reciprocal", out=rtC.all, in_=rtC.all)
                if ci < 2:
                    P.I("dve", "scalar_tensor_tensor", out=qTb[:, ci, :], in0=qsC.all, scalar=0.125, in1=rtC.all, op0=ALU.mult, op1=ALU.mult)
                else:
                    P.I("dve", "tensor_tensor", out=kTb[:, ci - 2, :], in0=qsC.all, in1=rtC.all, op=ALU.mult)
            else:
                P.I("act", "activation", out=vTb[:, ci - 4, :], in_=accC.all, func=AF.Silu)
        if DBG.get("ccut", 99) <= 1:
            return
        wz = load_wi(l, C_CA, 264)
        bka = ip_fm(wz, 0, 4)
        P.I("act", "activation", out=t1C.all, in_=bka[0:4, :], func=AF.Exp, bias=p4[:, 1:2], scale=1.0)
        P.I("act", "activation", out=t1C.all, in_=t1C.all, func=AF.Ln, bias=1.0, scale=1.0)
        P.I("dve", "tensor_scalar", out=t1C.all, in0=t1C.all, scalar1=p4[:, 2:3], scalar2=None, op0=ALU.mult)
        P.I("dve", "tensor_tensor_scan", out=gcC.all, data0=rmask[0:4, :], data1=t1C.all, initial=0.0, op0=ALU.mult, op1=ALU.add)
        bkb = ip_fm(wz, 4, 4)
        P.I("act", "activation", out=btC.all, in_=bkb[0:4, :], func=AF.Exp, scale=-1.0)
        P.I("dve", "tensor_scalar", out=btC.all, in0=btC.all, scalar1=1.0, scalar2=None, op0=ALU.add)
        P.I("dve", "reciprocal", out=btC.all, in_=btC.all)
        for c in range(2):
            bk = ip_fm(wz, 8 + c * 128, 128)
            P.I("act", "activation", out=zsC[:, c, :], in_=bk.all, func=AF.Silu)
        gc3 = V(gcC.ap.rearrange("p (n t) -> p n t", t=64), gcC.all.keys)
        t13 = V(t1C.ap.rearrange("p (n t) -> p n t", t=64), t1C.all.keys)
        P.I("dve", "tensor_tensor", out=t13, in0=gc3[:, :, 63:64].b([4, 8, 64]), in1=gc3, op=ALU.subtract)
        P.I("act", "activation", out=t1C.all, in_=t1C.all, func=AF.Exp)
        P.I("act", "activation", out=egC.all, in_=gcC.all, func=AF.Exp)
        if DBG.get("ccut", 99) <= 2:
            return
        for hp in range(2):
            bb = rbank()
            P.I("pe", "matmul", out=bb.all, lhsT=ehp[:, hp, :], rhs=btC.all, start=True, stop=True)
            P.I("dve", "tensor_tensor", out=kbTb[:, hp, :], in0=kTb[:, hp, :], in1=bb.all, op=ALU.mult)
            be = rbank()
            P.I("pe", "matmul", out=be.all, lhsT=ehp[:, hp, :], rhs=egC.all, start=True, stop=True)
            P.I("dve", "tensor_tensor", out=qgTb[:, hp, :], in0=qTb[:, hp, :], in1=be.all, op=ALU.mult)
            bd = rbank()
            P.I("pe", "matmul", out=bd[:, 0:8], lhsT=ehp[:, hp, :], rhs=V(egC.ap.rearrange("p (n t) -> p n t", t=64)[:, :, 63], egC.all.keys),
                start=True, stop=True)
            P.I("act", "activation", out=decb[:, hp, :], in_=bd[:, 0:8], func=AF.Copy)
        if DBG.get("ccut", 99) <= 3:
            return
        for tt in range(4):
            tok = slice(tt * 128, (tt + 1) * 128)
            bt_ = rbank()
            for qi, src in enumerate((gcC, btC, t1C, egC)):
                P.I("pe", "transpose", out=bt_[:, qi * 4:qi * 4 + 4], in_=src[:, tok], identity=ident[0:4, 0:4])
            P.I("act", "activation", out=tokq[:, tt, :, :], in_=V(bt_.ap[:, 0:16].rearrange("p (q h) -> p q h", h=4), bt_[:, 0:16].keys), func=AF.Copy)
            P.I("dve", "tensor_tensor", out=tokb[:, tt, :], in0=tokq[:, tt, 1, :], in1=tokq[:, tt, 3, :], op=ALU.mult)
        if DBG.get("ccut", 99) <= 4:
            return
        for i in range(2):
            P.I("dve", "memset", ap=uaug[i][:, :, 64:128], constant=0.0)
        f4 = lambda t_: V(t_.ap.rearrange("p h s -> p (h s)"), t_.all.keys)
        v4 = lambda t_: V(t_.ap.rearrange("p (c r) s -> p c r s", r=2), t_.all.keys)

        def c_prep(tt, sl):
            tok = slice(tt * 128, (tt + 1) * 128)
            bA, bB, bC = bufA[sl], bufB[sl], bufC[sl]
            N_, NT_, TT_ = Nb[sl], NTb[sl], TTb[sl]
            for c in range(2):
                P.I("pe", "transpose", out=pT[:, c, :], in_=kTb[:, c, tok], identity=identb.all)
                P.I("pe", "transpose", out=pT[:, 2 + c, :], in_=vTb[:, c, tok], identity=identb.all)
            kt4 = V(pT.ap[:, 0:2, :].rearrange("p c (h d) -> p (c h) d", d=64), pT[:, 0:2, :].keys)
            vt4 = V(pT.ap[:, 2:4, :].rearrange("p c (h d) -> p (c h) d", d=64), pT[:, 2:4, :].keys)
            P.I("dve", "tensor_tensor", out=vbt[sl].all, in0=vt4, in1=tokq[:, tt, 1, :][:, :, None].b([128, 4, 64]), op=ALU.mult)
            P.I("dve", "tensor_tensor", out=kbgt[sl].all, in0=kt4, in1=tokb[:, tt, :][:, :, None].b([128, 4, 64]), op=ALU.mult)
            P.I("dve", "tensor_tensor", out=kdt[sl].all, in0=kt4, in1=tokq[:, tt, 2, :][:, :, None].b([128, 4, 64]), op=ALU.mult)
            yield
            if DBG.get("pcut", 99) <= 1:
                return
            P.I("dve", "tensor_tensor", out=bA.all, in0=tokq[:, tt, 0, :][:, :, None].b([128, 4, 128]), in1=ident[:, None, :].b([128, 4, 128]), op=ALU.mult)
            bg = rbank()
            P.I("pe", "matmul", out=bg.all, lhsT=ones_f, rhs=f4(bA), start=True, stop=True)
            bg3 = V(bg.ap.rearrange("p (h s) -> p h s", s=128), bg.all.keys)
            P.I("dve", "tensor_tensor", out=bB.all, in0=bg3, in1=tokq[:, tt, 0, :][:, :, None].b([128, 4, 128]), op=ALU.subtract)
            yield
            P.I("dve", "tensor_scalar", out=bC.all, in0=bB.all, scalar1=3.0e38, scalar2=0.0, op0=ALU.min, op1=ALU.max)
            P.I("act", "activation", out=bC.all, in_=bC.all, func=AF.Exp, scale=-1.0)
            P.I("dve", "tensor_scalar", out=bA.all, in0=bB.all, scalar1=0.0, scalar2=-3.0e38, op0=ALU.min, op1=ALU.max)
            P.I("act", "activation", out=bA.all, in_=bA.all, func=AF.Exp, scale=1.0)
            yield
            P.I("dve", "tensor_tensor", out=bC.all, in0=bC.all, in1=nBDLs[:, None, :].b([128, 4, 128]), op=ALU.mult)
            P.I("dve", "tensor_tensor", out=bB.all, in0=bA.all, in1=nBDUs[:, None, :].b([128, 4, 128]), op=ALU.mult)
            P.I("dve", "tensor_tensor", out=bA.all, in0=bA.all, in1=BDUi[:, None, :].b([128, 4, 128]), op=ALU.mult)
            if DBG.get("pcut", 99) <= 2:
                return
            for (lt, rt_, dst, msk) in ((kbTb, kTb, N_[0], bC), (kTb, kbTb, NT_[0], bB), (kTb, qTb, atC[sl], bA)):
                bkp = [rbank(), rbank()]
                for h in range(4):
                    hp = h // 2
                    P.I("pe", "matmul", out=bkp[h % 2][:, hp * 128:(hp + 1) * 128], lhsT=lt[rows(h), hp, tok], rhs=rt_[rows(h), hp, tok], start=True, stop=True)
                for par in range(2):
                    P.I("dve", "tensor_tensor", out=v4(dst)[:, :, par, :], in0=V(bkp[par].ap[:, 0:256].rearrange("p (c s) -> p c s", s=128), bkp[par][:, 0:256].keys),
                        in1=v4(msk)[:, :, par, :], op=ALU.mult)
                yield
            if DBG.get("pcut", 99) <= 3:
                return
            P.I("dve", "tensor_tensor", out=TT_[0].all, in0=NT_[0].all, in1=ident[:, None, :].b([128, 4, 128]), op=ALU.add)
            cur = 0
            for k in range(1, 6):
                bn = rbank()
                for h in range(4):
                    P.I("pe", "matmul", out=bn[:, h * 128:(h + 1) * 128], lhsT=NT_[cur][:, h, :], rhs=N_[cur][:, h, :], start=True, stop=True)
                P.I("act", "activation", out=f4(N_[1 - cur]), in_=bn.all, func=AF.Copy)
                if k < 5:
                    bnt = rbank()
                    for h in range(4):
                        P.I("pe", "matmul", out=bnt[:, h * 128:(h + 1) * 128], lhsT=N_[cur][:, h, :], rhs=NT_[cur][:, h, :], start=True, stop=True)
                    P.I("act", "activation", out=f4(NT_[1 - cur]), in_=bnt.all, func=AF.Copy)
                cur = 1 - cur
                yield
                bt2 = rbank()
                tcur = (k - 1) % 2
                for h in range(4):
                    P.I("pe", "matmul", out=bt2[:, h * 128:(h + 1) * 128], lhsT=N_[cur][:, h, :], rhs=TT_[tcur][:, h, :], start=True, stop=True)
                P.I("dve", "tensor_tensor", out=f4(TT_[1 - tcur]), in0=bt2.all, in1=f4(TT_[tcur]), op=ALU.add)
                yield
            TT = TT_[1]
            if DBG.get("pcut", 99) <= 4:
                return
            bu = rbank()
            for h in range(4):
                P.I("pe", "matmul", out=bu[:, h * 64:(h + 1) * 64], lhsT=TT[:, h, :], rhs=vbt[sl][:, h, :], start=True, stop=True)
            P.I("act", "activation", out=uaug[sl][:, :, 0:64], in_=V(bu.ap[:, 0:256].rearrange("p (h e) -> p h e", e=64), bu[:, 0:256].keys), func=AF.Copy)
            bw = rbank()
            for h in range(4):
                P.I("pe", "matmul", out=bw[rows(h), (h // 2) * 128:(h // 2) * 128 + 128], lhsT=kbgt[sl][:, h, :], rhs=TT[:, h, :], start=True, stop=True)
            P.I("act", "activation", out=wTb[sl].all, in_=V(bw.ap[:, 0:256].rearrange("p (c t) -> p c t", t=128), bw[:, 0:256].keys), func=AF.Copy)
            yield

        def c_scan_out(tt, sl):
            tok = slice(tt * 128, (tt + 1) * 128)
            Xb = XbC[0]
            vn = vnb[0]
            for n2 in range(2):
                n = tt * 2 + n2
                rn = slice(n2 * 64, n2 * 64 + 64)
                P.I("act", "activation", out=Xb[:, n2, :, :], in_=XC.all, func=AF.Copy)
                bvp = [pb[6], rbank()]
                for h in range(4):
                    P.I("pe", "matmul", out=bvp[h % 2][rn, (h // 2) * 128:(h // 2) * 128 + 128], lhsT=wTb[sl][rows(h), h // 2, n2 * 64:n2 * 64 + 64],
                        rhs=Xb[rows(h), n2, h // 2, :], start=True, stop=True)
                vn4 = V(vn.ap.rearrange("p (c r) f -> p c r f", r=2), vn.all.keys)
                ua4 = V(uaug[sl].ap.rearrange("p (c r) f -> p c r f", r=2), uaug[sl].all.keys)
                for par in range(2):
                    P.I("dve", "tensor_tensor", out=vn4[rn, :, par, :], in0=ua4[rn, :, par, :],
                        in1=V(bvp[par].ap[rn, 0:256].rearrange("p (c f) -> p c f", f=128), bvp[par][:, 0:256].keys), op=ALU.subtract)
                bd2 = rbank()
                for h in range(4):
                    P.I("pe", "matmul", out=bd2[rows(h), (h // 2) * 128:(h // 2) * 128 + 128], lhsT=kdt[sl][rn, h, :],
                        rhs=vn[rn, h, :], start=True, stop=True)
                for hp in range(2):
                    P.I("dve", "scalar_tensor_tensor", out=XC[:, hp, :], in0=XC[:, hp, :], scalar=decb[:, hp, n:n + 1],
                        in1=bd2[:, hp * 128:(hp + 1) * 128], op0=ALU.mult, op1=ALU.add)
            pop = [pb[6], rbank()]
            pox = [rbank(), rbank()]
            for part in range(2):
                for h in range(4):
                    hp = h // 2
                    base = part * 256 + hp * 128
                    P.I("pe", "matmul", out=pop[h % 2][rows(h), base:base + 128], lhsT=vn[:, h, part * 64:part * 64 + 64], rhs=atC[sl][:, h, :],
                        start=True, stop=True)
                    for n2 in range(2):
                        P.I("pe", "matmul", out=pox[h % 2][rows(h), base + n2 * 64:base + n2 * 64 + 64],
                            lhsT=Xb[rows(h), n2, hp, part * 64:part * 64 + 64],
                            rhs=qgTb[rows(h), hp, tt * 128 + n2 * 64:tt * 128 + n2 * 64 + 64], start=True, stop=True)
            for par in range(2):
                pr = slice(par * 64, par * 64 + 64)
                P.I("act", "activation", out=ostC[pr, :, tok], in_=V(pop[par].ap[pr, 0:256].rearrange("p (c t) -> p c t", t=128), pop[par][:, 0:256].keys), func=AF.Copy)
                P.I("dve", "tensor_tensor", out=ostC[pr, :, tok], in0=ostC[pr, :, tok],
                    in1=V(pox[par].ap[pr, 0:256].rearrange("p (c t) -> p c t", t=128), pox[par][:, 0:256].keys), op=ALU.add)
                P.I("act", "activation", out=accC[pr, 0:256], in_=pop[par][pr, 256:512], func=AF.Copy)
                P.I("dve", "tensor_tensor", out=PTC[pr, :, tok], in0=V(accC.ap[pr, 0:256].rearrange("p (c t) -> p c t", t=128), accC[:, 0:256].keys),
                    in1=V(pox[par].ap[pr, 256:512].rearrange("p (c t) -> p c t", t=128), pox[par][:, 256:512].keys), op=ALU.add)

        for pair in range(2):
            gens = [c_prep(pair * 2 + i, i) for i in range(2)]
            alive = [True, True]
            while any(alive):
                for i in range(2):
                    if alive[i]:
                        try:
                            next(gens[i])
                        except StopIteration:
                            alive[i] = False
            for i in range(2):
                if DBG.get("pcut", 99) > 5:
                    c_scan_out(pair * 2 + i, i)
        P.I("sp", "dma_start", out=dv(dfo_d[g][:, 0], "dfo%d" % g), in_=ostC.all)
        P.I("sp", "dma_start", out=dv(dfb_d[g][:, 0], "dfb%d" % g), in_=PTC.all)
        P.I("sp", "dma_start", out=dv(dfb_d[g][:, 2], "dfb%d" % g), in_=zsC.all)

    def exchange(l):
        P.I("dve", "tensor_copy", out=packS[:, 0:256], in_=V(XC.ap.rearrange("p c f -> p (c f)"), XC.all.keys))
        P.I("dve", "tensor_copy", out=packS[:, 256:384], in_=V(SD.ap.rearrange("p c f -> p (c f)"), SD.all.keys))
        P.I("dve", "tensor_copy", out=packS[:, 384:386], in_=DcD[:, :, 32])
        if local_only and l == layers[-1]:
            P.I("sp", "dma_start", out=dv(pack_o, "pack"), in_=packS.all)
            return False
        if use_cc:
            P.I("sp", "dma_start", out=dv(packd, "packd"), in_=packS.all)
            P.custom("pool", lambda e: e.collective_compute("AllGather", op=ALU.bypass, replica_groups=[[0, 1, 2, 3], [4, 5, 6, 7]],
                                                            ins=[packd.opt()], outs=[gathd.opt()]),
                     reads=DK("packd"), writes=DK("gathd"), dma=True, cc=True)
            P.I("sp", "dma_start", out=gat.all, in_=dv(gathd.rearrange("(r p) f -> p r f", p=128), "gathd"))
        else:
            P.I("sp", "dma_start", out=gat.all, in_=dv(gath_d.rearrange("r p f -> p r f")))
        P.I("dve", "memset", ap=accS.all, constant=0.0)
        for j in range(1, 4):
            gC = V(gat.ap[:, j - 1, 0:256].rearrange("p (c f) -> p c f", f=128), gat[:, j - 1, 0:256].keys)
            gD = V(gat.ap[:, j - 1, 256:384].rearrange("p (c f) -> p c f", f=64), gat[:, j - 1, 256:384].keys)
            if j == 1:
                P.I("dve", "tensor_copy", out=curC.all, in_=gC[:, :, 0:64])
                P.I("dve", "tensor_copy", out=curD.all, in_=gD)
            else:
                bt_ = rbank()
                for hp in range(2):
                    P.I("pe", "transpose", out=bt_[0:64, hp * 128:(hp + 1) * 128], in_=gC[:, hp, 64:128], identity=ident)
                P.I("act", "activation", out=MTf.all, in_=V(bt_.ap[0:64, 0:256].rearrange("p (c f) -> p c f", f=128), bt_[:, 0:256].keys), func=AF.Copy)
                bs_ = rbank()
                for hp in range(2):
                    P.I("pe", "matmul", out=bs_[0:64, hp * 64:(hp + 1) * 64], lhsT=ident[:, 64:128], rhs=curC[:, hp, :], start=True, stop=True)
                P.I("act", "activation", out=curF.all, in_=V(bs_.ap[0:64, 0:128].rearrange("p (c f) -> p c f", f=64), bs_[:, 0:128].keys), func=AF.Copy)
                bm = rbank()
                for h in range(4):
                    hp = h // 2
                    rhs_ = curC[0:64, hp, :] if h % 2 == 0 else curF[:, hp, :]
                    P.I("pe", "matmul", out=bm[rows(h), hp * 64:(hp + 1) * 64], lhsT=MTf[:, hp, (h % 2) * 64:(h % 2) * 64 + 64], rhs=rhs_, start=True, stop=True)
                P.I("dve", "tensor_tensor", out=curC.all, in0=gC[:, :, 0:64], in1=V(bm.ap[:, 0:128].rearrange("p (c f) -> p c f", f=64), bm[:, 0:128].keys), op=ALU.add)
                for hp in range(2):
                    P.I("dve", "scalar_tensor_tensor", out=curD[:, hp, :], in0=curD[:, hp, :], scalar=gat[:, j - 1, 384 + hp:385 + hp],
                        in1=gD[:, hp, :], op0=ALU.mult, op1=ALU.add)
            P.I("dve", "scalar_tensor_tensor", out=accS[:, 0], in0=curC.all, scalar=sel[:, j:j + 1], in1=accS[:, 0], op0=ALU.mult, op1=ALU.add)
            P.I("dve", "scalar_tensor_tensor", out=accS[:, 1], in0=curD.all, scalar=sel[:, j:j + 1], in1=accS[:, 1], op0=ALU.mult, op1=ALU.add)
        P.I("act", "activation", out=SinB.all, in_=accS.all, func=AF.Copy)
        return True

    def post_group(l, g):
        P.I("sp", "dma_start", out=ost.all, in_=dv(dfo_d[g], "dfo%d" % g))
        P.I("sp", "dma_start", out=dbf.all, in_=dv(dfb_d[g], "dfb%d" % g))
        for m in range(2):
            if ("C" in skip and m == 0) or ("D" in skip and m == 1):
                P.I("dve", "memset", ap=mixCD[:, m * 2:m * 2 + 2, :], constant=0.0)
                continue
            for hp in range(2):
                for hh in range(2):
                    h = hp * 2 + hh
                    bk = rbank()
                    P.I("pe", "matmul", out=bk[rows(h), :], lhsT=SinB[rows(h), m, hp, :], rhs=dbf[rows(h), m, hp, :], start=True, stop=True)
                    P.I("dve", "tensor_tensor", out=ofQ[rows(h), :], in0=ost[rows(h), m, hp, :], in1=bk[rows(h), :], op=ALU.add)
                P.I("act", "activation", out=sqQ.all, in_=ofQ.all, func=AF.Square)
                bq = rbank()
                P.I("pe", "matmul", out=bq.all, lhsT=bonesb.all, rhs=sqQ.all, start=True, stop=True)
                P.I("act", "activation", out=rtQ.all, in_=bq.all, func=AF.Sqrt, bias=EPS, scale=1.0 / 64)
                P.I("dve", "reciprocal", out=rtQ.all, in_=rtQ.all)
                P.I("dve", "scalar_tensor_tensor", out=t1Q.all, in0=ofQ.all, scalar=pc(32 + m), in1=rtQ.all, op0=ALU.mult, op1=ALU.mult)
                P.I("dve", "tensor_tensor", out=mixCD[:, m * 2 + hp, :], in0=t1Q.all, in1=dbf[:, 2 + m, hp, :], op=ALU.mult)
        wout_part(g, mixCD, 4)

    nw_idx = {0: (0, 1), 1: (2, 3)}
    for li, l in enumerate(layers):
        load_nw(nw_idx[l][0])
        layer_setup(l)
        halo_prep(l, dv(xh_d) if (li == 0 or not use_cc) else None)
        for g in range(NG):
            norm_group(g, None)
            if "A" in skip:
                P.I("dve", "memset", ap=mixAB[:, 0:2, :], constant=0.0)
            else:
                mixer_A(l, g, g == 0)
            w2 = mixer_B(l, g, g == 0)
            wout_part(g, mixAB, 0)
            if "C" not in skip:
                mixer_C(l, g, g == 0, w2)
            if "D" not in skip:
                mixer_D(l, g, g == 0)
        if "X" not in skip:
            cont = exchange(l)
            if not cont:
                break
        load_nw(nw_idx[l][1])
        for g in range(NG):
            if "X" not in skip:
                post_group(l, g)
            norm_group(g, None)
            ffn_group(l, g, SF)
        if use_cc and li + 1 < len(layers):
            P.I("sp", "dma_start", out=dv(xsend, "xsend"), in_=xs[124:128, 15, :])
            P.custom("pool", lambda e: e.collective_compute("AllGather", op=ALU.bypass, replica_groups=[[0, 1, 2, 3], [4, 5, 6, 7]],
                                                            ins=[xsend.opt()], outs=[xgath.opt()]),
                     reads=DK("xsend"), writes=DK("xgath"), dma=True, cc=True)
    if local_only:
        allk = DK("pack")
        for g in range(NG):
            allk = allk + DK("dfo%d" % g) + DK("dfb%d" % g)
        P.ops.append(Op("sp", ("nop", None), allk, (), False))
        P.finalize()
        return nc, P
    if final_norm:
        load_nw(4)
        for Tt in range(16):
            P.I("act", "activation", out=junk.all, in_=xs[:, Tt, :], func=AF.Square, accum_out=ss[:, Tt:Tt + 1])
            P.I("act", "activation", out=rstd[:, Tt:Tt + 1], in_=ss[:, Tt:Tt + 1], func=AF.Sqrt, bias=EPS, scale=1.0 / D_MODEL)
            P.I("dve", "reciprocal", out=rstd[:, Tt:Tt + 1], in_=rstd[:, Tt:Tt + 1])
            P.I("dve", "scalar_tensor_tensor", out=xs[:, Tt, :], in0=xs[:, Tt, :], scalar=rstd[:, Tt:Tt + 1], in1=nwb.all,
                op0=ALU.mult, op1=ALU.mult)
    for Tt in range(16):
        P.I("sp", "dma_start", out=dv(y_o[Tt * 128:(Tt + 1) * 128, :], "y"), in_=xs[:, Tt, :])
    P.ops.append(Op("sp", ("nop", None), DK("y"), (), False))
    P.finalize()
    return nc, P


def host_consts(core):
    seg = core % 4
    c = np.zeros((128, 11, 128), np.float32)
    i = np.arange(128)
    c[:, 0, :] = np.eye(128)
    c[:, 1, :] = (i[None, :] <= i[:, None])
    same = (i[:, None] // 64) == (i[None, :] // 64)
    c[:, 2, :] = same & (i[None, :] >= i[:, None])
    c[:, 3, :] = same & (i[None, :] > i[:, None])
    c[:, 4, :] = same & (i[None, :] < i[:, None])
    c[:, 5, :] = 1.0
    c[:, 6, :] = same
    c[:, 7, 0:64] = (i[:, None] % 64) == np.arange(64)[None, :]
    c[:, 8, seg] = 1.0
    c[:, 9, :] = -1.0 * (same & (i[None, :] > i[:, None]))
    c[:, 10, :] = -1.0 * (same & (i[None, :] < i[:, None]))
    for t in range(4):
        if seg > 0:
            c[(seg - 1) * 4 + t, 8, 4 + t] = 1.0
    ehp = np.zeros((4, 2, 128), np.float32)
    for h in range(4):
        ehp[h, h // 2, (h % 2) * 64:(h % 2) * 64 + 64] = 1.0
    rmask = np.ones((128, 512), np.float32)
    rmask[:, ::64] = 0.0
    return c, ehp, rmask


def host_params(inp):
    pcol = np.zeros((2, 128, 40), np.float32)
    p = np.arange(128)
    for l in range(2):
        scw = inp["sc_conv_w"][l]
        for c in range(2):
            for k in range(3):
                pcol[l, :, c * 3 + k] = scw[k, c * 128 + p]
        dnw = inp["dn_conv_w"][l]
        for c in range(6):
            for k in range(4):
                pcol[l, :, 6 + c * 4 + k] = dnw[k, c * 128 + p]
        for c in range(2):
            pcol[l, :, 30 + c] = inp["gla_gate_bias"][l][c * 128 + p]
        pcol[l, :, 32] = inp["dn_norm_w"][l][p % 64]
        pcol[l, :, 33] = inp["gla_norm_w"][l][p % 64]
    p4 = np.stack([inp["dn_a_log"], inp["dn_dt_bias"]], axis=-1).astype(np.float32)
    lnwb = np.stack([inp["sgu_ln_w"], inp["sgu_ln_b"]], axis=1).astype(np.float32)
    nrm = np.stack([inp["norm1_w"][0], inp["norm2_w"][0], inp["norm1_w"][1], inp["norm2_w"][1],
                    inp["final_norm_w"]], axis=0).astype(np.float32)
    return pcol, p4, lnwb, nrm


def _in_maps(inp, consts, params, xcores, xh, gath):
    pcol, p4, lnwb, nrm = params
    maps = []
    for c in range(NCORES):
        cst, ehp, rmask = consts[c]
        maps.append({
            "x": xcores[c], "xh": xh[c], "w_in": inp["w_in"], "w_out": inp["w_out"],
            "w_gate_up": inp["w_gate_up"], "w_down": inp["w_down"], "nrm": nrm, "pcol": pcol, "p4": p4,
            "lnwb": lnwb, "wsp": inp["sgu_w_spatial"], "bsp": inp["sgu_b_spatial"], "wg2": inp["gla_w_gate2"],
            "cst": cst, "ehp": ehp, "rmask": rmask, "gath": gath[c],
        })
    return maps


def _halo(xcores):
    xh = []
    for c in range(NCORES):
        if c % 4 == 0:
            xh.append(np.zeros((4, D_MODEL), np.float32))
        else:
            xh.append(np.ascontiguousarray(xcores[c - 1][NT - 4:NT]))
    return xh


def kernel(**inputs):
    inp = {k: np.ascontiguousarray(np.asarray(v, dtype=np.float32)) for k, v in inputs.items()}
    x = inp["x"].reshape(2 * SEQ, D_MODEL)
    params = host_params(inp)
    consts = [host_consts(c) for c in range(NCORES)]
    xcores = [np.ascontiguousarray(x[c * NT:(c + 1) * NT]) for c in range(NCORES)]
    zg = [np.zeros((4, 128, 392), np.float32) for _ in range(NCORES)]
    xh = _halo(xcores)
    nc, _ = build([0, 1], False, True, use_cc=True)
    res = run_bass_kernel_spmd(nc, _in_maps(inp, consts, params, xcores, xh, zg), core_ids=list(range(NCORES)))
    out = np.concatenate([np.asarray(r["y"], np.float32) for r in res.results], axis=0)
    return out.reshape(2, SEQ, D_MODEL).astype(np.float32)
```

```python
import numpy as np
import concourse.bass as bass
import concourse.mybir as mybir
from concourse.bass_utils import run_bass_kernel_spmd

F32 = mybir.dt.float32
BF16 = mybir.dt.bfloat16
AF = mybir.ActivationFunctionType
ALU = mybir.AluOpType

D_MODEL = 1024
SEQ = 8192
NCORES = 8
NT = 2048
G = 512
NG = NT // G
D_FF = 2816
NJ = D_FF // 128
IN_COLS = 3352
EPS = 1e-6
SB_BASE = 16512
SB_END = 229376
GRAN = 256

C_AU, C_AV = 0, 256
C_BB, C_BC, C_BH = 512, 768, 1024
C_CQ, C_CK, C_CV, C_CA, C_CB, C_CZ = 1280, 1536, 1792, 2048, 2052, 2056
C_DQ, C_DK, C_DV, C_DG, C_DZ = 2312, 2568, 2824, 3080, 3096


DBG = {}
ATTACH_WAITS = True
ATTACH_ENGS = ("act", "dve")


class V:
    __slots__ = ("ap", "keys")

    def __init__(self, ap, keys):
        self.ap = ap
        self.keys = keys

    def b(self, shape):
        return V(self.ap.broadcast_to(list(shape)), self.keys)

    def __getitem__(self, idx):
        return V(self.ap[idx], self.keys)


class T:
    def __init__(self, space, name, ap, off, shape, esz):
        self.space, self.name, self.ap, self.off, self.shape, self.esz = space, name, ap, off, shape, esz
        st = [1] * len(shape)
        for i in range(len(shape) - 2, 0, -1):
            st[i] = st[i + 1] * shape[i + 1]
        self.st = st

    def __getitem__(self, idx):
        if not isinstance(idx, tuple):
            idx = (idx,)
        lo = 0
        hi = 0
        full = list(idx) + [slice(None)] * (len(self.shape) - len(idx))
        for d in range(1, len(self.shape)):
            ix = full[d]
            if isinstance(ix, slice):
                a = 0 if ix.start is None else ix.start
                b_ = self.shape[d] if ix.stop is None else ix.stop
            else:
                a, b_ = ix, ix + 1
            lo += a * self.st[d]
            hi += (b_ - 1) * self.st[d]
        b0 = self.off + lo * self.esz
        b1 = self.off + (hi + 1) * self.esz
        if self.space == "ps":
            keys = (("ps", self.off // 2048),)
        else:
            keys = tuple((self.space, g) for g in range(b0 // GRAN, (b1 - 1) // GRAN + 1))
        return V(self.ap[idx], keys)

    @property
    def all(self):
        return self[(slice(None),)]


class Op:
    __slots__ = ("eng", "fn", "reads", "writes", "dma", "eidx", "deps", "sig", "count", "sem", "waits", "cc")

    def __init__(self, eng, fn, reads, writes, dma):
        self.eng, self.fn, self.reads, self.writes, self.dma = eng, fn, reads, writes, dma
        self.deps = []
        self.sig = False
        self.count = 0
        self.sem = None
        self.waits = []
        self.cc = False


ENG_ATTR = {"pe": "tensor", "act": "scalar", "dve": "vector", "pool": "gpsimd", "sp": "sync"}
OUT_NAMES = ("out", "accum_out", "ap")


class Prog:
    def __init__(self, nc):
        self.nc = nc
        self.ops = []
        self.sb_off = SB_BASE
        self.ps_banks = 0
        self.marks = []

    def sb(self, name, shape, dt, at=None):
        esz = 2 if dt == BF16 else 4
        n = 1
        for s in shape[1:]:
            n *= s
        nbytes = ((n * esz + 63) // 64) * 64
        if at is None:
            off = self.sb_off
            self.sb_off += nbytes
        else:
            off = at
        assert off + nbytes <= SB_END, ("SBUF overflow", name, off, nbytes)
        h = self.nc.alloc_sbuf_tensor_at(name, list(shape), dt, offset=off)
        return T("sb", name, h.ap(), off, list(shape), esz)

    def ps(self, name, shape, dt):
        h = self.nc.alloc_psum_tensor(name, list(shape), dt)
        esz = 2 if dt == BF16 else 4
        off = self.ps_banks * 2048
        self.ps_banks += 1
        return T("ps", name, h.ap(), off, list(shape), esz)

    def I(self, eng, method, **kw):
        reads, writes, args = [], [], {}
        for k, v in kw.items():
            if isinstance(v, V):
                args[k] = v.ap
                if k in OUT_NAMES:
                    writes.extend(v.keys)
                else:
                    reads.extend(v.keys)
                    writes.extend(kk for kk in v.keys if kk[0] == "ps")
            else:
                args[k] = v
        extra_r = args.pop("_r", None)
        dma = method == "dma_start"
        fn = (method, args)
        op = Op(eng, fn, tuple(reads), tuple(writes), dma)
        self.ops.append(op)
        return op

    def custom(self, eng, fn, reads=(), writes=(), dma=False, cc=False):
        op = Op(eng, ("custom", fn), tuple(reads), tuple(writes), dma)
        op.cc = cc
        self.ops.append(op)
        return op

    def finalize(self, nsem_dma=10):
        nc = self.nc
        last_w = {}
        readers = {}
        per_eng = {e: [] for e in ENG_ATTR}
        for i, op in enumerate(self.ops):
            op.eidx = len(per_eng[op.eng])
            per_eng[op.eng].append(op)
            deps = set()
            for k in op.reads:
                w = last_w.get(k)
                if w is not None:
                    deps.add(w)
            for k in op.writes:
                w = last_w.get(k)
                if w is not None:
                    deps.add(w)
                for r in readers.get(k, ()):
                    deps.add(r)
            deps.discard(op)
            op.deps = deps
            for k in op.reads:
                readers.setdefault(k, []).append(op)
            for k in op.writes:
                last_w[k] = op
                readers[k] = []
        for op in self.ops:
            need = []
            for d in op.deps:
                if d.dma:
                    need.append(d)
                    continue
                if d.eng == op.eng and not op.dma:
                    if op.eng == "pe":
                        continue
                    if op.eng in ("act", "dve") and op.eidx - d.eidx > 3:
                        continue
                need.append(d)
                d.sig = True
            op.deps = need
        with_sems = {}
        eng_sem = {}
        dma_sems = {}
        names = []
        for e in ENG_ATTR:
            eng_sem[e] = nc.alloc_semaphore("s_" + e)
            dma_sems[e] = [nc.alloc_semaphore("d_%s_%d" % (e, i)) for i in range(nsem_dma)] if e in ("sp", "pool", "act") else []
        for e, lst in per_eng.items():
            cnt = 0
            ndma = 0
            for op in lst:
                if op.cc:
                    op.sem = nc.alloc_semaphore("cc_%d" % op.eidx)
                    op.count = 1
                elif op.dma:
                    op.sem = dma_sems[e][ndma % nsem_dma]
                    prev = 16 * (ndma // nsem_dma)
                    op.count = prev + 16
                    if prev > 0:
                        op.waits.append((op.sem, prev))
                    ndma += 1
                elif op.sig:
                    cnt += 1
                    op.count = cnt
                    op.sem = eng_sem[e]
        for e, lst in per_eng.items():
            waited = {}
            for op in lst:
                ws = {}
                for (s, c) in op.waits:
                    ws[s] = max(ws.get(s, 0), c)
                for d in op.deps:
                    ws[d.sem] = max(ws.get(d.sem, 0), d.count)
                out = []
                for s, c in ws.items():
                    if waited.get(s, 0) >= c:
                        continue
                    waited[s] = c
                    out.append((s, c))
                op.waits = out
        self.nwaits = sum(len(op.waits) for op in self.ops)
        with nc.Block() as block:
            for e, lst in per_eng.items():
                if not lst:
                    continue

                def body(engobj, lst=lst):
                    for op in lst:
                        kind, payload = op.fn
                        attach = None
                        waits = op.waits
                        if ATTACH_WAITS and waits and op.eng in ATTACH_ENGS and kind not in ("custom", "nop") and not op.dma:
                            attach = waits[-1]
                            waits = waits[:-1]
                        for (s, c) in waits:
                            engobj.wait_ge(s, c)
                        if kind == "custom":
                            ins = payload(engobj)
                        elif kind == "nop":
                            ins = None
                        else:
                            ins = getattr(engobj, kind)(**payload)
                        if attach is not None:
                            ins._wait_ge(attach[0], attach[1])
                        if ins is not None:
                            if op.cc:
                                ins.then_inc(op.sem, 1)
                            elif op.dma:
                                ins.then_inc(op.sem, 16)
                            elif op.sig:
                                ins.then_inc(op.sem, 1)

                getattr(block, ENG_ATTR[e])(body)


def build(layers, local_only, final_norm, use_cc=False, dbg=None, skip=()):
    nc = bass.Bass("TRN2", target_bir_lowering=False, num_devices=NCORES) if use_cc else bass.Bass("TRN2", target_bir_lowering=False)
    P = Prog(nc)

    def din(name, shape):
        return nc.dram_tensor(name, list(shape), F32, kind="ExternalInput").ap()

    x_d = din("x", [NT, D_MODEL])
    xh_d = din("xh", [4, D_MODEL])
    w_in_d = din("w_in", [2, D_MODEL, IN_COLS])
    w_out_d = din("w_out", [2, D_MODEL, D_MODEL])
    w_gu_d = din("w_gate_up", [2, D_MODEL, 2 * D_FF])
    w_dn_d = din("w_down", [2, D_FF, D_MODEL])
    nrm_d = din("nrm", [5, D_MODEL])
    pcol_d = din("pcol", [2, 128, 40])
    p4_d = din("p4", [2, 4, 2])
    lnwb_d = din("lnwb", [2, 2, 256])
    wsp_d = din("wsp", [2, 4, 128, 128])
    bsp_d = din("bsp", [2, 4, 128])
    wg2_d = din("wg2", [2, 16, 256])
    cst_d = din("cst", [128, 11, 128])
    ehp_d = din("ehp", [4, 2, 128])
    rmask_d = din("rmask", [128, 512])
    gath_d = din("gath", [4, 128, 392])
    if local_only:
        pack_o = nc.dram_tensor("pack", [128, 392], F32, kind="ExternalOutput").ap()
    else:
        y_o = nc.dram_tensor("y", [NT, D_MODEL], F32, kind="ExternalOutput").ap()
    packd = nc.dram_tensor("packd", [128, 392], F32, kind="Internal").ap()
    gathd = nc.dram_tensor("gathd", [512, 392], F32, kind="Internal").ap()
    xsend = nc.dram_tensor("xsend", [4, D_MODEL], F32, kind="Internal").ap()
    xgath = nc.dram_tensor("xgath", [16, D_MODEL], F32, kind="Internal").ap()
    dfo_d = nc.dram_tensor("dfo", [NG, 128, 2, 2, G], F32, kind="Internal").ap()
    dfb_d = nc.dram_tensor("dfb", [NG, 128, 4, 2, G], BF16, kind="Internal").ap()
    DK = lambda n: (("dram", n),)

    def dv(ap, name=None):
        return V(ap, DK(name) if name else ())

    xs = P.sb("xs", [128, 16, 1024], F32)
    cst = P.sb("cst", [128, 11, 128], F32)
    identb = P.sb("identb", [128, 128], BF16)
    bonesb = P.sb("bonesb", [128, 128], BF16)
    ehp = P.sb("ehp", [4, 2, 128], F32)
    rmask = P.sb("rmask", [128, 512], F32)
    nwb = P.sb("nwb", [128, 1024], F32)
    pcol = P.sb("pcol", [128, 40], F32)
    p4 = P.sb("p4", [4, 4], F32)
    ss = P.sb("ss", [128, 16], F32)
    rstd = P.sb("rstd", [128, 16], F32)
    hT = P.sb("hT", [128, 8, G], BF16)
    xn = [P.sb("xn%d" % i, [128, 1024], BF16) for i in range(2)]
    junk = P.sb("junk", [128, 1024], BF16)
    wo = P.sb("wo", [128, 8, 1024], BF16)
    SinB = P.sb("SinB", [128, 2, 2, 64], BF16)
    SCR = P.sb_off
    ident = cst[:, 0, :]
    maskL = cst[:, 1, :]
    BDUi = cst[:, 2, :]
    BDUs = cst[:, 3, :]
    BDLs = cst[:, 4, :]
    ones_f = cst[:, 5, :]
    eye2 = cst[:, 7, 0:64]
    sel = cst[:, 8, 0:4]
    nBDUs = cst[:, 9, :]
    nBDLs = cst[:, 10, :]

    pb = [P.ps("pb%d" % i, [128, 512], F32) for i in range(7)]
    pT = P.ps("pT", [128, 8, 128], BF16)
    rot = {"i": 0}

    def rbank():
        b = pb[2 + rot["i"] % 4]
        rot["i"] += 1
        return b

    wide = {"i": 0}

    def wbank():
        b = pb[wide["i"] % 6]
        wide["i"] += 1
        return b

    mmi = {"i": 0}

    def mbank():
        b = pb[mmi["i"] % 2]
        mmi["i"] += 1
        return b

    P.I("sp", "dma_start", out=cst.all, in_=dv(cst_d))
    P.I("sp", "dma_start", out=ehp.all, in_=dv(ehp_d))
    P.I("sp", "dma_start", out=rmask.all, in_=dv(rmask_d))
    P.I("dve", "tensor_copy", out=identb.all, in_=ident)
    P.I("dve", "tensor_copy", out=bonesb.all, in_=cst[:, 6, :])
    for t in range(16):
        P.I("sp", "dma_start", out=xs[:, t, :], in_=dv(x_d[t * 128:(t + 1) * 128, :]))

    def norm_group(g, nidx_loaded):
        for tt in range(4):
            Tt = g * 4 + tt
            P.I("act", "activation", out=junk.all, in_=xs[:, Tt, :], func=AF.Square, accum_out=ss[:, Tt:Tt + 1])
            P.I("act", "activation", out=rstd[:, Tt:Tt + 1], in_=ss[:, Tt:Tt + 1], func=AF.Sqrt, bias=EPS, scale=1.0 / D_MODEL)
            P.I("dve", "reciprocal", out=rstd[:, Tt:Tt + 1], in_=rstd[:, Tt:Tt + 1])
            xb = xn[tt % 2]
            P.I("dve", "scalar_tensor_tensor", out=xb.all, in0=xs[:, Tt, :], scalar=rstd[:, Tt:Tt + 1], in1=nwb.all,
                op0=ALU.mult, op1=ALU.mult)
            for fc in range(8):
                P.I("pe", "transpose", out=pT[:, fc, :], in_=xb[:, fc * 128:(fc + 1) * 128], identity=identb.all)
            P.I("act", "activation", out=hT[:, :, tt * 128:(tt + 1) * 128], in_=pT.all, func=AF.Copy)

    def load_nw(idx):
        P.I("sp", "dma_start", out=nwb.all, in_=dv(nrm_d[idx:idx + 1, :].partition_broadcast(128)[:, 0, :]))

    def ffn_group(l, g, S):
        gT, wg, wd, sg = S["gT"], S["wg"], S["wd"], S["sg"]
        nblk = (NJ + 1) // 2
        for jb in range(nblk):
            j0 = jb * 2
            nj = min(2, NJ - j0)
            w = wg[jb % 3]
            P.I("pool", "dma_start", out=w[:, :, 0, 0:nj * 128],
                in_=dv(w_gu_d[l][:, j0 * 128:(j0 + nj) * 128].rearrange("(kc p) n -> p kc n", p=128)))
            P.I("pool", "dma_start", out=w[:, :, 1, 0:nj * 128],
                in_=dv(w_gu_d[l][:, D_FF + j0 * 128:D_FF + (j0 + nj) * 128].rearrange("(kc p) n -> p kc n", p=128)))
            for jj in range(nj):
                j = j0 + jj
                pa, pu = wbank(), wbank()
                for kc in range(8):
                    P.I("pe", "matmul", out=pa.all, lhsT=w[:, kc, 0, jj * 128:(jj + 1) * 128], rhs=hT[:, kc, :],
                        start=(kc == 0), stop=(kc == 7))
                for kc in range(8):
                    P.I("pe", "matmul", out=pu.all, lhsT=w[:, kc, 1, jj * 128:(jj + 1) * 128], rhs=hT[:, kc, :],
                        start=(kc == 0), stop=(kc == 7))
                s_ = sg[j % 2]
                P.I("act", "activation", out=s_.all, in_=pa.all, func=AF.Silu)
                P.I("dve", "tensor_tensor", out=gT[:, j, :], in0=s_.all, in1=pu.all, op=ALU.mult)
        for half in range(2):
            w = wd[half]
            P.I("pool", "dma_start", out=w.all,
                in_=dv(w_dn_d[l][:, half * 512:(half + 1) * 512].rearrange("(j p) n -> p j n", p=128)))
            for tt in range(4):
                Tt = g * 4 + tt
                pa = wbank()
                for j in range(NJ):
                    P.I("pe", "matmul", out=pa.all, lhsT=gT[:, j, tt * 128:(tt + 1) * 128], rhs=w[:, j, :],
                        start=(j == 0), stop=(j == NJ - 1))
                P.I("dve", "tensor_tensor", out=xs[:, Tt, half * 512:(half + 1) * 512],
                    in0=xs[:, Tt, half * 512:(half + 1) * 512], in1=pa.all, op=ALU.add)

    def alloc_ffn_at(_unused=None):
        o = {"i": SCR}

        def a(name, shape, dt):
            t = P.sb(name, shape, dt, at=o["i"])
            esz = 2 if dt == BF16 else 4
            n = 1
            for s in shape[1:]:
                n *= s
            o["i"] += ((n * esz + 63) // 64) * 64
            return t
        S = {}
        S["gT"] = a("gT", [128, NJ, G], BF16)
        S["wg"] = [a("wg%d" % i, [128, 8, 2, 256], BF16) for i in range(3)]
        S["wd"] = [a("wd%d" % i, [128, NJ, 512], BF16) for i in range(2)]
        S["sg"] = [a("sg%d" % i, [128, G], F32) for i in range(2)]
        return S


    class Arena:
        def __init__(self, base):
            self.o = base

        def a(self, name, shape, dt):
            t = P.sb(name, shape, dt, at=self.o)
            esz = 2 if dt == BF16 else 4
            n = 1
            for s_ in shape[1:]:
                n *= s_
            self.o += ((n * esz + 63) // 64) * 64
            return t

    AR = Arena(SCR)
    wi = [AR.a("wi%d" % i, [128, 8, 528], BF16) for i in range(2)]
    mixAB = AR.a("mixAB", [128, 4, G], BF16)
    hTh = AR.a("hTh", [128, 8, 4], BF16)
    WsT = AR.a("WsT", [128, 4, 128], BF16)
    bsb = AR.a("bsb", [128, 2, 128], F32)
    lnw = AR.a("lnw", [128, 2, 256], F32)
    wg2b = AR.a("wg2b", [16, 256], BF16)
    haloB = AR.a("haloB", [128, 2, 4], F32)
    haloC = AR.a("haloC", [128, 6, 4], F32)
    XC = AR.a("XC", [128, 2, 128], F32)
    SD = AR.a("SD", [128, 2, 64], F32)
    DcD = AR.a("DcD", [128, 2, 33], F32)
    packS = AR.a("packS", [128, 392], F32)
    R1 = AR.o
    S1 = Arena(R1)
    xht = S1.a("xht", [4, 1024], F32)
    xhn = S1.a("xhn", [4, 1024], BF16)
    gxh = S1.a("gxh", [16, 1024], F32)
    wsp = S1.a("wsp", [128, 4, 128], F32)
    A1 = Arena(R1)
    uA = A1.a("uA", [128, 2, G], F32)
    vgA = [A1.a("vgA%d" % i, [128, 256], F32) for i in range(2)]
    vlnA = [A1.a("vlnA%d" % i, [128, 256], BF16) for i in range(2)]
    stA = A1.a("stA", [128, 8], F32)
    tmpA = A1.a("tmpA", [128, 2, 128], F32)
    xgA = A1.a("xgA", [128, G], F32)
    tgA = A1.a("tgA", [128, G], F32)
    B1 = Arena(R1)
    gbB = B1.a("gbB", [128, 2, G], F32)
    gcB = B1.a("gcB", [128, G], F32)
    tB = B1.a("tB", [128, 2, G + 2], F32)
    accB = B1.a("accB", [128, G], F32)
    D1 = Arena(R1)
    qfD = D1.a("qfD", [128, 2, G], F32)
    kfD = D1.a("kfD", [128, 2, G], F32)
    vtD = D1.a("vtD", [128, 4, 256], BF16)
    glrD = D1.a("glrD", [16, G], BF16)
    zsD = D1.a("zsD", [128, 2, G], BF16)
    efD = D1.a("efD", [128, G], F32)
    csD = D1.a("csD", [128, 2, G], F32)
    d1D = D1.a("d1D", [128, G], F32)
    aeD = D1.a("aeD", [128, G], F32)
    a3D = D1.a("a3D", [128, 2, G], F32)
    qaD = D1.a("qaD", [128, 2, G], BF16)
    kaD = D1.a("kaD", [128, 2, G], BF16)
    qgD = D1.a("qgD", [128, 2, G], BF16)
    klD = D1.a("klD", [128, 2, G], BF16)
    qgDD = D1.a("qgDD", [128, 2, G], BF16)
    kltD = D1.a("kltD", [128, 4, 256], BF16)
    atD = [D1.a("atD%d" % i, [128, 4, 128], BF16) for i in range(2)]
    SbD = [D1.a("SbD%d" % i, [128, 2, 2, 64], BF16) for i in range(2)]
    ostD = D1.a("ostD", [128, 2, G], F32)
    R2 = R1
    C1 = Arena(R2)
    cbC = C1.a("cbC", [128, G + 3], F32)
    accC = C1.a("accC", [128, G], F32)
    qsC = accC
    sqC = C1.a("sqC", [128, G], BF16)
    rtC = C1.a("rtC", [128, G], F32)
    qTb = C1.a("qTb", [128, 2, G], BF16)
    kTb = C1.a("kTb", [128, 2, G], BF16)
    kbTb = C1.a("kbTb", [128, 2, G], BF16)
    qgTb = C1.a("qgTb", [128, 2, G], BF16)
    vTb = C1.a("vTb", [128, 2, G], BF16)
    zsC = C1.a("zsC", [128, 2, G], BF16)
    t1C = C1.a("t1C", [4, G], F32)
    gcC = C1.a("gcC", [4, G], F32)
    egC = C1.a("egC", [4, G], F32)
    btC = C1.a("btC", [4, G], F32)
    tokq = C1.a("tokq", [128, 4, 4, 4], F32)
    tokb = C1.a("tokb", [128, 4, 4], F32)
    decb = C1.a("decb", [128, 2, 8], F32)
    vbt = [C1.a("vbt%d" % i, [128, 4, 64], BF16) for i in range(2)]
    kbgt = [C1.a("kbgt%d" % i, [128, 4, 64], BF16) for i in range(2)]
    kdt = [C1.a("kdt%d" % i, [128, 4, 64], BF16) for i in range(2)]
    bufA = [C1.a("bufA%d" % i, [128, 4, 128], F32) for i in range(2)]
    bufB = [C1.a("bufB%d" % i, [128, 4, 128], F32) for i in range(2)]
    bufC = [C1.a("bufC%d" % i, [128, 4, 128], F32) for i in range(2)]
    Nb = [[C1.a("Nb%d_%d" % (j, i), [128, 4, 128], BF16) for i in range(2)] for j in range(2)]
    NTb = [[C1.a("NTb%d_%d" % (j, i), [128, 4, 128], BF16) for i in range(2)] for j in range(2)]
    atC = [C1.a("atC%d" % i, [128, 4, 128], BF16) for i in range(2)]
    TTb = [[C1.a("TTb%d_%d" % (j, i), [128, 4, 128], BF16) for i in range(2)] for j in range(2)]
    uaug = [C1.a("uaug%d" % i, [128, 4, 128], F32) for i in range(2)]
    wTb = [C1.a("wTb%d" % i, [128, 2, 128], BF16) for i in range(2)]
    _vnb = C1.a("vnb", [128, 4, 128], BF16)
    vnb = [_vnb, _vnb]
    _XbC = C1.a("XbC", [128, 2, 2, 128], BF16)
    XbC = [_XbC, _XbC]
    ostC = C1.a("ostC", [128, 2, G], F32)
    PTC = C1.a("PTC", [128, 2, G], BF16)
    END_LOCAL = C1.o
    Q1 = Arena(R1)
    gat = Q1.a("gat", [128, 4, 392], F32)
    curC = Q1.a("curC", [128, 2, 64], F32)
    curD = Q1.a("curD", [128, 2, 64], F32)
    accS = Q1.a("accS", [128, 2, 2, 64], F32)
    MTf = Q1.a("MTf", [64, 2, 128], F32)
    curF = Q1.a("curF", [64, 2, 64], F32)
    ost = Q1.a("ost", [128, 2, 2, G], F32)
    dbf = Q1.a("dbf", [128, 4, 2, G], BF16)
    ofQ = Q1.a("ofQ", [128, G], F32)
    sqQ = Q1.a("sqQ", [128, G], BF16)
    rtQ = Q1.a("rtQ", [128, G], F32)
    t1Q = Q1.a("t1Q", [128, G], F32)
    mixCD = Q1.a("mixCD", [128, 4, G], BF16)
    print("SBUF plan: SCR=%d R1=%d R2=%d endlocal=%d endpost=%d" % (SCR, R1, R2, END_LOCAL, Q1.o))
    SF = alloc_ffn_at(max(Q1.o, 0) if False else None)

    def rows(h):
        return slice((h % 2) * 64, (h % 2) * 64 + 64)

    def pc(col):
        return pcol[:, col:col + 1]

    def layer_setup(l):
        P.I("sp", "dma_start", out=pcol.all, in_=dv(pcol_d[l]))
        P.I("sp", "dma_start", out=p4[:, 0:2], in_=dv(p4_d[l]))
        P.I("dve", "tensor_scalar", out=pcol[:, 34:36], in0=pcol[:, 30:32], scalar1=-1.0, scalar2=None, op0=ALU.mult)
        P.I("act", "activation", out=p4[:, 2:3], in_=p4[:, 0:1], func=AF.Exp)
        P.I("dve", "tensor_scalar", out=p4[:, 2:3], in0=p4[:, 2:3], scalar1=-1.0, scalar2=None, op0=ALU.mult)
        for hf in range(2):
            P.I("pool", "dma_start", out=wo[:, hf * 4:(hf + 1) * 4, :],
                in_=dv(w_out_d[l][hf * 512:(hf + 1) * 512, :].rearrange("(kc p) n -> p kc n", p=128)))
        P.I("sp", "dma_start", out=wsp.all, in_=dv(wsp_d[l].rearrange("h t s -> t h s")))
        P.I("dve", "tensor_tensor", out=wsp.all, in0=wsp.all, in1=maskL[:, None, :].b([128, 4, 128]), op=ALU.mult)
        bk = rbank()
        for h in range(4):
            P.I("pe", "transpose", out=bk[:, h * 128:(h + 1) * 128], in_=wsp[:, h, :], identity=ident)
        P.I("act", "activation", out=WsT.all, in_=bk.all, func=AF.Copy)
        for h in range(4):
            P.I("sp", "dma_start", out=bsb[rows(h), h // 2, :], in_=dv(bsp_d[l][h:h + 1, :].partition_broadcast(64)[:, 0, :]))
        for i in range(2):
            P.I("sp", "dma_start", out=lnw[:, i, :], in_=dv(lnwb_d[l][i:i + 1, :].partition_broadcast(128)[:, 0, :]))
        P.I("pool", "dma_start", out=wg2b.all, in_=dv(wg2_d[l]))
        P.I("dve", "memset", ap=XC.all, constant=0.0)
        for hp in range(2):
            P.I("dve", "tensor_copy", out=XC[:, hp, 64:128], in_=eye2)
        P.I("dve", "memset", ap=SD.all, constant=0.0)
        P.I("dve", "memset", ap=DcD.all, constant=1.0)
        for i in range(2):
            P.I("dve", "memset", ap=uaug[i].all, constant=0.0)
        P.I("dve", "memset", ap=packS.all, constant=0.0)

    def halo_prep(l, from_x_dram):
        if from_x_dram is not None:
            P.I("sp", "dma_start", out=xht.all, in_=from_x_dram)
        else:
            P.I("sp", "dma_start", out=gxh.all, in_=dv(xgath, "xgath"))
            for half in range(2):
                bk = rbank()
                P.I("pe", "matmul", out=bk[0:4, :], lhsT=cst[0:16, 8, 4:8], rhs=gxh[:, half * 512:(half + 1) * 512], start=True, stop=True)
                P.I("act", "activation", out=xht[:, half * 512:(half + 1) * 512], in_=bk[0:4, :], func=AF.Copy)
        P.I("act", "activation", out=xhn.all, in_=xht.all, func=AF.Square, accum_out=p4[:, 3:4])
        P.I("act", "activation", out=p4[:, 3:4], in_=p4[:, 3:4], func=AF.Sqrt, bias=EPS, scale=1.0 / D_MODEL)
        P.I("dve", "reciprocal", out=p4[:, 3:4], in_=p4[:, 3:4])
        P.I("dve", "scalar_tensor_tensor", out=xhn.all, in0=xht.all, scalar=p4[:, 3:4], in1=nwb[0:4, :],
            op0=ALU.mult, op1=ALU.mult)
        for fc in range(8):
            P.I("pe", "transpose", out=pT[:, fc, 0:4], in_=xhn[:, fc * 128:(fc + 1) * 128], identity=identb[0:4, 0:4])
        P.I("act", "activation", out=hTh.all, in_=pT[:, :, 0:4], func=AF.Copy)

    def ip_fm(w, c0, n, rhs_of_kc=None, N=G):
        bk = mbank()
        for kc in range(8):
            P.I("pe", "matmul", out=bk[0:n, 0:N], lhsT=w[:, kc, c0:c0 + n],
                rhs=(hT[:, kc, :] if rhs_of_kc is None else rhs_of_kc(kc)), start=(kc == 0), stop=(kc == 7))
        return bk

    def ip_tm(w, c0, n, tt):
        bk = mbank()
        for kc in range(8):
            P.I("pe", "matmul", out=bk[:, 0:n], lhsT=hT[:, kc, tt * 128:(tt + 1) * 128], rhs=w[:, kc, c0:c0 + n],
                start=(kc == 0), stop=(kc == 7))
        return bk

    wslot = {"i": 0}

    def load_wi(l, c0, n):
        w = wi[wslot["i"] % 2]
        wslot["i"] += 1
        P.I("pool", "dma_start", out=w[:, :, 0:n], in_=dv(w_in_d[l][:, c0:c0 + n].rearrange("(kc p) n -> p kc n", p=128)))
        return w

    def gelu_tanh(out_v, ps_v, xg, tg):
        P.I("act", "activation", out=xg, in_=ps_v, func=AF.Copy)
        P.I("dve", "tensor_tensor", out=tg, in0=xg, in1=xg, op=ALU.mult)
        P.I("dve", "tensor_scalar", out=tg, in0=tg, scalar1=0.044715, scalar2=1.0, op0=ALU.mult, op1=ALU.add)
        P.I("dve", "tensor_tensor", out=tg, in0=tg, in1=xg, op=ALU.mult)
        P.I("act", "activation", out=tg, in_=tg, func=AF.Sigmoid, scale=1.5957691216057308)
        P.I("dve", "tensor_tensor", out=out_v, in0=xg, in1=tg, op=ALU.mult)

    def mixer_A(l, g, first):
        w = load_wi(l, 0, 512)
        for c in range(2):
            bk = ip_fm(w, C_AU + c * 128, 128)
            gelu_tanh(uA[:, c, :], bk.all, xgA.all, tgA.all)
        tmpA2 = V(tmpA.ap.rearrange("p c t -> p (c t)"), tmpA.all.keys)
        for tt in range(4):
            bk = ip_tm(w, C_AV, 256, tt)
            vg, vl = vgA[tt % 2], vlnA[tt % 2]
            gelu_tanh(vg.all, bk[:, 0:256], xgA[:, 0:256], tgA[:, 0:256])
            P.I("dve", "tensor_reduce", out=stA[:, 0:1], in_=vg.all, axis=mybir.AxisListType.X, op=ALU.add)
            P.I("act", "activation", out=stA[:, 4:5], in_=stA[:, 0:1], func=AF.Copy, scale=-1.0 / 256)
            P.I("dve", "tensor_scalar", out=vg.all, in0=vg.all, scalar1=stA[:, 4:5], scalar2=None, op0=ALU.add)
            P.I("act", "activation", out=tmpA2, in_=vg.all, func=AF.Square, accum_out=stA[:, 1:2])
            P.I("act", "activation", out=stA[:, 2:3], in_=stA[:, 1:2], func=AF.Sqrt, bias=EPS, scale=1.0 / 256)
            P.I("dve", "reciprocal", out=stA[:, 2:3], in_=stA[:, 2:3])
            P.I("dve", "scalar_tensor_tensor", out=vg.all, in0=vg.all, scalar=stA[:, 2:3], in1=lnw[:, 0, :], op0=ALU.mult, op1=ALU.mult)
            P.I("dve", "tensor_tensor", out=vl.all, in0=vg.all, in1=lnw[:, 1, :], op=ALU.add)
            bm = rbank()
            for h in range(4):
                P.I("pe", "matmul", out=bm[rows(h), (h // 2) * 128:(h // 2) * 128 + 128], lhsT=vl[:, h * 64:(h + 1) * 64],
                    rhs=WsT[:, h, :], start=True, stop=True)
            P.I("dve", "tensor_tensor", out=tmpA.all, in0=V(bm.ap[:, 0:256].rearrange("p (c t) -> p c t", t=128), bm[:, 0:256].keys), in1=bsb.all, op=ALU.add)
            P.I("dve", "tensor_tensor", out=mixAB[:, 0:2, tt * 128:(tt + 1) * 128], in0=tmpA.all,
                in1=uA[:, :, tt * 128:(tt + 1) * 128], op=ALU.mult)

    def mixer_B(l, g, first):
        w = load_wi(l, 512, 512)
        w2 = load_wi(l, 1024, 512)
        for c in range(2):
            bk = ip_fm(w, c * 128, 128)
            P.I("act", "activation", out=gbB[:, c, :], in_=bk.all, func=AF.Copy)
        for c in range(2):
            bk = ip_fm(w, 256 + c * 128, 128)
            P.I("act", "activation", out=gcB.all, in_=bk.all, func=AF.Copy)
            if first:
                b3 = ip_fm(w, 256 + c * 128, 128, rhs_of_kc=lambda kc: hTh[:, kc, :], N=4)
                P.I("act", "activation", out=haloB[:, c, :], in_=b3[:, 0:4], func=AF.Copy)
                b4 = ip_fm(w2, c * 128, 128, rhs_of_kc=lambda kc: hTh[:, kc, :], N=4)
                P.I("dve", "tensor_tensor", out=haloB[:, c, :], in0=haloB[:, c, :], in1=b4[:, 0:4], op=ALU.mult)
            P.I("dve", "tensor_copy", out=tB[:, c, 0:2], in_=haloB[:, c, 2:4])
            bk2 = ip_fm(w2, c * 128, 128)
            P.I("dve", "tensor_tensor", out=tB[:, c, 2:G + 2], in0=gcB.all, in1=bk2.all, op=ALU.mult)
            k0 = c * 3
            P.I("dve", "tensor_scalar", out=accB.all, in0=tB[:, c, 2:G + 2], scalar1=pc(k0 + 2), scalar2=None, op0=ALU.mult)
            P.I("dve", "scalar_tensor_tensor", out=accB.all, in0=tB[:, c, 1:G + 1], scalar=pc(k0 + 1), in1=accB.all, op0=ALU.mult, op1=ALU.add)
            P.I("dve", "scalar_tensor_tensor", out=accB.all, in0=tB[:, c, 0:G], scalar=pc(k0 + 0), in1=accB.all, op0=ALU.mult, op1=ALU.add)
            P.I("dve", "tensor_tensor", out=mixAB[:, 2 + c, :], in0=accB.all, in1=gbB[:, c, :], op=ALU.mult)
            P.I("dve", "tensor_copy", out=haloB[:, c, 2:4], in_=tB[:, c, G:G + 2])
        return w2

    def wout_part(g, mixbuf, kc0):
        for tt in range(4):
            Tt = g * 4 + tt
            for half in range(2):
                bk = wbank()
                for k in range(4):
                    P.I("pe", "matmul", out=bk.all, lhsT=mixbuf[:, k, tt * 128:(tt + 1) * 128],
                        rhs=wo[:, kc0 + k, half * 512:(half + 1) * 512], start=(k == 0), stop=(k == 3))
                P.I("dve", "tensor_tensor", out=xs[:, Tt, half * 512:(half + 1) * 512],
                    in0=xs[:, Tt, half * 512:(half + 1) * 512], in1=bk.all, op=ALU.add)

    def mixer_D(l, g, first):
        w = load_wi(l, C_DQ, 512)
        for c in range(2):
            bk = ip_fm(w, c * 128, 128)
            P.I("act", "activation", out=qfD[:, c, :], in_=bk.all, func=AF.Copy)
        for c in range(2):
            bk = ip_fm(w, 256 + c * 128, 128)
            P.I("act", "activation", out=kfD[:, c, :], in_=bk.all, func=AF.Copy)
        w = load_wi(l, C_DV, 528)
        for tt in range(4):
            bk = ip_tm(w, 0, 256, tt)
            P.I("act", "activation", out=vtD[:, tt, :], in_=bk[:, 0:256], func=AF.Copy)
        bk = ip_fm(w, 256, 16)
        P.I("act", "activation", out=glrD.all, in_=bk[0:16, :], func=AF.Copy)
        for c in range(2):
            bk = ip_fm(w, 272 + c * 128, 128)
            P.I("act", "activation", out=zsD[:, c, :], in_=bk.all, func=AF.Silu)
        if DBG.get("dcut", 99) <= 1:
            return
        for c in range(2):
            bk = mbank()
            P.I("pe", "matmul", out=bk.all, lhsT=wg2b[:, c * 128:(c + 1) * 128], rhs=glrD.all, start=True, stop=True)
            P.I("act", "activation", out=efD.all, in_=bk.all, func=AF.Exp, bias=pc(34 + c), scale=-1.0)
            P.I("act", "activation", out=efD.all, in_=efD.all, func=AF.Ln, bias=1.0, scale=1.0)
            P.I("dve", "tensor_tensor_scan", out=csD[:, c, :], data0=rmask.all, data1=efD.all, initial=0.0, op0=ALU.mult, op1=ALU.add)
            cs3 = V(csD.ap[:, c, :].rearrange("p (n t) -> p n t", t=64), csD[:, c, :].keys)
            d13 = V(d1D.ap.rearrange("p (n t) -> p n t", t=64), d1D.all.keys)
            P.I("dve", "tensor_tensor", out=d13, in0=cs3, in1=cs3[:, :, 32:33].b([128, 8, 64]), op=ALU.subtract)
            P.I("act", "activation", out=aeD.all, in_=d1D.all, func=AF.Exp, scale=-1.0 / 16)
            P.I("dve", "scalar_tensor_tensor", out=qaD[:, c, :], in0=qfD[:, c, :], scalar=0.125, in1=aeD.all, op0=ALU.mult, op1=ALU.mult)
            P.I("act", "activation", out=aeD.all, in_=d1D.all, func=AF.Exp, scale=1.0 / 16)
            P.I("dve", "tensor_tensor", out=kaD[:, c, :], in0=kfD[:, c, :], in1=aeD.all, op=ALU.mult)
            P.I("act", "activation", out=a3D[:, c, :], in_=csD[:, c, :], func=AF.Exp, scale=-1.0 / 16)
            P.I("dve", "scalar_tensor_tensor", out=qgD[:, c, :], in0=qfD[:, c, :], scalar=0.125, in1=a3D[:, c, :], op0=ALU.mult, op1=ALU.mult)
            P.I("dve", "tensor_tensor", out=d13, in0=cs3, in1=cs3[:, :, 63:64].b([128, 8, 64]), op=ALU.subtract)
            P.I("act", "activation", out=aeD.all, in_=d1D.all, func=AF.Exp, scale=1.0 / 16)
            P.I("dve", "tensor_tensor", out=klD[:, c, :], in0=kfD[:, c, :], in1=aeD.all, op=ALU.mult)
        if DBG.get("dcut", 99) <= 2:
            return
        a34 = V(a3D.ap.rearrange("p c (n t) -> p c n t", t=64), a3D.all.keys)
        for n in range(8):
            ng = g * 8 + n
            P.I("dve", "tensor_tensor", out=DcD[:, :, ng + 1], in0=DcD[:, :, ng], in1=a34[:, :, n, 63], op=ALU.mult)
        for c in range(2):
            q3 = V(qgD.ap[:, c, :].rearrange("p (n t) -> p n t", t=64), qgD[:, c, :].keys)
            o3 = V(qgDD.ap[:, c, :].rearrange("p (n t) -> p n t", t=64), qgDD[:, c, :].keys)
            P.I("dve", "tensor_tensor", out=o3, in0=q3, in1=DcD[:, c, g * 8:g * 8 + 8][:, :, None].b([128, 8, 64]), op=ALU.mult)
        if DBG.get("dcut", 99) <= 3:
            return
        for tt in range(4):
            for c in range(2):
                P.I("pe", "transpose", out=pT[:, c, :], in_=klD[:, c, tt * 128:(tt + 1) * 128], identity=identb.all)
            P.I("act", "activation", out=kltD[:, tt, :], in_=pT[:, 0:2, :], func=AF.Copy)
        if DBG.get("dcut", 99) <= 4:
            return
        for tt in range(4):
            tok = slice(tt * 128, (tt + 1) * 128)
            at = atD[tt % 2]
            bkp = [rbank(), rbank()]
            for h in range(4):
                P.I("pe", "matmul", out=bkp[h % 2][:, (h // 2) * 128:(h // 2) * 128 + 128], lhsT=kaD[rows(h), h // 2, tok], rhs=qaD[rows(h), h // 2, tok],
                    start=True, stop=True)
            at4 = V(at.ap.rearrange("p (c r) s -> p c r s", r=2), at.all.keys)
            for par in range(2):
                P.I("dve", "tensor_tensor", out=at4[:, :, par, :], in0=V(bkp[par].ap[:, 0:256].rearrange("p (c s) -> p c s", s=128), bkp[par][:, 0:256].keys),
                    in1=BDUi[:, None, :].b([128, 2, 128]), op=ALU.mult)
            if DBG.get("dsub", 9) <= 1:
                continue
            bsn = [rbank(), rbank()]
            for n2 in range(2):
                rn = slice(n2 * 64, n2 * 64 + 64)
                for h in range(4):
                    P.I("pe", "matmul", out=bsn[n2][rows(h), (h // 2) * 64:(h // 2) * 64 + 64],
                        lhsT=kltD[rn, tt, h * 64:(h + 1) * 64], rhs=vtD[rn, tt, h * 64:(h + 1) * 64], start=True, stop=True)
            Sb = SbD[tt % 2]
            for n2 in range(2):
                n = tt * 2 + n2
                P.I("act", "activation", out=Sb[:, n2, :, :], in_=SD.all, func=AF.Copy)
                for hp in range(2):
                    P.I("dve", "scalar_tensor_tensor", out=SD[:, hp, :], in0=SD[:, hp, :], scalar=a34[:, hp, n, 63:64],
                        in1=bsn[n2][:, hp * 64:hp * 64 + 64], op0=ALU.mult, op1=ALU.add)
            if DBG.get("dsub", 9) <= 2:
                continue
            pop = [pb[6], rbank()]
            pox = [rbank(), rbank()]
            for h in range(4):
                hp = h // 2
                P.I("pe", "matmul", out=pop[h % 2][rows(h), hp * 128:(hp + 1) * 128], lhsT=vtD[:, tt, h * 64:(h + 1) * 64], rhs=at[:, h, :],
                    start=True, stop=True)
                for n2 in range(2):
                    P.I("pe", "matmul", out=pox[h % 2][rows(h), hp * 128 + n2 * 64:hp * 128 + n2 * 64 + 64], lhsT=Sb[rows(h), n2, hp, :],
                        rhs=qgD[rows(h), hp, tt * 128 + n2 * 64:tt * 128 + n2 * 64 + 64], start=True, stop=True)
            for par in range(2):
                pr = slice(par * 64, par * 64 + 64)
                P.I("act", "activation", out=ostD[pr, :, tok], in_=V(pop[par].ap[pr, 0:256].rearrange("p (c t) -> p c t", t=128), pop[par][:, 0:256].keys), func=AF.Copy)
                P.I("dve", "tensor_tensor", out=ostD[pr, :, tok], in0=ostD[pr, :, tok],
                    in1=V(pox[par].ap[pr, 0:256].rearrange("p (c t) -> p c t", t=128), pox[par][:, 0:256].keys), op=ALU.add)
        if DBG.get("dcut", 99) <= 5:
            return
        P.I("sp", "dma_start", out=dv(dfo_d[g][:, 1], "dfo%d" % g), in_=ostD.all)
        P.I("sp", "dma_start", out=dv(dfb_d[g][:, 1], "dfb%d" % g), in_=qgDD.all)
        P.I("sp", "dma_start", out=dv(dfb_d[g][:, 3], "dfb%d" % g), in_=zsD.all)

    def mixer_C(l, g, first, w2):
        wk = load_wi(l, C_CK, 512)
        srcs = [(w2, 256), (w2, 384), (wk, 0), (wk, 128), (wk, 256), (wk, 384)]
        for ci, (w, c0) in enumerate(srcs):
            bk = ip_fm(w, c0, 128)
            P.I("act", "activation", out=cbC[:, 3:G + 3], in_=bk.all, func=AF.Copy)
            if first:
                b3 = ip_fm(w, c0, 128, rhs_of_kc=lambda kc: hTh[:, kc, :], N=4)
                P.I("act", "activation", out=haloC[:, ci, :], in_=b3[:, 0:4], func=AF.Copy)
            P.I("dve", "tensor_copy", out=cbC[:, 0:3], in_=haloC[:, ci, 1:4])
            k0 = 6 + ci * 4
            P.I("act", "activation", out=accC.all, in_=cbC[:, 3:G + 3], func=AF.Copy, scale=pc(k0 + 3))
            for k in range(3):
                P.I("dve", "scalar_tensor_tensor", out=accC.all, in0=cbC[:, k:G + k], scalar=pc(k0 + k), in1=accC.all, op0=ALU.mult, op1=ALU.add)
            P.I("dve", "tensor_copy", out=haloC[:, ci, 1:4], in_=cbC[:, G:G + 3])
            if ci < 4:
                P.I("act", "activation", out=qsC.all, in_=accC.all, func=AF.Silu)
                P.I("act", "activation", out=sqC.all, in_=qsC.all, func=AF.Square)
                bq = rbank()
                P.I("pe", "matmul", out=bq.all, lhsT=bonesb.all, rhs=sqC.all, start=True, stop=True)
                P.I("act", "activation", out=rtC.all, in_=bq.all, func=AF.Sqrt, bias=EPS, scale=1.0)
                P.I("dve", "reciprocal", out=rtC.all, in_=rtC.all)
                if ci < 2:
                    P.I("dve", "scalar_tensor_tensor", out=qTb[:, ci, :], in0=qsC.all, scalar=0.125, in1=rtC.all, op0=ALU.mult, op1=ALU.mult)
                else:
                    P.I("dve", "tensor_tensor", out=kTb[:, ci - 2, :], in0=qsC.all, in1=rtC.all, op=ALU.mult)
            else:
                P.I("act", "activation", out=vTb[:, ci - 4, :], in_=accC.all, func=AF.Silu)
        if DBG.get("ccut", 99) <= 1:
            return
        wz = load_wi(l, C_CA, 264)
        bka = ip_fm(wz, 0, 4)
        P.I("act", "activation", out=t1C.all, in_=bka[0:4, :], func=AF.Exp, bias=p4[:, 1:2], scale=1.0)
        P.I("act", "activation", out=t1C.all, in_=t1C.all, func=AF.Ln, bias=1.0, scale=1.0)
        P.I("dve", "tensor_scalar", out=t1C.all, in0=t1C.all, scalar1=p4[:, 2:3], scalar2=None, op0=ALU.mult)
        P.I("dve", "tensor_tensor_scan", out=gcC.all, data0=rmask[0:4, :], data1=t1C.all, initial=0.0, op0=ALU.mult, op1=ALU.add)
        bkb = ip_fm(wz, 4, 4)
        P.I("act", "activation", out=btC.all, in_=bkb[0:4, :], func=AF.Exp, scale=-1.0)
        P.I("dve", "tensor_scalar", out=btC.all, in0=btC.all, scalar1=1.0, scalar2=None, op0=ALU.add)
        P.I("dve", "reciprocal", out=btC.all, in_=btC.all)
        for c in range(2):
            bk = ip_fm(wz, 8 + c * 128, 128)
            P.I("act", "activation", out=zsC[:, c, :], in_=bk.all, func=AF.Silu)
        gc3 = V(gcC.ap.rearrange("p (n t) -> p n t", t=64), gcC.all.keys)
        t13 = V(t1C.ap.rearrange("p (n t) -> p n t", t=64), t1C.all.keys)
        P.I("dve", "tensor_tensor", out=t13, in0=gc3[:, :, 63:64].b([4, 8, 64]), in1=gc3, op=ALU.subtract)
        P.I("act", "activation", out=t1C.all, in_=t1C.all, func=AF.Exp)
        P.I("act", "activation", out=egC.all, in_=gcC.all, func=AF.Exp)
        if DBG.get("ccut", 99) <= 2:
            return
        for hp in range(2):
            bb = rbank()
            P.I("pe", "matmul", out=bb.all, lhsT=ehp[:, hp, :], rhs=btC.all, start=True, stop=True)
            P.I("dve", "tensor_tensor", out=kbTb[:, hp, :], in0=kTb[:, hp, :], in1=bb.all, op=ALU.mult)
            be = rbank()
            P.I("pe", "matmul", out=be.all, lhsT=ehp[:, hp, :], rhs=egC.all, start=True, stop=True)
            P.I("dve", "tensor_tensor", out=qgTb[:, hp, :], in0=qTb[:, hp, :], in1=be.all, op=ALU.mult)
            bd = rbank()
            P.I("pe", "matmul", out=bd[:, 0:8], lhsT=ehp[:, hp, :], rhs=V(egC.ap.rearrange("p (n t) -> p n t", t=64)[:, :, 63], egC.all.keys),
                start=True, stop=True)
            P.I("act", "activation", out=decb[:, hp, :], in_=bd[:, 0:8], func=AF.Copy)
        if DBG.get("ccut", 99) <= 3:
            return
        for tt in range(4):
            tok = slice(tt * 128, (tt + 1) * 128)
            bt_ = rbank()
            for qi, src in enumerate((gcC, btC, t1C, egC)):
                P.I("pe", "transpose", out=bt_[:, qi * 4:qi * 4 + 4], in_=src[:, tok], identity=ident[0:4, 0:4])
            P.I("act", "activation", out=tokq[:, tt, :, :], in_=V(bt_.ap[:, 0:16].rearrange("p (q h) -> p q h", h=4), bt_[:, 0:16].keys), func=AF.Copy)
            P.I("dve", "tensor_tensor", out=tokb[:, tt, :], in0=tokq[:, tt, 1, :], in1=tokq[:, tt, 3, :], op=ALU.mult)
        if DBG.get("ccut", 99) <= 4:
            return
        for i in range(2):
            P.I("dve", "memset", ap=uaug[i][:, :, 64:128], constant=0.0)
        f4 = lambda t_: V(t_.ap.rearrange("p h s -> p (h s)"), t_.all.keys)
        v4 = lambda t_: V(t_.ap.rearrange("p (c r) s -> p c r s", r=2), t_.all.keys)

        def c_prep(tt, sl):
            tok = slice(tt * 128, (tt + 1) * 128)
            bA, bB, bC = bufA[sl], bufB[sl], bufC[sl]
            N_, NT_, TT_ = Nb[sl], NTb[sl], TTb[sl]
            for c in range(2):
                P.I("pe", "transpose", out=pT[:, c, :], in_=kTb[:, c, tok], identity=identb.all)
                P.I("pe", "transpose", out=pT[:, 2 + c, :], in_=vTb[:, c, tok], identity=identb.all)
            kt4 = V(pT.ap[:, 0:2, :].rearrange("p c (h d) -> p (c h) d", d=64), pT[:, 0:2, :].keys)
            vt4 = V(pT.ap[:, 2:4, :].rearrange("p c (h d) -> p (c h) d", d=64), pT[:, 2:4, :].keys)
            P.I("dve", "tensor_tensor", out=vbt[sl].all, in0=vt4, in1=tokq[:, tt, 1, :][:, :, None].b([128, 4, 64]), op=ALU.mult)
            P.I("dve", "tensor_tensor", out=kbgt[sl].all, in0=kt4, in1=tokb[:, tt, :][:, :, None].b([128, 4, 64]), op=ALU.mult)
            P.I("dve", "tensor_tensor", out=kdt[sl].all, in0=kt4, in1=tokq[:, tt, 2, :][:, :, None].b([128, 4, 64]), op=ALU.mult)
            yield
            if DBG.get("pcut", 99) <= 1:
                return
            P.I("dve", "tensor_tensor", out=bA.all, in0=tokq[:, tt, 0, :][:, :, None].b([128, 4, 128]), in1=ident[:, None, :].b([128, 4, 128]), op=ALU.mult)
            bg = rbank()
            P.I("pe", "matmul", out=bg.all, lhsT=ones_f, rhs=f4(bA), start=True, stop=True)
            bg3 = V(bg.ap.rearrange("p (h s) -> p h s", s=128), bg.all.keys)
            P.I("dve", "tensor_tensor", out=bB.all, in0=bg3, in1=tokq[:, tt, 0, :][:, :, None].b([128, 4, 128]), op=ALU.subtract)
            yield
            P.I("dve", "tensor_scalar", out=bC.all, in0=bB.all, scalar1=3.0e38, scalar2=0.0, op0=ALU.min, op1=ALU.max)
            P.I("act", "activation", out=bC.all, in_=bC.all, func=AF.Exp, scale=-1.0)
            P.I("dve", "tensor_scalar", out=bA.all, in0=bB.all, scalar1=0.0, scalar2=-3.0e38, op0=ALU.min, op1=ALU.max)
            P.I("act", "activation", out=bA.all, in_=bA.all, func=AF.Exp, scale=1.0)
            yield
            P.I("dve", "tensor_tensor", out=bC.all, in0=bC.all, in1=nBDLs[:, None, :].b([128, 4, 128]), op=ALU.mult)
            P.I("dve", "tensor_tensor", out=bB.all, in0=bA.all, in1=nBDUs[:, None, :].b([128, 4, 128]), op=ALU.mult)
            P.I("dve", "tensor_tensor", out=bA.all, in0=bA.all, in1=BDUi[:, None, :].b([128, 4, 128]), op=ALU.mult)
            if DBG.get("pcut", 99) <= 2:
                return
            for (lt, rt_, dst, msk) in ((kbTb, kTb, N_[0], bC), (kTb, kbTb, NT_[0], bB), (kTb, qTb, atC[sl], bA)):
                bkp = [rbank(), rbank()]
                for h in range(4):
                    hp = h // 2
                    P.I("pe", "matmul", out=bkp[h % 2][:, hp * 128:(hp + 1) * 128], lhsT=lt[rows(h), hp, tok], rhs=rt_[rows(h), hp, tok], start=True, stop=True)
                for par in range(2):
                    P.I("dve", "tensor_tensor", out=v4(dst)[:, :, par, :], in0=V(bkp[par].ap[:, 0:256].rearrange("p (c s) -> p c s", s=128), bkp[par][:, 0:256].keys),
                        in1=v4(msk)[:, :, par, :], op=ALU.mult)
                yield
            if DBG.get("pcut", 99) <= 3:
                return
            P.I("dve", "tensor_tensor", out=TT_[0].all, in0=NT_[0].all, in1=ident[:, None, :].b([128, 4, 128]), op=ALU.add)
            cur = 0
            for k in range(1, 6):
                bn = rbank()
                for h in range(4):
                    P.I("pe", "matmul", out=bn[:, h * 128:(h + 1) * 128], lhsT=NT_[cur][:, h, :], rhs=N_[cur][:, h, :], start=True, stop=True)
                P.I("act", "activation", out=f4(N_[1 - cur]), in_=bn.all, func=AF.Copy)
                if k < 5:
                    bnt = rbank()
                    for h in range(4):
                        P.I("pe", "matmul", out=bnt[:, h * 128:(h + 1) * 128], lhsT=N_[cur][:, h, :], rhs=NT_[cur][:, h, :], start=True, stop=True)
                    P.I("act", "activation", out=f4(NT_[1 - cur]), in_=bnt.all, func=AF.Copy)
                cur = 1 - cur
                yield
                bt2 = rbank()
                tcur = (k - 1) % 2
                for h in range(4):
                    P.I("pe", "matmul", out=bt2[:, h * 128:(h + 1) * 128], lhsT=N_[cur][:, h, :], rhs=TT_[tcur][:, h, :], start=True, stop=True)
                P.I("dve", "tensor_tensor", out=f4(TT_[1 - tcur]), in0=bt2.all, in1=f4(TT_[tcur]), op=ALU.add)
                yield
            TT = TT_[1]
            if DBG.get("pcut", 99) <= 4:
                return
            bu = rbank()
            for h in range(4):
                P.I("pe", "matmul", out=bu[:, h * 64:(h + 1) * 64], lhsT=TT[:, h, :], rhs=vbt[sl][:, h, :], start=True, stop=True)
            P.I("act", "activation", out=uaug[sl][:, :, 0:64], in_=V(bu.ap[:, 0:256].rearrange("p (h e) -> p h e", e=64), bu[:, 0:256].keys), func=AF.Copy)
            bw = rbank()
            for h in range(4):
                P.I("pe", "matmul", out=bw[rows(h), (h // 2) * 128:(h // 2) * 128 + 128], lhsT=kbgt[sl][:, h, :], rhs=TT[:, h, :], start=True, stop=True)
            P.I("act", "activation", out=wTb[sl].all, in_=V(bw.ap[:, 0:256].rearrange("p (c t) -> p c t", t=128), bw[:, 0:256].keys), func=AF.Copy)
            yield

        def c_scan_out(tt, sl):
            tok = slice(tt * 128, (tt + 1) * 128)
            Xb = XbC[0]
            vn = vnb[0]
            for n2 in range(2):
                n = tt * 2 + n2
                rn = slice(n2 * 64, n2 * 64 + 64)
                P.I("act", "activation", out=Xb[:, n2, :, :], in_=XC.all, func=AF.Copy)
                bvp = [pb[6], rbank()]
                for h in range(4):
                    P.I("pe", "matmul", out=bvp[h % 2][rn, (h // 2) * 128:(h // 2) * 128 + 128], lhsT=wTb[sl][rows(h), h // 2, n2 * 64:n2 * 64 + 64],
                        rhs=Xb[rows(h), n2, h // 2, :], start=True, stop=True)
                vn4 = V(vn.ap.rearrange("p (c r) f -> p c r f", r=2), vn.all.keys)
                ua4 = V(uaug[sl].ap.rearrange("p (c r) f -> p c r f", r=2), uaug[sl].all.keys)
                for par in range(2):
                    P.I("dve", "tensor_tensor", out=vn4[rn, :, par, :], in0=ua4[rn, :, par, :],
                        in1=V(bvp[par].ap[rn, 0:256].rearrange("p (c f) -> p c f", f=128), bvp[par][:, 0:256].keys), op=ALU.subtract)
                bd2 = rbank()
                for h in range(4):
                    P.I("pe", "matmul", out=bd2[rows(h), (h // 2) * 128:(h // 2) * 128 + 128], lhsT=kdt[sl][rn, h, :],
                        rhs=vn[rn, h, :], start=True, stop=True)
                for hp in range(2):
                    P.I("dve", "scalar_tensor_tensor", out=XC[:, hp, :], in0=XC[:, hp, :], scalar=decb[:, hp, n:n + 1],
                        in1=bd2[:, hp * 128:(hp + 1) * 128], op0=ALU.mult, op1=ALU.add)
            pop = [pb[6], rbank()]
            pox = [rbank(), rbank()]
            for part in range(2):
                for h in range(4):
                    hp = h // 2
                    base = part * 256 + hp * 128
                    P.I("pe", "matmul", out=pop[h % 2][rows(h), base:base + 128], lhsT=vn[:, h, part * 64:part * 64 + 64], rhs=atC[sl][:, h, :],
                        start=True, stop=True)
                    for n2 in range(2):
                        P.I("pe", "matmul", out=pox[h % 2][rows(h), base + n2 * 64:base + n2 * 64 + 64],
                            lhsT=Xb[rows(h), n2, hp, part * 64:part * 64 + 64],
                            rhs=qgTb[rows(h), hp, tt * 128 + n2 * 64:tt * 128 + n2 * 64 + 64], start=True, stop=True)
            for par in range(2):
                pr = slice(par * 64, par * 64 + 64)
                P.I("act", "activation", out=ostC[pr, :, tok], in_=V(pop[par].ap[pr, 0:256].rearrange("p (c t) -> p c t", t=128), pop[par][:, 0:256].keys), func=AF.Copy)
                P.I("dve", "tensor_tensor", out=ostC[pr, :, tok], in0=ostC[pr, :, tok],
                    in1=V(pox[par].ap[pr, 0:256].rearrange("p (c t) -> p c t", t=128), pox[par][:, 0:256].keys), op=ALU.add)
                P.I("act", "activation", out=accC[pr, 0:256], in_=pop[par][pr, 256:512], func=AF.Copy)
                P.I("dve", "tensor_tensor", out=PTC[pr, :, tok], in0=V(accC.ap[pr, 0:256].rearrange("p (c t) -> p c t", t=128), accC[:, 0:256].keys),
                    in1=V(pox[par].ap[pr, 256:512].rearrange("p (c t) -> p c t", t=128), pox[par][:, 256:512].keys), op=ALU.add)

        for pair in range(2):
            gens = [c_prep(pair * 2 + i, i) for i in range(2)]
            alive = [True, True]
            while any(alive):
                for i in range(2):
                    if alive[i]:
                        try:
                            next(gens[i])
                        except StopIteration:
                            alive[i] = False
            for i in range(2):
                if DBG.get("pcut", 99) > 5:
                    c_scan_out(pair * 2 + i, i)
        P.I("sp", "dma_start", out=dv(dfo_d[g][:, 0], "dfo%d" % g), in_=ostC.all)
        P.I("sp", "dma_start", out=dv(dfb_d[g][:, 0], "dfb%d" % g), in_=PTC.all)
        P.I("sp", "dma_start", out=dv(dfb_d[g][:, 2], "dfb%d" % g), in_=zsC.all)

    def exchange(l):
        P.I("dve", "tensor_copy", out=packS[:, 0:256], in_=V(XC.ap.rearrange("p c f -> p (c f)"), XC.all.keys))
        P.I("dve", "tensor_copy", out=packS[:, 256:384], in_=V(SD.ap.rearrange("p c f -> p (c f)"), SD.all.keys))
        P.I("dve", "tensor_copy", out=packS[:, 384:386], in_=DcD[:, :, 32])
        if local_only and l == layers[-1]:
            P.I("sp", "dma_start", out=dv(pack_o, "pack"), in_=packS.all)
            return False
        if use_cc:
            P.I("sp", "dma_start", out=dv(packd, "packd"), in_=packS.all)
            P.custom("pool", lambda e: e.collective_compute("AllGather", op=ALU.bypass, replica_groups=[[0, 1, 2, 3], [4, 5, 6, 7]],
                                                            ins=[packd.opt()], outs=[gathd.opt()]),
                     reads=DK("packd"), writes=DK("gathd"), dma=True, cc=True)
            P.I("sp", "dma_start", out=gat.all, in_=dv(gathd.rearrange("(r p) f -> p r f", p=128), "gathd"))
        else:
            P.I("sp", "dma_start", out=gat.all, in_=dv(gath_d.rearrange("r p f -> p r f")))
        P.I("dve", "memset", ap=accS.all, constant=0.0)
        for j in range(1, 4):
            gC = V(gat.ap[:, j - 1, 0:256].rearrange("p (c f) -> p c f", f=128), gat[:, j - 1, 0:256].keys)
            gD = V(gat.ap[:, j - 1, 256:384].rearrange("p (c f) -> p c f", f=64), gat[:, j - 1, 256:384].keys)
            if j == 1:
                P.I("dve", "tensor_copy", out=curC.all, in_=gC[:, :, 0:64])
                P.I("dve", "tensor_copy", out=curD.all, in_=gD)
            else:
                bt_ = rbank()
                for hp in range(2):
                    P.I("pe", "transpose", out=bt_[0:64, hp * 128:(hp + 1) * 128], in_=gC[:, hp, 64:128], identity=ident)
                P.I("act", "activation", out=MTf.all, in_=V(bt_.ap[0:64, 0:256].rearrange("p (c f) -> p c f", f=128), bt_[:, 0:256].keys), func=AF.Copy)
                bs_ = rbank()
                for hp in range(2):
                    P.I("pe", "matmul", out=bs_[0:64, hp * 64:(hp + 1) * 64], lhsT=ident[:, 64:128], rhs=curC[:, hp, :], start=True, stop=True)
                P.I("act", "activation", out=curF.all, in_=V(bs_.ap[0:64, 0:128].rearrange("p (c f) -> p c f", f=64), bs_[:, 0:128].keys), func=AF.Copy)
                bm = rbank()
                for h in range(4):
                    hp = h // 2
                    rhs_ = curC[0:64, hp, :] if h % 2 == 0 else curF[:, hp, :]
                    P.I("pe", "matmul", out=bm[rows(h), hp * 64:(hp + 1) * 64], lhsT=MTf[:, hp, (h % 2) * 64:(h % 2) * 64 + 64], rhs=rhs_, start=True, stop=True)
                P.I("dve", "tensor_tensor", out=curC.all, in0=gC[:, :, 0:64], in1=V(bm.ap[:, 0:128].rearrange("p (c f) -> p c f", f=64), bm[:, 0:128].keys), op=ALU.add)
                for hp in range(2):
                    P.I("dve", "scalar_tensor_tensor", out=curD[:, hp, :], in0=curD[:, hp, :], scalar=gat[:, j - 1, 384 + hp:385 + hp],
                        in1=gD[:, hp, :], op0=ALU.mult, op1=ALU.add)
            P.I("dve", "scalar_tensor_tensor", out=accS[:, 0], in0=curC.all, scalar=sel[:, j:j + 1], in1=accS[:, 0], op0=ALU.mult, op1=ALU.add)
            P.I("dve", "scalar_tensor_tensor", out=accS[:, 1], in0=curD.all, scalar=sel[:, j:j + 1], in1=accS[:, 1], op0=ALU.mult, op1=ALU.add)
        P.I("act", "activation", out=SinB.all, in_=accS.all, func=AF.Copy)
        return True

    def post_group(l, g):
        P.I("sp", "dma_start", out=ost.all, in_=dv(dfo_d[g], "dfo%d" % g))
        P.I("sp", "dma_start", out=dbf.all, in_=dv(dfb_d[g], "dfb%d" % g))
        for m in range(2):
            if ("C" in skip and m == 0) or ("D" in skip and m == 1):
                P.I("dve", "memset", ap=mixCD[:, m * 2:m * 2 + 2, :], constant=0.0)
                continue
            for hp in range(2):
                for hh in range(2):
                    h = hp * 2 + hh
                    bk = rbank()
                    P.I("pe", "matmul", out=bk[rows(h), :], lhsT=SinB[rows(h), m, hp, :], rhs=dbf[rows(h), m, hp, :], start=True, stop=True)
                    P.I("dve", "tensor_tensor", out=ofQ[rows(h), :], in0=ost[rows(h), m, hp, :], in1=bk[rows(h), :], op=ALU.add)
                P.I("act", "activation", out=sqQ.all, in_=ofQ.all, func=AF.Square)
                bq = rbank()
                P.I("pe", "matmul", out=bq.all, lhsT=bonesb.all, rhs=sqQ.all, start=True, stop=True)
                P.I("act", "activation", out=rtQ.all, in_=bq.all, func=AF.Sqrt, bias=EPS, scale=1.0 / 64)
                P.I("dve", "reciprocal", out=rtQ.all, in_=rtQ.all)
                P.I("dve", "scalar_tensor_tensor", out=t1Q.all, in0=ofQ.all, scalar=pc(32 + m), in1=rtQ.all, op0=ALU.mult, op1=ALU.mult)
                P.I("dve", "tensor_tensor", out=mixCD[:, m * 2 + hp, :], in0=t1Q.all, in1=dbf[:, 2 + m, hp, :], op=ALU.mult)
        wout_part(g, mixCD, 4)

    nw_idx = {0: (0, 1), 1: (2, 3)}
    for li, l in enumerate(layers):
        load_nw(nw_idx[l][0])
        layer_setup(l)
        halo_prep(l, dv(xh_d) if (li == 0 or not use_cc) else None)
        for g in range(NG):
            norm_group(g, None)
            if "A" in skip:
                P.I("dve", "memset", ap=mixAB[:, 0:2, :], constant=0.0)
            else:
                mixer_A(l, g, g == 0)
            w2 = mixer_B(l, g, g == 0)
            wout_part(g, mixAB, 0)
            if "C" not in skip:
                mixer_C(l, g, g == 0, w2)
            if "D" not in skip:
                mixer_D(l, g, g == 0)
        if "X" not in skip:
            cont = exchange(l)
            if not cont:
                break
        load_nw(nw_idx[l][1])
        for g in range(NG):
            if "X" not in skip:
                post_group(l, g)
            norm_group(g, None)
            ffn_group(l, g, SF)
        if use_cc and li + 1 < len(layers):
            P.I("sp", "dma_start", out=dv(xsend, "xsend"), in_=xs[124:128, 15, :])
            P.custom("pool", lambda e: e.collective_compute("AllGather", op=ALU.bypass, replica_groups=[[0, 1, 2, 3], [4, 5, 6, 7]],
                                                            ins=[xsend.opt()], outs=[xgath.opt()]),
                     reads=DK("xsend"), writes=DK("xgath"), dma=True, cc=True)
    if local_only:
        allk = DK("pack")
        for g in range(NG):
            allk = allk + DK("dfo%d" % g) + DK("dfb%d" % g)
        P.ops.append(Op("sp", ("nop", None), allk, (), False))
        P.finalize()
        return nc, P
    if final_norm:
        load_nw(4)
        for Tt in range(16):
            P.I("act", "activation", out=junk.all, in_=xs[:, Tt, :], func=AF.Square, accum_out=ss[:, Tt:Tt + 1])
            P.I("act", "activation", out=rstd[:, Tt:Tt + 1], in_=ss[:, Tt:Tt + 1], func=AF.Sqrt, bias=EPS, scale=1.0 / D_MODEL)
            P.I("dve", "reciprocal", out=rstd[:, Tt:Tt + 1], in_=rstd[:, Tt:Tt + 1])
            P.I("dve", "scalar_tensor_tensor", out=xs[:, Tt, :], in0=xs[:, Tt, :], scalar=rstd[:, Tt:Tt + 1], in1=nwb.all,
                op0=ALU.mult, op1=ALU.mult)
    for Tt in range(16):
        P.I("sp", "dma_start", out=dv(y_o[Tt * 128:(Tt + 1) * 128, :], "y"), in_=xs[:, Tt, :])
    P.ops.append(Op("sp", ("nop", None), DK("y"), (), False))
    P.finalize()
    return nc, P


def host_consts(core):
    seg = core % 4
    c = np.zeros((128, 11, 128), np.float32)
    i = np.arange(128)
    c[:, 0, :] = np.eye(128)
    c[:, 1, :] = (i[None, :] <= i[:, None])
    same = (i[:, None] // 64) == (i[None, :] // 64)
    c[:, 2, :] = same & (i[None, :] >= i[:, None])
    c[:, 3, :] = same & (i[None, :] > i[:, None])
    c[:, 4, :] = same & (i[None, :] < i[:, None])
    c[:, 5, :] = 1.0
    c[:, 6, :] = same
    c[:, 7, 0:64] = (i[:, None] % 64) == np.arange(64)[None, :]
    c[:, 8, seg] = 1.0
    c[:, 9, :] = -1.0 * (same & (i[None, :] > i[:, None]))
    c[:, 10, :] = -1.0 * (same & (i[None, :] < i[:, None]))
    for t in range(4):
        if seg > 0:
            c[(seg - 1) * 4 + t, 8, 4 + t] = 1.0
    ehp = np.zeros((4, 2, 128), np.float32)
    for h in range(4):
        ehp[h, h // 2, (h % 2) * 64:(h % 2) * 64 + 64] = 1.0
    rmask = np.ones((128, 512), np.float32)
    rmask[:, ::64] = 0.0
    return c, ehp, rmask


def host_params(inp):
    pcol = np.zeros((2, 128, 40), np.float32)
    p = np.arange(128)
    for l in range(2):
        scw = inp["sc_conv_w"][l]
        for c in range(2):
            for k in range(3):
                pcol[l, :, c * 3 + k] = scw[k, c * 128 + p]
        dnw = inp["dn_conv_w"][l]
        for c in range(6):
            for k in range(4):
                pcol[l, :, 6 + c * 4 + k] = dnw[k, c * 128 + p]
        for c in range(2):
            pcol[l, :, 30 + c] = inp["gla_gate_bias"][l][c * 128 + p]
        pcol[l, :, 32] = inp["dn_norm_w"][l][p % 64]
        pcol[l, :, 33] = inp["gla_norm_w"][l][p % 64]
    p4 = np.stack([inp["dn_a_log"], inp["dn_dt_bias"]], axis=-1).astype(np.float32)
    lnwb = np.stack([inp["sgu_ln_w"], inp["sgu_ln_b"]], axis=1).astype(np.float32)
    nrm = np.stack([inp["norm1_w"][0], inp["norm2_w"][0], inp["norm1_w"][1], inp["norm2_w"][1],
                    inp["final_norm_w"]], axis=0).astype(np.float32)
    return pcol, p4, lnwb, nrm


def _in_maps(inp, consts, params, xcores, xh, gath):
    pcol, p4, lnwb, nrm = params
    maps = []
    for c in range(NCORES):
        cst, ehp, rmask = consts[c]
        maps.append({
            "x": xcores[c], "xh": xh[c], "w_in": inp["w_in"], "w_out": inp["w_out"],
            "w_gate_up": inp["w_gate_up"], "w_down": inp["w_down"], "nrm": nrm, "pcol": pcol, "p4": p4,
            "lnwb": lnwb, "wsp": inp["sgu_w_spatial"], "bsp": inp["sgu_b_spatial"], "wg2": inp["gla_w_gate2"],
            "cst": cst, "ehp": ehp, "rmask": rmask, "gath": gath[c],
        })
    return maps


def _halo(xcores):
    xh = []
    for c in range(NCORES):
        if c % 4 == 0:
            xh.append(np.zeros((4, D_MODEL), np.float32))
        else:
            xh.append(np.ascontiguousarray(xcores[c - 1][NT - 4:NT]))
    return xh


def kernel(**inputs):
    inp = {k: np.ascontiguousarray(np.asarray(v, dtype=np.float32)) for k, v in inputs.items()}
    x = inp["x"].reshape(2 * SEQ, D_MODEL)
    params = host_params(inp)
    consts = [host_consts(c) for c in range(NCORES)]
    xcores = [np.ascontiguousarray(x[c * NT:(c + 1) * NT]) for c in range(NCORES)]
    zg = [np.zeros((4, 128, 392), np.float32) for _ in range(NCORES)]
    xh = _halo(xcores)
    nc, _ = build([0, 1], False, True, use_cc=True)
    res = run_bass_kernel_spmd(nc, _in_maps(inp, consts, params, xcores, xh, zg), core_ids=list(range(NCORES)))
    out = np.concatenate([np.asarray(r["y"], np.float32) for r in res.results], axis=0)
    return out.reshape(2, SEQ, D_MODEL).astype(np.float32)
```

```python
import numpy as np
import concourse.bass as bass
import concourse.mybir as mybir
from concourse.bass_utils import run_bass_kernel_spmd

F32 = mybir.dt.float32
BF16 = mybir.dt.bfloat16
AF = mybir.ActivationFunctionType
ALU = mybir.AluOpType

D_MODEL = 1024
SEQ = 8192
NCORES = 8
NT = 2048
G = 512
NG = NT // G
D_FF = 2816
NJ = D_FF // 128
IN_COLS = 3352
EPS = 1e-6
SB_BASE = 16512
SB_END = 229312
GRAN = 256

C_AU, C_AV = 0, 256
C_BB, C_BC, C_BH = 512, 768, 1024
C_CQ, C_CK, C_CV, C_CA, C_CB, C_CZ = 1280, 1536, 1792, 2048, 2052, 2056
C_DQ, C_DK, C_DV, C_DG, C_DZ = 2312, 2568, 2824, 3080, 3096


DBG = {}
ATTACH_WAITS = True
ATTACH_ENGS = ("act", "dve", "pe")


class V:
    __slots__ = ("ap", "keys")

    def __init__(self, ap, keys):
        self.ap = ap
        self.keys = keys

    def b(self, shape):
        return V(self.ap.broadcast_to(list(shape)), self.keys)

    def __getitem__(self, idx):
        return V(self.ap[idx], self.keys)


class T:
    def __init__(self, space, name, ap, off, shape, esz):
        self.space, self.name, self.ap, self.off, self.shape, self.esz = space, name, ap, off, shape, esz
        st = [1] * len(shape)
        for i in range(len(shape) - 2, 0, -1):
            st[i] = st[i + 1] * shape[i + 1]
        self.st = st

    def __getitem__(self, idx):
        if not isinstance(idx, tuple):
            idx = (idx,)
        lo = 0
        hi = 0
        full = list(idx) + [slice(None)] * (len(self.shape) - len(idx))
        for d in range(1, len(self.shape)):
            ix = full[d]
            if isinstance(ix, slice):
                a = 0 if ix.start is None else ix.start
                b_ = self.shape[d] if ix.stop is None else ix.stop
            else:
                a, b_ = ix, ix + 1
            lo += a * self.st[d]
            hi += (b_ - 1) * self.st[d]
        b0 = self.off + lo * self.esz
        b1 = self.off + (hi + 1) * self.esz
        if self.space == "ps":
            keys = (("ps", self.off // 2048),)
        else:
            keys = tuple((self.space, g) for g in range(b0 // GRAN, (b1 - 1) // GRAN + 1))
        return V(self.ap[idx], keys)

    @property
    def all(self):
        return self[(slice(None),)]


class Op:
    __slots__ = ("eng", "fn", "reads", "writes", "dma", "eidx", "deps", "sig", "count", "sem", "waits", "cc")

    def __init__(self, eng, fn, reads, writes, dma):
        self.eng, self.fn, self.reads, self.writes, self.dma = eng, fn, reads, writes, dma
        self.deps = []
        self.sig = False
        self.count = 0
        self.sem = None
        self.waits = []
        self.cc = False


ENG_ATTR = {"pe": "tensor", "act": "scalar", "dve": "vector", "pool": "gpsimd", "sp": "sync"}
OUT_NAMES = ("out", "accum_out", "ap")


class Prog:
    def __init__(self, nc):
        self.nc = nc
        self.ops = []
        self.sb_off = SB_BASE
        self.ps_banks = 0
        self.marks = []

    def sb(self, name, shape, dt, at=None):
        esz = 2 if dt == BF16 else 4
        n = 1
        for s in shape[1:]:
            n *= s
        nbytes = ((n * esz + 63) // 64) * 64
        if at is None:
            off = self.sb_off
            self.sb_off += nbytes
        else:
            off = at
        assert off + nbytes <= SB_END, ("SBUF overflow", name, off, nbytes)
        h = self.nc.alloc_sbuf_tensor_at(name, list(shape), dt, offset=off)
        return T("sb", name, h.ap(), off, list(shape), esz)

    def ps(self, name, shape, dt):
        h = self.nc.alloc_psum_tensor(name, list(shape), dt)
        esz = 2 if dt == BF16 else 4
        off = self.ps_banks * 2048
        self.ps_banks += 1
        return T("ps", name, h.ap(), off, list(shape), esz)

    def I(self, eng, method, **kw):
        reads, writes, args = [], [], {}
        for k, v in kw.items():
            if isinstance(v, V):
                args[k] = v.ap
                if k in OUT_NAMES:
                    writes.extend(v.keys)
                else:
                    reads.extend(v.keys)
                    writes.extend(kk for kk in v.keys if kk[0] == "ps")
            else:
                args[k] = v
        extra_r = args.pop("_r", None)
        dma = method == "dma_start"
        fn = (method, args)
        op = Op(eng, fn, tuple(reads), tuple(writes), dma)
        self.ops.append(op)
        return op

    def custom(self, eng, fn, reads=(), writes=(), dma=False, cc=False):
        op = Op(eng, ("custom", fn), tuple(reads), tuple(writes), dma)
        op.cc = cc
        self.ops.append(op)
        return op

    def finalize(self, nsem_dma=10):
        nc = self.nc
        last_w = {}
        readers = {}
        per_eng = {e: [] for e in ENG_ATTR}
        for i, op in enumerate(self.ops):
            op.eidx = len(per_eng[op.eng])
            per_eng[op.eng].append(op)
            deps = set()
            for k in op.reads:
                w = last_w.get(k)
                if w is not None:
                    deps.add(w)
            for k in op.writes:
                w = last_w.get(k)
                if w is not None:
                    deps.add(w)
                for r in readers.get(k, ()):
                    deps.add(r)
            deps.discard(op)
            op.deps = deps
            for k in op.reads:
                readers.setdefault(k, []).append(op)
            for k in op.writes:
                last_w[k] = op
                readers[k] = []
        for op in self.ops:
            need = []
            for d in op.deps:
                if d.dma:
                    need.append(d)
                    continue
                if d.eng == op.eng and not op.dma:
                    if op.eng == "pe":
                        continue
                    if op.eng in ("act", "dve") and op.eidx - d.eidx > 3:
                        continue
                need.append(d)
                d.sig = True
            op.deps = need
        with_sems = {}
        eng_sem = {}
        dma_sems = {}
        names = []
        for e in ENG_ATTR:
            eng_sem[e] = nc.alloc_semaphore("s_" + e)
            dma_sems[e] = [nc.alloc_semaphore("d_%s_%d" % (e, i)) for i in range(nsem_dma)] if e in ("sp", "pool", "act") else []
        for e, lst in per_eng.items():
            cnt = 0
            ndma = 0
            for op in lst:
                if op.cc:
                    op.sem = nc.alloc_semaphore("cc_%d" % op.eidx)
                    op.count = 1
                elif op.dma:
                    op.sem = dma_sems[e][ndma % nsem_dma]
                    prev = 16 * (ndma // nsem_dma)
                    op.count = prev + 16
                    if prev > 0:
                        op.waits.append((op.sem, prev))
                    ndma += 1
                elif op.sig:
                    cnt += 1
                    op.count = cnt
                    op.sem = eng_sem[e]
        for e, lst in per_eng.items():
            waited = {}
            for op in lst:
                ws = {}
                for (s, c) in op.waits:
                    ws[s] = max(ws.get(s, 0), c)
                for d in op.deps:
                    ws[d.sem] = max(ws.get(d.sem, 0), d.count)
                out = []
                for s, c in ws.items():
                    if waited.get(s, 0) >= c:
                        continue
                    waited[s] = c
                    out.append((s, c))
                op.waits = out
        self.nwaits = sum(len(op.waits) for op in self.ops)
        with nc.Block() as block:
            for e, lst in per_eng.items():
                if not lst:
                    continue

                def body(engobj, lst=lst):
                    for op in lst:
                        kind, payload = op.fn
                        attach = None
                        waits = op.waits
                        if ATTACH_WAITS and waits and op.eng in ATTACH_ENGS and kind not in ("custom", "nop") and not op.dma:
                            attach = waits[-1]
                            waits = waits[:-1]
                        for (s, c) in waits:
                            engobj.wait_ge(s, c)
                        if kind == "custom":
                            ins = payload(engobj)
                        elif kind == "nop":
                            ins = None
                        else:
                            ins = getattr(engobj, kind)(**payload)
                        if attach is not None:
                            ins._wait_ge(attach[0], attach[1])
                        if ins is not None:
                            if op.cc:
                                ins.then_inc(op.sem, 1)
                            elif op.dma:
                                ins.then_inc(op.sem, 16)
                            elif op.sig:
                                ins.then_inc(op.sem, 1)

                getattr(block, ENG_ATTR[e])(body)


def build(layers, local_only, final_norm, use_cc=False, dbg=None, skip=()):
    nc = bass.Bass("TRN2", target_bir_lowering=False, num_devices=NCORES) if use_cc else bass.Bass("TRN2", target_bir_lowering=False)
    P = Prog(nc)

    def din(name, shape):
        return nc.dram_tensor(name, list(shape), F32, kind="ExternalInput").ap()

    x_d = din("x", [NT, D_MODEL])
    xh_d = din("xh", [4, D_MODEL])
    w_in_d = din("w_in", [2, D_MODEL, IN_COLS])
    w_out_d = din("w_out", [2, D_MODEL, D_MODEL])
    w_gu_d = din("w_gate_up", [2, D_MODEL, 2 * D_FF])
    w_dn_d = din("w_down", [2, D_FF, D_MODEL])
    nrm_d = din("nrm", [5, D_MODEL])
    pcol_d = din("pcol", [2, 128, 40])
    p4_d = din("p4", [2, 4, 2])
    lnwb_d = din("lnwb", [2, 2, 256])
    wsp_d = din("wsp", [2, 4, 128, 128])
    bsp_d = din("bsp", [2, 4, 128])
    wg2_d = din("wg2", [2, 16, 256])
    cst_d = din("cst", [128, 11, 128])
    ehp_d = din("ehp", [4, 2, 128])
    rmask_d = din("rmask", [128, 512])
    gath_d = din("gath", [4, 128, 392])
    if local_only:
        pack_o = nc.dram_tensor("pack", [128, 392], F32, kind="ExternalOutput").ap()
    else:
        y_o = nc.dram_tensor("y", [NT, D_MODEL], F32, kind="ExternalOutput").ap()
    packd = nc.dram_tensor("packd", [128, 392], F32, kind="Internal").ap()
    gathd = nc.dram_tensor("gathd", [512, 392], F32, kind="Internal").ap()
    xsend = nc.dram_tensor("xsend", [4, D_MODEL], F32, kind="Internal").ap()
    xgath = nc.dram_tensor("xgath", [16, D_MODEL], F32, kind="Internal").ap()
    dfo_d = nc.dram_tensor("dfo", [NG, 128, 2, 2, G], F32, kind="Internal").ap()
    dfb_d = nc.dram_tensor("dfb", [NG, 128, 4, 2, G], BF16, kind="Internal").ap()
    DK = lambda n: (("dram", n),)

    def dv(ap, name=None):
        return V(ap, DK(name) if name else ())

    xs = P.sb("xs", [128, 16, 1024], F32)
    cst = P.sb("cst", [128, 11, 128], F32)
    identb = P.sb("identb", [128, 128], BF16)
    bonesb = P.sb("bonesb", [128, 128], BF16)
    ehp = P.sb("ehp", [4, 2, 128], F32)
    rmask = P.sb("rmask", [128, 512], F32)
    nwb = P.sb("nwb", [128, 1024], F32)
    pcol = P.sb("pcol", [128, 40], F32)
    p4 = P.sb("p4", [4, 4], F32)
    ss = P.sb("ss", [128, 16], F32)
    rstd = P.sb("rstd", [128, 16], F32)
    hT = P.sb("hT", [128, 8, G], BF16)
    xn = [P.sb("xn%d" % i, [128, 1024], BF16) for i in range(2)]
    junk = P.sb("junk", [128, 1024], BF16)
    wo = P.sb("wo", [128, 8, 1024], BF16)
    SinB = P.sb("SinB", [128, 2, 2, 64], BF16)
    SCR = P.sb_off
    ident = cst[:, 0, :]
    maskL = cst[:, 1, :]
    BDUi = cst[:, 2, :]
    BDUs = cst[:, 3, :]
    BDLs = cst[:, 4, :]
    ones_f = cst[:, 5, :]
    eye2 = cst[:, 7, 0:64]
    sel = cst[:, 8, 0:4]
    nBDUs = cst[:, 9, :]
    nBDLs = cst[:, 10, :]

    pb = [P.ps("pb%d" % i, [128, 512], F32) for i in range(7)]
    pT = P.ps("pT", [128, 8, 128], BF16)
    rot = {"i": 0}

    def rbank():
        b = pb[2 + rot["i"] % 4]
        rot["i"] += 1
        return b

    wide = {"i": 0}

    def wbank():
        b = pb[wide["i"] % 6]
        wide["i"] += 1
        return b

    mmi = {"i": 0}

    def mbank():
        b = pb[mmi["i"] % 2]
        mmi["i"] += 1
        return b

    P.I("sp", "dma_start", out=cst.all, in_=dv(cst_d))
    P.I("sp", "dma_start", out=ehp.all, in_=dv(ehp_d))
    P.I("sp", "dma_start", out=rmask.all, in_=dv(rmask_d))
    P.I("dve", "tensor_copy", out=identb.all, in_=ident)
    P.I("dve", "tensor_copy", out=bonesb.all, in_=cst[:, 6, :])
    for t in range(16):
        P.I("sp", "dma_start", out=xs[:, t, :], in_=dv(x_d[t * 128:(t + 1) * 128, :]))

    def norm_group(g, nidx_loaded):
        for tt in range(4):
            Tt = g * 4 + tt
            P.I("act", "activation", out=junk.all, in_=xs[:, Tt, :], func=AF.Square, accum_out=ss[:, Tt:Tt + 1])
            P.I("act", "activation", out=rstd[:, Tt:Tt + 1], in_=ss[:, Tt:Tt + 1], func=AF.Sqrt, bias=EPS, scale=1.0 / D_MODEL)
            P.I("dve", "reciprocal", out=rstd[:, Tt:Tt + 1], in_=rstd[:, Tt:Tt + 1])
            xb = xn[tt % 2]
            P.I("dve", "scalar_tensor_tensor", out=xb.all, in0=xs[:, Tt, :], scalar=rstd[:, Tt:Tt + 1], in1=nwb.all,
                op0=ALU.mult, op1=ALU.mult)
            for fc in range(8):
                P.I("pe", "transpose", out=pT[:, fc, :], in_=xb[:, fc * 128:(fc + 1) * 128], identity=identb.all)
            P.I("act", "activation", out=hT[:, :, tt * 128:(tt + 1) * 128], in_=pT.all, func=AF.Copy)

    def load_nw(idx):
        P.I("sp", "dma_start", out=nwb.all, in_=dv(nrm_d[idx:idx + 1, :].partition_broadcast(128)[:, 0, :]))

    def ffn_group(l, g, S):
        gT, wg, wd, sg = S["gT"], S["wg"], S["wd"], S["sg"]
        nblk = (NJ + 1) // 2
        for jb in range(nblk):
            j0 = jb * 2
            nj = min(2, NJ - j0)
            w = wg[jb % 3]
            P.I("pool", "dma_start", out=w[:, :, 0, 0:nj * 128],
                in_=dv(w_gu_d[l][:, j0 * 128:(j0 + nj) * 128].rearrange("(kc p) n -> p kc n", p=128)))
            P.I("pool", "dma_start", out=w[:, :, 1, 0:nj * 128],
                in_=dv(w_gu_d[l][:, D_FF + j0 * 128:D_FF + (j0 + nj) * 128].rearrange("(kc p) n -> p kc n", p=128)))
            for jj in range(nj):
                j = j0 + jj
                pa, pu = wbank(), wbank()
                for kc in range(8):
                    P.I("pe", "matmul", out=pa.all, lhsT=w[:, kc, 0, jj * 128:(jj + 1) * 128], rhs=hT[:, kc, :],
                        start=(kc == 0), stop=(kc == 7))
                for kc in range(8):
                    P.I("pe", "matmul", out=pu.all, lhsT=w[:, kc, 1, jj * 128:(jj + 1) * 128], rhs=hT[:, kc, :],
                        start=(kc == 0), stop=(kc == 7))
                s_ = sg[j % 2]
                P.I("act", "activation", out=s_.all, in_=pa.all, func=AF.Silu)
                P.I("dve", "tensor_tensor", out=gT[:, j, :], in0=s_.all, in1=pu.all, op=ALU.mult)
        for half in range(2):
            w = wd[half]
            P.I("pool", "dma_start", out=w.all,
                in_=dv(w_dn_d[l][:, half * 512:(half + 1) * 512].rearrange("(j p) n -> p j n", p=128)))
            for tt in range(4):
                Tt = g * 4 + tt
                pa = wbank()
                for j in range(NJ):
                    P.I("pe", "matmul", out=pa.all, lhsT=gT[:, j, tt * 128:(tt + 1) * 128], rhs=w[:, j, :],
                        start=(j == 0), stop=(j == NJ - 1))
                P.I("dve", "tensor_tensor", out=xs[:, Tt, half * 512:(half + 1) * 512],
                    in0=xs[:, Tt, half * 512:(half + 1) * 512], in1=pa.all, op=ALU.add)

    def alloc_ffn_at(_unused=None):
        o = {"i": SCR}

        def a(name, shape, dt):
            t = P.sb(name, shape, dt, at=o["i"])
            esz = 2 if dt == BF16 else 4
            n = 1
            for s in shape[1:]:
                n *= s
            o["i"] += ((n * esz + 63) // 64) * 64
            return t
        S = {}
        S["gT"] = a("gT", [128, NJ, G], BF16)
        S["wg"] = [a("wg%d" % i, [128, 8, 2, 256], BF16) for i in range(3)]
        S["wd"] = [a("wd%d" % i, [128, NJ, 512], BF16) for i in range(2)]
        S["sg"] = [a("sg%d" % i, [128, G], F32) for i in range(2)]
        return S


    class Arena:
        def __init__(self, base):
            self.o = base

        def a(self, name, shape, dt):
            t = P.sb(name, shape, dt, at=self.o)
            esz = 2 if dt == BF16 else 4
            n = 1
            for s_ in shape[1:]:
                n *= s_
            self.o += ((n * esz + 63) // 64) * 64
            return t

    AR = Arena(SCR)
    wi = [AR.a("wi%d" % i, [128, 8, 528], BF16) for i in range(2)]
    mixAB = AR.a("mixAB", [128, 4, G], BF16)
    hTh = AR.a("hTh", [128, 8, 4], BF16)
    WsT = AR.a("WsT", [128, 4, 128], BF16)
    bsb = AR.a("bsb", [128, 2, 128], F32)
    lnw = AR.a("lnw", [128, 2, 256], F32)
    wg2b = AR.a("wg2b", [16, 256], BF16)
    haloB = AR.a("haloB", [128, 2, 4], F32)
    haloC = AR.a("haloC", [128, 6, 4], F32)
    XC = AR.a("XC", [128, 2, 128], F32)
    SD = AR.a("SD", [128, 2, 64], F32)
    DcD = AR.a("DcD", [128, 2, 33], F32)
    packS = AR.a("packS", [128, 392], F32)
    R1 = AR.o
    S1 = Arena(R1)
    xht = S1.a("xht", [4, 1024], F32)
    xhn = S1.a("xhn", [4, 1024], BF16)
    gxh = S1.a("gxh", [16, 1024], F32)
    wsp = S1.a("wsp", [128, 4, 128], F32)
    A1 = Arena(R1)
    uA = A1.a("uA", [128, 2, G], F32)
    vgA = [A1.a("vgA%d" % i, [128, 256], F32) for i in range(2)]
    vlnA = [A1.a("vlnA%d" % i, [128, 256], BF16) for i in range(2)]
    stA = A1.a("stA", [128, 8], F32)
    tmpA = A1.a("tmpA", [128, 2, 128], F32)
    xgA = A1.a("xgA", [128, G], F32)
    tgA = A1.a("tgA", [128, G], F32)
    B1 = Arena(R1)
    gbB = B1.a("gbB", [128, 2, G], F32)
    gcB = B1.a("gcB", [128, G], F32)
    tB = B1.a("tB", [128, 2, G + 2], F32)
    accB = B1.a("accB", [128, G], F32)
    D1 = Arena(R1)
    qfD = D1.a("qfD", [128, 2, G], F32)
    kfD = D1.a("kfD", [128, 2, G], F32)
    vtD = D1.a("vtD", [128, 4, 256], BF16)
    glrD = D1.a("glrD", [16, G], BF16)
    zsD = D1.a("zsD", [128, 2, G], BF16)
    efD = D1.a("efD", [128, G], F32)
    csD = D1.a("csD", [128, 2, G], F32)
    d1D = D1.a("d1D", [128, G], F32)
    aeD = D1.a("aeD", [128, G], F32)
    a3D = D1.a("a3D", [128, 2, G], F32)
    qaD = D1.a("qaD", [128, 2, G], BF16)
    kaD = D1.a("kaD", [128, 2, G], BF16)
    qgD = D1.a("qgD", [128, 2, G], BF16)
    klD = D1.a("klD", [128, 2, G], BF16)
    qgDD = D1.a("qgDD", [128, 2, G], BF16)
    kltD = D1.a("kltD", [128, 4, 256], BF16)
    atD = [D1.a("atD%d" % i, [128, 4, 128], BF16) for i in range(2)]
    SbD = [D1.a("SbD%d" % i, [128, 2, 2, 64], BF16) for i in range(2)]
    ostD = D1.a("ostD", [128, 2, G], F32)
    R2 = R1
    C1 = Arena(R2)
    cbC = C1.a("cbC", [128, G + 3], F32)
    accC = C1.a("accC", [128, G], F32)
    qsC = accC
    sqC = C1.a("sqC", [128, G], BF16)
    rtC = C1.a("rtC", [128, G], F32)
    qTb = C1.a("qTb", [128, 2, G], BF16)
    kTb = C1.a("kTb", [128, 2, G], BF16)
    kbTb = C1.a("kbTb", [128, 2, G], BF16)
    qgTb = C1.a("qgTb", [128, 2, G], BF16)
    vTb = C1.a("vTb", [128, 2, G], BF16)
    zsC = C1.a("zsC", [128, 2, G], BF16)
    t1C = C1.a("t1C", [4, G], F32)
    gcC = C1.a("gcC", [4, G], F32)
    egC = C1.a("egC", [4, G], F32)
    btC = C1.a("btC", [4, G], F32)
    tokq = C1.a("tokq", [128, 4, 4, 4], F32)
    tokb = C1.a("tokb", [128, 4, 4], F32)
    decb = C1.a("decb", [128, 2, 8], F32)
    vbt = [C1.a("vbt%d" % i, [128, 4, 64], BF16) for i in range(2)]
    kbgt = [C1.a("kbgt%d" % i, [128, 4, 64], BF16) for i in range(2)]
    kdt = [C1.a("kdt%d" % i, [128, 4, 64], BF16) for i in range(2)]
    bufA = [C1.a("bufA%d" % i, [128, 4, 128], F32) for i in range(2)]
    bufB = [C1.a("bufB%d" % i, [128, 4, 128], F32) for i in range(2)]
    bufC = [C1.a("bufC%d" % i, [128, 4, 128], F32) for i in range(2)]
    Nb = [[C1.a("Nb%d_%d" % (j, i), [128, 4, 128], BF16) for i in range(2)] for j in range(2)]
    NTb = [[C1.a("NTb%d_%d" % (j, i), [128, 4, 128], BF16) for i in range(2)] for j in range(2)]
    atC = [C1.a("atC%d" % i, [128, 4, 128], BF16) for i in range(2)]
    TTb = [[C1.a("TTb%d_%d" % (j, i), [128, 4, 128], BF16) for i in range(2)] for j in range(2)]
    uaug = [C1.a("uaug%d" % i, [128, 4, 128], F32) for i in range(2)]
    wTb = [C1.a("wTb%d" % i, [128, 2, 128], BF16) for i in range(2)]
    _vnb = C1.a("vnb", [128, 4, 128], BF16)
    vnb = [_vnb, _vnb]
    _XbC = C1.a("XbC", [128, 2, 2, 128], BF16)
    XbC = [_XbC, _XbC]
    ostC = C1.a("ostC", [128, 2, G], F32)
    PTC = C1.a("PTC", [128, 2, G], BF16)
    END_LOCAL = C1.o
    Q1 = Arena(R1)
    gat = Q1.a("gat", [128, 4, 392], F32)
    curC = Q1.a("curC", [128, 2, 64], F32)
    curD = Q1.a("curD", [128, 2, 64], F32)
    accS = Q1.a("accS", [128, 2, 2, 64], F32)
    MTf = Q1.a("MTf", [64, 2, 128], F32)
    curF = Q1.a("curF", [64, 2, 64], F32)
    ost = Q1.a("ost", [128, 2, 2, G], F32)
    dbf = Q1.a("dbf", [128, 4, 2, G], BF16)
    ofQ = Q1.a("ofQ", [128, G], F32)
    sqQ = Q1.a("sqQ", [128, G], BF16)
    rtQ = Q1.a("rtQ", [128, G], F32)
    t1Q = Q1.a("t1Q", [128, G], F32)
    mixCD = Q1.a("mixCD", [128, 4, G], BF16)
    print("SBUF plan: SCR=%d R1=%d R2=%d endlocal=%d endpost=%d" % (SCR, R1, R2, END_LOCAL, Q1.o))
    SF = alloc_ffn_at(max(Q1.o, 0) if False else None)

    def rows(h):
        return slice((h % 2) * 64, (h % 2) * 64 + 64)

    def pc(col):
        return pcol[:, col:col + 1]

    def layer_setup(l):
        P.I("sp", "dma_start", out=pcol.all, in_=dv(pcol_d[l]))
        P.I("sp", "dma_start", out=p4[:, 0:2], in_=dv(p4_d[l]))
        P.I("dve", "tensor_scalar", out=pcol[:, 34:36], in0=pcol[:, 30:32], scalar1=-1.0, scalar2=None, op0=ALU.mult)
        P.I("act", "activation", out=p4[:, 2:3], in_=p4[:, 0:1], func=AF.Exp)
        P.I("dve", "tensor_scalar", out=p4[:, 2:3], in0=p4[:, 2:3], scalar1=-1.0, scalar2=None, op0=ALU.mult)
        for hf in range(2):
            P.I("pool", "dma_start", out=wo[:, hf * 4:(hf + 1) * 4, :],
                in_=dv(w_out_d[l][hf * 512:(hf + 1) * 512, :].rearrange("(kc p) n -> p kc n", p=128)))
        P.I("sp", "dma_start", out=wsp.all, in_=dv(wsp_d[l].rearrange("h t s -> t h s")))
        P.I("dve", "tensor_tensor", out=wsp.all, in0=wsp.all, in1=maskL[:, None, :].b([128, 4, 128]), op=ALU.mult)
        bk = rbank()
        for h in range(4):
            P.I("pe", "transpose", out=bk[:, h * 128:(h + 1) * 128], in_=wsp[:, h, :], identity=ident)
        P.I("act", "activation", out=WsT.all, in_=bk.all, func=AF.Copy)
        for h in range(4):
            P.I("sp", "dma_start", out=bsb[rows(h), h // 2, :], in_=dv(bsp_d[l][h:h + 1, :].partition_broadcast(64)[:, 0, :]))
        for i in range(2):
            P.I("sp", "dma_start", out=lnw[:, i, :], in_=dv(lnwb_d[l][i:i + 1, :].partition_broadcast(128)[:, 0, :]))
        P.I("pool", "dma_start", out=wg2b.all, in_=dv(wg2_d[l]))
        P.I("dve", "memset", ap=XC.all, constant=0.0)
        for hp in range(2):
            P.I("dve", "tensor_copy", out=XC[:, hp, 64:128], in_=eye2)
        P.I("dve", "memset", ap=SD.all, constant=0.0)
        P.I("dve", "memset", ap=DcD.all, constant=1.0)
        for i in range(2):
            P.I("dve", "memset", ap=uaug[i].all, constant=0.0)
        P.I("dve", "memset", ap=packS.all, constant=0.0)

    def halo_prep(l, from_x_dram):
        if from_x_dram is not None:
            P.I("sp", "dma_start", out=xht.all, in_=from_x_dram)
        else:
            P.I("sp", "dma_start", out=gxh.all, in_=dv(xgath, "xgath"))
            for half in range(2):
                bk = rbank()
                P.I("pe", "matmul", out=bk[0:4, :], lhsT=cst[0:16, 8, 4:8], rhs=gxh[:, half * 512:(half + 1) * 512], start=True, stop=True)
                P.I("act", "activation", out=xht[:, half * 512:(half + 1) * 512], in_=bk[0:4, :], func=AF.Copy)
        P.I("act", "activation", out=xhn.all, in_=xht.all, func=AF.Square, accum_out=p4[:, 3:4])
        P.I("act", "activation", out=p4[:, 3:4], in_=p4[:, 3:4], func=AF.Sqrt, bias=EPS, scale=1.0 / D_MODEL)
        P.I("dve", "reciprocal", out=p4[:, 3:4], in_=p4[:, 3:4])
        P.I("dve", "scalar_tensor_tensor", out=xhn.all, in0=xht.all, scalar=p4[:, 3:4], in1=nwb[0:4, :],
            op0=ALU.mult, op1=ALU.mult)
        for fc in range(8):
            P.I("pe", "transpose", out=pT[:, fc, 0:4], in_=xhn[:, fc * 128:(fc + 1) * 128], identity=identb[0:4, 0:4])
        P.I("act", "activation", out=hTh.all, in_=pT[:, :, 0:4], func=AF.Copy)

    def ip_fm(w, c0, n, rhs_of_kc=None, N=G):
        bk = mbank()
        for kc in range(8):
            P.I("pe", "matmul", out=bk[0:n, 0:N], lhsT=w[:, kc, c0:c0 + n],
                rhs=(hT[:, kc, :] if rhs_of_kc is None else rhs_of_kc(kc)), start=(kc == 0), stop=(kc == 7))
        return bk

    def ip_tm(w, c0, n, tt):
        bk = mbank()
        for kc in range(8):
            P.I("pe", "matmul", out=bk[:, 0:n], lhsT=hT[:, kc, tt * 128:(tt + 1) * 128], rhs=w[:, kc, c0:c0 + n],
                start=(kc == 0), stop=(kc == 7))
        return bk

    wslot = {"i": 0}

    def load_wi(l, c0, n):
        w = wi[wslot["i"] % 2]
        wslot["i"] += 1
        P.I("pool", "dma_start", out=w[:, :, 0:n], in_=dv(w_in_d[l][:, c0:c0 + n].rearrange("(kc p) n -> p kc n", p=128)))
        return w

    def gelu_tanh(out_v, ps_v, xg, tg):
        P.I("act", "activation", out=xg, in_=ps_v, func=AF.Copy)
        P.I("dve", "tensor_tensor", out=tg, in0=xg, in1=xg, op=ALU.mult)
        P.I("dve", "tensor_scalar", out=tg, in0=tg, scalar1=0.044715, scalar2=1.0, op0=ALU.mult, op1=ALU.add)
        P.I("dve", "tensor_tensor", out=tg, in0=tg, in1=xg, op=ALU.mult)
        P.I("act", "activation", out=tg, in_=tg, func=AF.Sigmoid, scale=1.5957691216057308)
        P.I("dve", "tensor_tensor", out=out_v, in0=xg, in1=tg, op=ALU.mult)

    def mixer_A(l, g, first):
        w = load_wi(l, 0, 512)
        for c in range(2):
            bk = ip_fm(w, C_AU + c * 128, 128)
            gelu_tanh(uA[:, c, :], bk.all, xgA.all, tgA.all)
        tmpA2 = V(tmpA.ap.rearrange("p c t -> p (c t)"), tmpA.all.keys)
        for tt in range(4):
            bk = ip_tm(w, C_AV, 256, tt)
            vg, vl = vgA[tt % 2], vlnA[tt % 2]
            gelu_tanh(vg.all, bk[:, 0:256], xgA[:, 0:256], tgA[:, 0:256])
            P.I("dve", "tensor_reduce", out=stA[:, 0:1], in_=vg.all, axis=mybir.AxisListType.X, op=ALU.add)
            P.I("act", "activation", out=stA[:, 4:5], in_=stA[:, 0:1], func=AF.Copy, scale=-1.0 / 256)
            P.I("dve", "tensor_scalar", out=vg.all, in0=vg.all, scalar1=stA[:, 4:5], scalar2=None, op0=ALU.add)
            P.I("act", "activation", out=tmpA2, in_=vg.all, func=AF.Square, accum_out=stA[:, 1:2])
            P.I("act", "activation", out=stA[:, 2:3], in_=stA[:, 1:2], func=AF.Sqrt, bias=EPS, scale=1.0 / 256)
            P.I("dve", "reciprocal", out=stA[:, 2:3], in_=stA[:, 2:3])
            P.I("dve", "scalar_tensor_tensor", out=vg.all, in0=vg.all, scalar=stA[:, 2:3], in1=lnw[:, 0, :], op0=ALU.mult, op1=ALU.mult)
            P.I("dve", "tensor_tensor", out=vl.all, in0=vg.all, in1=lnw[:, 1, :], op=ALU.add)
            bm = rbank()
            for h in range(4):
                P.I("pe", "matmul", out=bm[rows(h), (h // 2) * 128:(h // 2) * 128 + 128], lhsT=vl[:, h * 64:(h + 1) * 64],
                    rhs=WsT[:, h, :], start=True, stop=True)
            P.I("dve", "tensor_tensor", out=tmpA.all, in0=V(bm.ap[:, 0:256].rearrange("p (c t) -> p c t", t=128), bm[:, 0:256].keys), in1=bsb.all, op=ALU.add)
            P.I("dve", "tensor_tensor", out=mixAB[:, 0:2, tt * 128:(tt + 1) * 128], in0=tmpA.all,
                in1=uA[:, :, tt * 128:(tt + 1) * 128], op=ALU.mult)

    def mixer_B(l, g, first):
        w = load_wi(l, 512, 512)
        w2 = load_wi(l, 1024, 512)
        for c in range(2):
            bk = ip_fm(w, c * 128, 128)
            P.I("act", "activation", out=gbB[:, c, :], in_=bk.all, func=AF.Copy)
        for c in range(2):
            bk = ip_fm(w, 256 + c * 128, 128)
            P.I("act", "activation", out=gcB.all, in_=bk.all, func=AF.Copy)
            if first:
                b3 = ip_fm(w, 256 + c * 128, 128, rhs_of_kc=lambda kc: hTh[:, kc, :], N=4)
                P.I("act", "activation", out=haloB[:, c, :], in_=b3[:, 0:4], func=AF.Copy)
                b4 = ip_fm(w2, c * 128, 128, rhs_of_kc=lambda kc: hTh[:, kc, :], N=4)
                P.I("dve", "tensor_tensor", out=haloB[:, c, :], in0=haloB[:, c, :], in1=b4[:, 0:4], op=ALU.mult)
            P.I("dve", "tensor_copy", out=tB[:, c, 0:2], in_=haloB[:, c, 2:4])
            bk2 = ip_fm(w2, c * 128, 128)
            P.I("dve", "tensor_tensor", out=tB[:, c, 2:G + 2], in0=gcB.all, in1=bk2.all, op=ALU.mult)
            k0 = c * 3
            P.I("dve", "tensor_scalar", out=accB.all, in0=tB[:, c, 2:G + 2], scalar1=pc(k0 + 2), scalar2=None, op0=ALU.mult)
            P.I("dve", "scalar_tensor_tensor", out=accB.all, in0=tB[:, c, 1:G + 1], scalar=pc(k0 + 1), in1=accB.all, op0=ALU.mult, op1=ALU.add)
            P.I("dve", "scalar_tensor_tensor", out=accB.all, in0=tB[:, c, 0:G], scalar=pc(k0 + 0), in1=accB.all, op0=ALU.mult, op1=ALU.add)
            P.I("dve", "tensor_tensor", out=mixAB[:, 2 + c, :], in0=accB.all, in1=gbB[:, c, :], op=ALU.mult)
            P.I("dve", "tensor_copy", out=haloB[:, c, 2:4], in_=tB[:, c, G:G + 2])
        return w2

    def wout_part(g, mixbuf, kc0):
        for tt in range(4):
            Tt = g * 4 + tt
            for half in range(2):
                bk = wbank()
                for k in range(4):
                    P.I("pe", "matmul", out=bk.all, lhsT=mixbuf[:, k, tt * 128:(tt + 1) * 128],
                        rhs=wo[:, kc0 + k, half * 512:(half + 1) * 512], start=(k == 0), stop=(k == 3))
                P.I("dve", "tensor_tensor", out=xs[:, Tt, half * 512:(half + 1) * 512],
                    in0=xs[:, Tt, half * 512:(half + 1) * 512], in1=bk.all, op=ALU.add)

    def mixer_D(l, g, first):
        w = load_wi(l, C_DQ, 512)
        for c in range(2):
            bk = ip_fm(w, c * 128, 128)
            P.I("act", "activation", out=qfD[:, c, :], in_=bk.all, func=AF.Copy)
        for c in range(2):
            bk = ip_fm(w, 256 + c * 128, 128)
            P.I("act", "activation", out=kfD[:, c, :], in_=bk.all, func=AF.Copy)
        w = load_wi(l, C_DV, 528)
        for tt in range(4):
            bk = ip_tm(w, 0, 256, tt)
            P.I("act", "activation", out=vtD[:, tt, :], in_=bk[:, 0:256], func=AF.Copy)
        bk = ip_fm(w, 256, 16)
        P.I("act", "activation", out=glrD.all, in_=bk[0:16, :], func=AF.Copy)
        for c in range(2):
            bk = ip_fm(w, 272 + c * 128, 128)
            P.I("act", "activation", out=zsD[:, c, :], in_=bk.all, func=AF.Silu)
        if DBG.get("dcut", 99) <= 1:
            return
        for c in range(2):
            bk = mbank()
            P.I("pe", "matmul", out=bk.all, lhsT=wg2b[:, c * 128:(c + 1) * 128], rhs=glrD.all, start=True, stop=True)
            P.I("act", "activation", out=efD.all, in_=bk.all, func=AF.Exp, bias=pc(34 + c), scale=-1.0)
            P.I("act", "activation", out=efD.all, in_=efD.all, func=AF.Ln, bias=1.0, scale=1.0)
            P.I("dve", "tensor_tensor_scan", out=csD[:, c, :], data0=rmask.all, data1=efD.all, initial=0.0, op0=ALU.mult, op1=ALU.add)
            cs3 = V(csD.ap[:, c, :].rearrange("p (n t) -> p n t", t=64), csD[:, c, :].keys)
            d13 = V(d1D.ap.rearrange("p (n t) -> p n t", t=64), d1D.all.keys)
            P.I("dve", "tensor_tensor", out=d13, in0=cs3, in1=cs3[:, :, 32:33].b([128, 8, 64]), op=ALU.subtract)
            P.I("act", "activation", out=aeD.all, in_=d1D.all, func=AF.Exp, scale=-1.0 / 16)
            P.I("dve", "scalar_tensor_tensor", out=qaD[:, c, :], in0=qfD[:, c, :], scalar=0.125, in1=aeD.all, op0=ALU.mult, op1=ALU.mult)
            P.I("act", "activation", out=aeD.all, in_=d1D.all, func=AF.Exp, scale=1.0 / 16)
            P.I("dve", "tensor_tensor", out=kaD[:, c, :], in0=kfD[:, c, :], in1=aeD.all, op=ALU.mult)
            P.I("act", "activation", out=a3D[:, c, :], in_=csD[:, c, :], func=AF.Exp, scale=-1.0 / 16)
            P.I("dve", "scalar_tensor_tensor", out=qgD[:, c, :], in0=qfD[:, c, :], scalar=0.125, in1=a3D[:, c, :], op0=ALU.mult, op1=ALU.mult)
            P.I("dve", "tensor_tensor", out=d13, in0=cs3, in1=cs3[:, :, 63:64].b([128, 8, 64]), op=ALU.subtract)
            P.I("act", "activation", out=aeD.all, in_=d1D.all, func=AF.Exp, scale=1.0 / 16)
            P.I("dve", "tensor_tensor", out=klD[:, c, :], in0=kfD[:, c, :], in1=aeD.all, op=ALU.mult)
        if DBG.get("dcut", 99) <= 2:
            return
        a34 = V(a3D.ap.rearrange("p c (n t) -> p c n t", t=64), a3D.all.keys)
        for n in range(8):
            ng = g * 8 + n
            P.I("dve", "tensor_tensor", out=DcD[:, :, ng + 1], in0=DcD[:, :, ng], in1=a34[:, :, n, 63], op=ALU.mult)
        for c in range(2):
            q3 = V(qgD.ap[:, c, :].rearrange("p (n t) -> p n t", t=64), qgD[:, c, :].keys)
            o3 = V(qgDD.ap[:, c, :].rearrange("p (n t) -> p n t", t=64), qgDD[:, c, :].keys)
            P.I("dve", "tensor_tensor", out=o3, in0=q3, in1=DcD[:, c, g * 8:g * 8 + 8][:, :, None].b([128, 8, 64]), op=ALU.mult)
        if DBG.get("dcut", 99) <= 3:
            return
        for tt in range(4):
            for c in range(2):
                P.I("pe", "transpose", out=pT[:, c, :], in_=klD[:, c, tt * 128:(tt + 1) * 128], identity=identb.all)
            P.I("act", "activation", out=kltD[:, tt, :], in_=pT[:, 0:2, :], func=AF.Copy)
        if DBG.get("dcut", 99) <= 4:
            return
        for tt in range(4):
            tok = slice(tt * 128, (tt + 1) * 128)
            at = atD[tt % 2]
            bkp = [rbank(), rbank()]
            for h in range(4):
                P.I("pe", "matmul", out=bkp[h % 2][:, (h // 2) * 128:(h // 2) * 128 + 128], lhsT=kaD[rows(h), h // 2, tok], rhs=qaD[rows(h), h // 2, tok],
                    start=True, stop=True)
            at4 = V(at.ap.rearrange("p (c r) s -> p c r s", r=2), at.all.keys)
            for par in range(2):
                P.I("dve", "tensor_tensor", out=at4[:, :, par, :], in0=V(bkp[par].ap[:, 0:256].rearrange("p (c s) -> p c s", s=128), bkp[par][:, 0:256].keys),
                    in1=BDUi[:, None, :].b([128, 2, 128]), op=ALU.mult)
            if DBG.get("dsub", 9) <= 1:
                continue
            bsn = [rbank(), rbank()]
            for n2 in range(2):
                rn = slice(n2 * 64, n2 * 64 + 64)
                for h in range(4):
                    P.I("pe", "matmul", out=bsn[n2][rows(h), (h // 2) * 64:(h // 2) * 64 + 64],
                        lhsT=kltD[rn, tt, h * 64:(h + 1) * 64], rhs=vtD[rn, tt, h * 64:(h + 1) * 64], start=True, stop=True)
            Sb = SbD[tt % 2]
            for n2 in range(2):
                n = tt * 2 + n2
                P.I("act", "activation", out=Sb[:, n2, :, :], in_=SD.all, func=AF.Copy)
                for hp in range(2):
                    P.I("dve", "scalar_tensor_tensor", out=SD[:, hp, :], in0=SD[:, hp, :], scalar=a34[:, hp, n, 63:64],
                        in1=bsn[n2][:, hp * 64:hp * 64 + 64], op0=ALU.mult, op1=ALU.add)
            if DBG.get("dsub", 9) <= 2:
                continue
            pop = [pb[6], rbank()]
            pox = [rbank(), rbank()]
            for h in range(4):
                hp = h // 2
                P.I("pe", "matmul", out=pop[h % 2][rows(h), hp * 128:(hp + 1) * 128], lhsT=vtD[:, tt, h * 64:(h + 1) * 64], rhs=at[:, h, :],
                    start=True, stop=True)
                for n2 in range(2):
                    P.I("pe", "matmul", out=pox[h % 2][rows(h), hp * 128 + n2 * 64:hp * 128 + n2 * 64 + 64], lhsT=Sb[rows(h), n2, hp, :],
                        rhs=qgD[rows(h), hp, tt * 128 + n2 * 64:tt * 128 + n2 * 64 + 64], start=True, stop=True)
            for par in range(2):
                pr = slice(par * 64, par * 64 + 64)
                P.I("act", "activation", out=ostD[pr, :, tok], in_=V(pop[par].ap[pr, 0:256].rearrange("p (c t) -> p c t", t=128), pop[par][:, 0:256].keys), func=AF.Copy)
                P.I("dve", "tensor_tensor", out=ostD[pr, :, tok], in0=ostD[pr, :, tok],
                    in1=V(pox[par].ap[pr, 0:256].rearrange("p (c t) -> p c t", t=128), pox[par][:, 0:256].keys), op=ALU.add)
        if DBG.get("dcut", 99) <= 5:
            return
        P.I("sp", "dma_start", out=dv(dfo_d[g][:, 1], "dfo%d" % g), in_=ostD.all)
        P.I("sp", "dma_start", out=dv(dfb_d[g][:, 1], "dfb%d" % g), in_=qgDD.all)
        P.I("sp", "dma_start", out=dv(dfb_d[g][:, 3], "dfb%d" % g), in_=zsD.all)

    def mixer_C(l, g, first, w2):
        wk = load_wi(l, C_CK, 512)
        srcs = [(w2, 256), (w2, 384), (wk, 0), (wk, 128), (wk, 256), (wk, 384)]
        for ci, (w, c0) in enumerate(srcs):
            bk = ip_fm(w, c0, 128)
            P.I("act", "activation", out=cbC[:, 3:G + 3], in_=bk.all, func=AF.Copy)
            if first:
                b3 = ip_fm(w, c0, 128, rhs_of_kc=lambda kc: hTh[:, kc, :], N=4)
                P.I("act", "activation", out=haloC[:, ci, :], in_=b3[:, 0:4], func=AF.Copy)
            P.I("dve", "tensor_copy", out=cbC[:, 0:3], in_=haloC[:, ci, 1:4])
            k0 = 6 + ci * 4
            P.I("act", "activation", out=accC.all, in_=cbC[:, 3:G + 3], func=AF.Copy, scale=pc(k0 + 3))
            for k in range(3):
                P.I("dve", "scalar_tensor_tensor", out=accC.all, in0=cbC[:, k:G + k], scalar=pc(k0 + k), in1=accC.all, op0=ALU.mult, op1=ALU.add)
            P.I("dve", "tensor_copy", out=haloC[:, ci, 1:4], in_=cbC[:, G:G + 3])
            if ci < 4:
                P.I("act", "activation", out=qsC.all, in_=accC.all, func=AF.Silu)
                P.I("act", "activation", out=sqC.all, in_=qsC.all, func=AF.Square)
                bq = rbank()
                P.I("pe", "matmul", out=bq.all, lhsT=bonesb.all, rhs=sqC.all, start=True, stop=True)
                P.I("act", "activation", out=rtC.all, in_=bq.all, func=AF.Sqrt, bias=EPS, scale=1.0)
                P.I("dve", "reciprocal", out=rtC.all, in_=rtC.all)
                if ci < 2:
                    P.I("dve", "scalar_tensor_tensor", out=qTb[:, ci, :], in0=qsC.all, scalar=0.125, in1=rtC.all, op0=ALU.mult, op1=ALU.mult)
                else:
                    P.I("dve", "tensor_tensor", out=kTb[:, ci - 2, :], in0=qsC.all, in1=rtC.all, op=ALU.mult)
            else:
                P.I("act", "activation", out=vTb[:, ci - 4, :], in_=accC.all, func=AF.Silu)
        if DBG.get("ccut", 99) <= 1:
            return
        wz = load_wi(l, C_CA, 264)
        bka = ip_fm(wz, 0, 4)
        P.I("act", "activation", out=t1C.all, in_=bka[0:4, :], func=AF.Exp, bias=p4[:, 1:2], scale=1.0)
        P.I("act", "activation", out=t1C.all, in_=t1C.all, func=AF.Ln, bias=1.0, scale=1.0)
        P.I("dve", "tensor_scalar", out=t1C.all, in0=t1C.all, scalar1=p4[:, 2:3], scalar2=None, op0=ALU.mult)
        P.I("dve", "tensor_tensor_scan", out=gcC.all, data0=rmask[0:4, :], data1=t1C.all, initial=0.0, op0=ALU.mult, op1=ALU.add)
        bkb = ip_fm(wz, 4, 4)
        P.I("act", "activation", out=btC.all, in_=bkb[0:4, :], func=AF.Exp, scale=-1.0)
        P.I("dve", "tensor_scalar", out=btC.all, in0=btC.all, scalar1=1.0, scalar2=None, op0=ALU.add)
        P.I("dve", "reciprocal", out=btC.all, in_=btC.all)
        for c in range(2):
            bk = ip_fm(wz, 8 + c * 128, 128)
            P.I("act", "activation", out=zsC[:, c, :], in_=bk.all, func=AF.Silu)
        gc3 = V(gcC.ap.rearrange("p (n t) -> p n t", t=64), gcC.all.keys)
        t13 = V(t1C.ap.rearrange("p (n t) -> p n t", t=64), t1C.all.keys)
        P.I("dve", "tensor_tensor", out=t13, in0=gc3[:, :, 63:64].b([4, 8, 64]), in1=gc3, op=ALU.subtract)
        P.I("act", "activation", out=t1C.all, in_=t1C.all, func=AF.Exp)
        P.I("act", "activation", out=egC.all, in_=gcC.all, func=AF.Exp)
        if DBG.get("ccut", 99) <= 2:
            return
        for hp in range(2):
            bb = rbank()
            P.I("pe", "matmul", out=bb.all, lhsT=ehp[:, hp, :], rhs=btC.all, start=True, stop=True)
            P.I("dve", "tensor_tensor", out=kbTb[:, hp, :], in0=kTb[:, hp, :], in1=bb.all, op=ALU.mult)
            be = rbank()
            P.I("pe", "matmul", out=be.all, lhsT=ehp[:, hp, :], rhs=egC.all, start=True, stop=True)
            P.I("dve", "tensor_tensor", out=qgTb[:, hp, :], in0=qTb[:, hp, :], in1=be.all, op=ALU.mult)
            bd = rbank()
            P.I("pe", "matmul", out=bd[:, 0:8], lhsT=ehp[:, hp, :], rhs=V(egC.ap.rearrange("p (n t) -> p n t", t=64)[:, :, 63], egC.all.keys),
                start=True, stop=True)
            P.I("act", "activation", out=decb[:, hp, :], in_=bd[:, 0:8], func=AF.Copy)
        if DBG.get("ccut", 99) <= 3:
            return
        for tt in range(4):
            tok = slice(tt * 128, (tt + 1) * 128)
            bt_ = rbank()
            for qi, src in enumerate((gcC, btC, t1C, egC)):
                P.I("pe", "transpose", out=bt_[:, qi * 4:qi * 4 + 4], in_=src[:, tok], identity=ident[0:4, 0:4])
            P.I("act", "activation", out=tokq[:, tt, :, :], in_=V(bt_.ap[:, 0:16].rearrange("p (q h) -> p q h", h=4), bt_[:, 0:16].keys), func=AF.Copy)
            P.I("dve", "tensor_tensor", out=tokb[:, tt, :], in0=tokq[:, tt, 1, :], in1=tokq[:, tt, 3, :], op=ALU.mult)
        if DBG.get("ccut", 99) <= 4:
            return
        for i in range(2):
            P.I("dve", "memset", ap=uaug[i][:, :, 64:128], constant=0.0)
        f4 = lambda t_: V(t_.ap.rearrange("p h s -> p (h s)"), t_.all.keys)
        v4 = lambda t_: V(t_.ap.rearrange("p (c r) s -> p c r s", r=2), t_.all.keys)

        def c_prep(tt, sl):
            tok = slice(tt * 128, (tt + 1) * 128)
            bA, bB, bC = bufA[sl], bufB[sl], bufC[sl]
            N_, NT_, TT_ = Nb[sl], NTb[sl], TTb[sl]
            for c in range(2):
                P.I("pe", "transpose", out=pT[:, c, :], in_=kTb[:, c, tok], identity=identb.all)
                P.I("pe", "transpose", out=pT[:, 2 + c, :], in_=vTb[:, c, tok], identity=identb.all)
            kt4 = V(pT.ap[:, 0:2, :].rearrange("p c (h d) -> p (c h) d", d=64), pT[:, 0:2, :].keys)
            vt4 = V(pT.ap[:, 2:4, :].rearrange("p c (h d) -> p (c h) d", d=64), pT[:, 2:4, :].keys)
            P.I("dve", "tensor_tensor", out=vbt[sl].all, in0=vt4, in1=tokq[:, tt, 1, :][:, :, None].b([128, 4, 64]), op=ALU.mult)
            P.I("dve", "tensor_tensor", out=kbgt[sl].all, in0=kt4, in1=tokb[:, tt, :][:, :, None].b([128, 4, 64]), op=ALU.mult)
            P.I("dve", "tensor_tensor", out=kdt[sl].all, in0=kt4, in1=tokq[:, tt, 2, :][:, :, None].b([128, 4, 64]), op=ALU.mult)
            yield
            if DBG.get("pcut", 99) <= 1:
                return
            P.I("dve", "tensor_tensor", out=bA.all, in0=tokq[:, tt, 0, :][:, :, None].b([128, 4, 128]), in1=ident[:, None, :].b([128, 4, 128]), op=ALU.mult)
            bg = rbank()
            P.I("pe", "matmul", out=bg.all, lhsT=ones_f, rhs=f4(bA), start=True, stop=True)
            bg3 = V(bg.ap.rearrange("p (h s) -> p h s", s=128), bg.all.keys)
            P.I("dve", "tensor_tensor", out=bB.all, in0=bg3, in1=tokq[:, tt, 0, :][:, :, None].b([128, 4, 128]), op=ALU.subtract)
            yield
            P.I("dve", "tensor_scalar", out=bC.all, in0=bB.all, scalar1=3.0e38, scalar2=0.0, op0=ALU.min, op1=ALU.max)
            P.I("act", "activation", out=bC.all, in_=bC.all, func=AF.Exp, scale=-1.0)
            P.I("dve", "tensor_scalar", out=bA.all, in0=bB.all, scalar1=0.0, scalar2=-3.0e38, op0=ALU.min, op1=ALU.max)
            P.I("act", "activation", out=bA.all, in_=bA.all, func=AF.Exp, scale=1.0)
            yield
            P.I("dve", "tensor_tensor", out=bC.all, in0=bC.all, in1=nBDLs[:, None, :].b([128, 4, 128]), op=ALU.mult)
            P.I("dve", "tensor_tensor", out=bB.all, in0=bA.all, in1=nBDUs[:, None, :].b([128, 4, 128]), op=ALU.mult)
            P.I("dve", "tensor_tensor", out=bA.all, in0=bA.all, in1=BDUi[:, None, :].b([128, 4, 128]), op=ALU.mult)
            if DBG.get("pcut", 99) <= 2:
                return
            for (lt, rt_, dst, msk) in ((kbTb, kTb, N_[0], bC), (kTb, kbTb, NT_[0], bB), (kTb, qTb, atC[sl], bA)):
                bkp = [rbank(), rbank()]
                for h in range(4):
                    hp = h // 2
                    P.I("pe", "matmul", out=bkp[h % 2][:, hp * 128:(hp + 1) * 128], lhsT=lt[rows(h), hp, tok], rhs=rt_[rows(h), hp, tok], start=True, stop=True)
                for par in range(2):
                    P.I("dve", "tensor_tensor", out=v4(dst)[:, :, par, :], in0=V(bkp[par].ap[:, 0:256].rearrange("p (c s) -> p c s", s=128), bkp[par][:, 0:256].keys),
                        in1=v4(msk)[:, :, par, :], op=ALU.mult)
                yield
            if DBG.get("pcut", 99) <= 3:
                return
            P.I("dve", "tensor_tensor", out=TT_[0].all, in0=NT_[0].all, in1=ident[:, None, :].b([128, 4, 128]), op=ALU.add)
            cur = 0
            for k in range(1, 6):
                bn = rbank()
                for h in range(4):
                    P.I("pe", "matmul", out=bn[:, h * 128:(h + 1) * 128], lhsT=NT_[cur][:, h, :], rhs=N_[cur][:, h, :], start=True, stop=True)
                P.I("act", "activation", out=f4(N_[1 - cur]), in_=bn.all, func=AF.Copy)
                if k < 5:
                    bnt = rbank()
                    for h in range(4):
                        P.I("pe", "matmul", out=bnt[:, h * 128:(h + 1) * 128], lhsT=N_[cur][:, h, :], rhs=NT_[cur][:, h, :], start=True, stop=True)
                    P.I("act", "activation", out=f4(NT_[1 - cur]), in_=bnt.all, func=AF.Copy)
                cur = 1 - cur
                yield
                bt2 = rbank()
                tcur = (k - 1) % 2
                for h in range(4):
                    P.I("pe", "matmul", out=bt2[:, h * 128:(h + 1) * 128], lhsT=N_[cur][:, h, :], rhs=TT_[tcur][:, h, :], start=True, stop=True)
                P.I("dve", "tensor_tensor", out=f4(TT_[1 - tcur]), in0=bt2.all, in1=f4(TT_[tcur]), op=ALU.add)
                yield
            TT = TT_[1]
            if DBG.get("pcut", 99) <= 4:
                return
            bu = rbank()
            for h in range(4):
                P.I("pe", "matmul", out=bu[:, h * 64:(h + 1) * 64], lhsT=TT[:, h, :], rhs=vbt[sl][:, h, :], start=True, stop=True)
            P.I("act", "activation", out=uaug[sl][:, :, 0:64], in_=V(bu.ap[:, 0:256].rearrange("p (h e) -> p h e", e=64), bu[:, 0:256].keys), func=AF.Copy)
            bw = rbank()
            for h in range(4):
                P.I("pe", "matmul", out=bw[rows(h), (h // 2) * 128:(h // 2) * 128 + 128], lhsT=kbgt[sl][:, h, :], rhs=TT[:, h, :], start=True, stop=True)
            P.I("act", "activation", out=wTb[sl].all, in_=V(bw.ap[:, 0:256].rearrange("p (c t) -> p c t", t=128), bw[:, 0:256].keys), func=AF.Copy)
            yield

        def c_scan_out(tt, sl):
            tok = slice(tt * 128, (tt + 1) * 128)
            Xb = XbC[0]
            vn = vnb[0]
            for n2 in range(2):
                n = tt * 2 + n2
                rn = slice(n2 * 64, n2 * 64 + 64)
                P.I("act", "activation", out=Xb[:, n2, :, :], in_=XC.all, func=AF.Copy)
                bvp = [pb[6], rbank()]
                for h in range(4):
                    P.I("pe", "matmul", out=bvp[h % 2][rn, (h // 2) * 128:(h // 2) * 128 + 128], lhsT=wTb[sl][rows(h), h // 2, n2 * 64:n2 * 64 + 64],
                        rhs=Xb[rows(h), n2, h // 2, :], start=True, stop=True)
                vn4 = V(vn.ap.rearrange("p (c r) f -> p c r f", r=2), vn.all.keys)
                ua4 = V(uaug[sl].ap.rearrange("p (c r) f -> p c r f", r=2), uaug[sl].all.keys)
                for par in range(2):
                    P.I("dve", "tensor_tensor", out=vn4[rn, :, par, :], in0=ua4[rn, :, par, :],
                        in1=V(bvp[par].ap[rn, 0:256].rearrange("p (c f) -> p c f", f=128), bvp[par][:, 0:256].keys), op=ALU.subtract)
                bd2 = rbank()
                for h in range(4):
                    P.I("pe", "matmul", out=bd2[rows(h), (h // 2) * 128:(h // 2) * 128 + 128], lhsT=kdt[sl][rn, h, :],
                        rhs=vn[rn, h, :], start=True, stop=True)
                for hp in range(2):
                    P.I("dve", "scalar_tensor_tensor", out=XC[:, hp, :], in0=XC[:, hp, :], scalar=decb[:, hp, n:n + 1],
                        in1=bd2[:, hp * 128:(hp + 1) * 128], op0=ALU.mult, op1=ALU.add)
            pop = [pb[6], rbank()]
            pox = [rbank(), rbank()]
            for part in range(2):
                for h in range(4):
                    hp = h // 2
                    base = part * 256 + hp * 128
                    P.I("pe", "matmul", out=pop[h % 2][rows(h), base:base + 128], lhsT=vn[:, h, part * 64:part * 64 + 64], rhs=atC[sl][:, h, :],
                        start=True, stop=True)
                    for n2 in range(2):
                        P.I("pe", "matmul", out=pox[h % 2][rows(h), base + n2 * 64:base + n2 * 64 + 64],
                            lhsT=Xb[rows(h), n2, hp, part * 64:part * 64 + 64],
                            rhs=qgTb[rows(h), hp, tt * 128 + n2 * 64:tt * 128 + n2 * 64 + 64], start=True, stop=True)
            for par in range(2):
                pr = slice(par * 64, par * 64 + 64)
                P.I("act", "activation", out=ostC[pr, :, tok], in_=V(pop[par].ap[pr, 0:256].rearrange("p (c t) -> p c t", t=128), pop[par][:, 0:256].keys), func=AF.Copy)
                P.I("dve", "tensor_tensor", out=ostC[pr, :, tok], in0=ostC[pr, :, tok],
                    in1=V(pox[par].ap[pr, 0:256].rearrange("p (c t) -> p c t", t=128), pox[par][:, 0:256].keys), op=ALU.add)
                P.I("act", "activation", out=accC[pr, 0:256], in_=pop[par][pr, 256:512], func=AF.Copy)
                P.I("dve", "tensor_tensor", out=PTC[pr, :, tok], in0=V(accC.ap[pr, 0:256].rearrange("p (c t) -> p c t", t=128), accC[:, 0:256].keys),
                    in1=V(pox[par].ap[pr, 256:512].rearrange("p (c t) -> p c t", t=128), pox[par][:, 256:512].keys), op=ALU.add)

        for pair in range(2):
            gens = [c_prep(pair * 2 + i, i) for i in range(2)]
            alive = [True, True]
            while any(alive):
                for i in range(2):
                    if alive[i]:
                        try:
                            next(gens[i])
                        except StopIteration:
                            alive[i] = False
            for i in range(2):
                if DBG.get("pcut", 99) > 5:
                    c_scan_out(pair * 2 + i, i)
        P.I("sp", "dma_start", out=dv(dfo_d[g][:, 0], "dfo%d" % g), in_=ostC.all)
        P.I("sp", "dma_start", out=dv(dfb_d[g][:, 0], "dfb%d" % g), in_=PTC.all)
        P.I("sp", "dma_start", out=dv(dfb_d[g][:, 2], "dfb%d" % g), in_=zsC.all)

    def exchange(l):
        P.I("dve", "tensor_copy", out=packS[:, 0:256], in_=V(XC.ap.rearrange("p c f -> p (c f)"), XC.all.keys))
        P.I("dve", "tensor_copy", out=packS[:, 256:384], in_=V(SD.ap.rearrange("p c f -> p (c f)"), SD.all.keys))
        P.I("dve", "tensor_copy", out=packS[:, 384:386], in_=DcD[:, :, 32])
        if local_only and l == layers[-1]:
            P.I("sp", "dma_start", out=dv(pack_o, "pack"), in_=packS.all)
            return False
        if use_cc:
            P.I("sp", "dma_start", out=dv(packd, "packd"), in_=packS.all)
            P.custom("pool", lambda e: e.collective_compute("AllGather", op=ALU.bypass, replica_groups=[[0, 1, 2, 3], [4, 5, 6, 7]],
                                                            ins=[packd.opt()], outs=[gathd.opt()]),
                     reads=DK("packd"), writes=DK("gathd"), dma=True, cc=True)
            P.I("sp", "dma_start", out=gat.all, in_=dv(gathd.rearrange("(r p) f -> p r f", p=128), "gathd"))
        else:
            P.I("sp", "dma_start", out=gat.all, in_=dv(gath_d.rearrange("r p f -> p r f")))
        P.I("dve", "memset", ap=accS.all, constant=0.0)
        for j in range(1, 4):
            gC = V(gat.ap[:, j - 1, 0:256].rearrange("p (c f) -> p c f", f=128), gat[:, j - 1, 0:256].keys)
            gD = V(gat.ap[:, j - 1, 256:384].rearrange("p (c f) -> p c f", f=64), gat[:, j - 1, 256:384].keys)
            if j == 1:
                P.I("dve", "tensor_copy", out=curC.all, in_=gC[:, :, 0:64])
                P.I("dve", "tensor_copy", out=curD.all, in_=gD)
            else:
                bt_ = rbank()
                for hp in range(2):
                    P.I("pe", "transpose", out=bt_[0:64, hp * 128:(hp + 1) * 128], in_=gC[:, hp, 64:128], identity=ident)
                P.I("act", "activation", out=MTf.all, in_=V(bt_.ap[0:64, 0:256].rearrange("p (c f) -> p c f", f=128), bt_[:, 0:256].keys), func=AF.Copy)
                bs_ = rbank()
                for hp in range(2):
                    P.I("pe", "matmul", out=bs_[0:64, hp * 64:(hp + 1) * 64], lhsT=ident[:, 64:128], rhs=curC[:, hp, :], start=True, stop=True)
                P.I("act", "activation", out=curF.all, in_=V(bs_.ap[0:64, 0:128].rearrange("p (c f) -> p c f", f=64), bs_[:, 0:128].keys), func=AF.Copy)
                bm = rbank()
                for h in range(4):
                    hp = h // 2
                    rhs_ = curC[0:64, hp, :] if h % 2 == 0 else curF[:, hp, :]
                    P.I("pe", "matmul", out=bm[rows(h), hp * 64:(hp + 1) * 64], lhsT=MTf[:, hp, (h % 2) * 64:(h % 2) * 64 + 64], rhs=rhs_, start=True, stop=True)
                P.I("dve", "tensor_tensor", out=curC.all, in0=gC[:, :, 0:64], in1=V(bm.ap[:, 0:128].rearrange("p (c f) -> p c f", f=64), bm[:, 0:128].keys), op=ALU.add)
                for hp in range(2):
                    P.I("dve", "scalar_tensor_tensor", out=curD[:, hp, :], in0=curD[:, hp, :], scalar=gat[:, j - 1, 384 + hp:385 + hp],
                        in1=gD[:, hp, :], op0=ALU.mult, op1=ALU.add)
            P.I("dve", "scalar_tensor_tensor", out=accS[:, 0], in0=curC.all, scalar=sel[:, j:j + 1], in1=accS[:, 0], op0=ALU.mult, op1=ALU.add)
            P.I("dve", "scalar_tensor_tensor", out=accS[:, 1], in0=curD.all, scalar=sel[:, j:j + 1], in1=accS[:, 1], op0=ALU.mult, op1=ALU.add)
        P.I("act", "activation", out=SinB.all, in_=accS.all, func=AF.Copy)
        return True

    def post_group(l, g):
        P.I("sp", "dma_start", out=ost.all, in_=dv(dfo_d[g], "dfo%d" % g))
        P.I("sp", "dma_start", out=dbf.all, in_=dv(dfb_d[g], "dfb%d" % g))
        for m in range(2):
            if ("C" in skip and m == 0) or ("D" in skip and m == 1):
                P.I("dve", "memset", ap=mixCD[:, m * 2:m * 2 + 2, :], constant=0.0)
                continue
            for hp in range(2):
                for hh in range(2):
                    h = hp * 2 + hh
                    bk = rbank()
                    P.I("pe", "matmul", out=bk[rows(h), :], lhsT=SinB[rows(h), m, hp, :], rhs=dbf[rows(h), m, hp, :], start=True, stop=True)
                    P.I("dve", "tensor_tensor", out=ofQ[rows(h), :], in0=ost[rows(h), m, hp, :], in1=bk[rows(h), :], op=ALU.add)
                P.I("act", "activation", out=sqQ.all, in_=ofQ.all, func=AF.Square)
                bq = rbank()
                P.I("pe", "matmul", out=bq.all, lhsT=bonesb.all, rhs=sqQ.all, start=True, stop=True)
                P.I("act", "activation", out=rtQ.all, in_=bq.all, func=AF.Sqrt, bias=EPS, scale=1.0 / 64)
                P.I("dve", "reciprocal", out=rtQ.all, in_=rtQ.all)
                P.I("dve", "scalar_tensor_tensor", out=t1Q.all, in0=ofQ.all, scalar=pc(32 + m), in1=rtQ.all, op0=ALU.mult, op1=ALU.mult)
                P.I("dve", "tensor_tensor", out=mixCD[:, m * 2 + hp, :], in0=t1Q.all, in1=dbf[:, 2 + m, hp, :], op=ALU.mult)
        wout_part(g, mixCD, 4)

    nw_idx = {0: (0, 1), 1: (2, 3)}
    for li, l in enumerate(layers):
        load_nw(nw_idx[l][0])
        layer_setup(l)
        halo_prep(l, dv(xh_d) if (li == 0 or not use_cc) else None)
        for g in range(NG):
            norm_group(g, None)
            if "A" in skip:
                P.I("dve", "memset", ap=mixAB[:, 0:2, :], constant=0.0)
            else:
                mixer_A(l, g, g == 0)
            w2 = mixer_B(l, g, g == 0)
            wout_part(g, mixAB, 0)
            if "C" not in skip:
                mixer_C(l, g, g == 0, w2)
            if "D" not in skip:
                mixer_D(l, g, g == 0)
        if "X" not in skip:
            cont = exchange(l)
            if not cont:
                break
        load_nw(nw_idx[l][1])
        for g in range(NG):
            if "X" not in skip:
                post_group(l, g)
            norm_group(g, None)
            ffn_group(l, g, SF)
        if use_cc and li + 1 < len(layers):
            P.I("sp", "dma_start", out=dv(xsend, "xsend"), in_=xs[124:128, 15, :])
            P.custom("pool", lambda e: e.collective_compute("AllGather", op=ALU.bypass, replica_groups=[[0, 1, 2, 3], [4, 5, 6, 7]],
                                                            ins=[xsend.opt()], outs=[xgath.opt()]),
                     reads=DK("xsend"), writes=DK("xgath"), dma=True, cc=True)
    if local_only:
        allk = DK("pack")
        for g in range(NG):
            allk = allk + DK("dfo%d" % g) + DK("dfb%d" % g)
        P.ops.append(Op("sp", ("nop", None), allk, (), False))
        P.finalize()
        return nc, P
    if final_norm:
        load_nw(4)
        for Tt in range(16):
            P.I("act", "activation", out=junk.all, in_=xs[:, Tt, :], func=AF.Square, accum_out=ss[:, Tt:Tt + 1])
            P.I("act", "activation", out=rstd[:, Tt:Tt + 1], in_=ss[:, Tt:Tt + 1], func=AF.Sqrt, bias=EPS, scale=1.0 / D_MODEL)
            P.I("dve", "reciprocal", out=rstd[:, Tt:Tt + 1], in_=rstd[:, Tt:Tt + 1])
            P.I("dve", "scalar_tensor_tensor", out=xs[:, Tt, :], in0=xs[:, Tt, :], scalar=rstd[:, Tt:Tt + 1], in1=nwb.all,
                op0=ALU.mult, op1=ALU.mult)
    for Tt in range(16):
        P.I("sp", "dma_start", out=dv(y_o[Tt * 128:(Tt + 1) * 128, :], "y"), in_=xs[:, Tt, :])
    P.ops.append(Op("sp", ("nop", None), DK("y"), (), False))
    P.finalize()
    return nc, P


def host_consts(core):
    seg = core % 4
    c = np.zeros((128, 11, 128), np.float32)
    i = np.arange(128)
    c[:, 0, :] = np.eye(128)
    c[:, 1, :] = (i[None, :] <= i[:, None])
    same = (i[:, None] // 64) == (i[None, :] // 64)
    c[:, 2, :] = same & (i[None, :] >= i[:, None])
    c[:, 3, :] = same & (i[None, :] > i[:, None])
    c[:, 4, :] = same & (i[None, :] < i[:, None])
    c[:, 5, :] = 1.0
    c[:, 6, :] = same
    c[:, 7, 0:64] = (i[:, None] % 64) == np.arange(64)[None, :]
    c[:, 8, seg] = 1.0
    c[:, 9, :] = -1.0 * (same & (i[None, :] > i[:, None]))
    c[:, 10, :] = -1.0 * (same & (i[None, :] < i[:, None]))
    for t in range(4):
        if seg > 0:
            c[(seg - 1) * 4 + t, 8, 4 + t] = 1.0
    ehp = np.zeros((4, 2, 128), np.float32)
    for h in range(4):
        ehp[h, h // 2, (h % 2) * 64:(h % 2) * 64 + 64] = 1.0
    rmask = np.ones((128, 512), np.float32)
    rmask[:, ::64] = 0.0
    return c, ehp, rmask


def host_params(inp):
    pcol = np.zeros((2, 128, 40), np.float32)
    p = np.arange(128)
    for l in range(2):
        scw = inp["sc_conv_w"][l]
        for c in range(2):
            for k in range(3):
                pcol[l, :, c * 3 + k] = scw[k, c * 128 + p]
        dnw = inp["dn_conv_w"][l]
        for c in range(6):
            for k in range(4):
                pcol[l, :, 6 + c * 4 + k] = dnw[k, c * 128 + p]
        for c in range(2):
            pcol[l, :, 30 + c] = inp["gla_gate_bias"][l][c * 128 + p]
        pcol[l, :, 32] = inp["dn_norm_w"][l][p % 64]
        pcol[l, :, 33] = inp["gla_norm_w"][l][p % 64]
    p4 = np.stack([inp["dn_a_log"], inp["dn_dt_bias"]], axis=-1).astype(np.float32)
    lnwb = np.stack([inp["sgu_ln_w"], inp["sgu_ln_b"]], axis=1).astype(np.float32)
    nrm = np.stack([inp["norm1_w"][0], inp["norm2_w"][0], inp["norm1_w"][1], inp["norm2_w"][1],
                    inp["final_norm_w"]], axis=0).astype(np.float32)
    return pcol, p4, lnwb, nrm


def _in_maps(inp, consts, params, xcores, xh, gath):
    pcol, p4, lnwb, nrm = params
    maps = []
    for c in range(NCORES):
        cst, ehp, rmask = consts[c]
        maps.append({
            "x": xcores[c], "xh": xh[c], "w_in": inp["w_in"], "w_out": inp["w_out"],
            "w_gate_up": inp["w_gate_up"], "w_down": inp["w_down"], "nrm": nrm, "pcol": pcol, "p4": p4,
            "lnwb": lnwb, "wsp": inp["sgu_w_spatial"], "bsp": inp["sgu_b_spatial"], "wg2": inp["gla_w_gate2"],
            "cst": cst, "ehp": ehp, "rmask": rmask, "gath": gath[c],
        })
    return maps


def _halo(xcores):
    xh = []
    for c in range(NCORES):
        if c % 4 == 0:
            xh.append(np.zeros((4, D_MODEL), np.float32))
        else:
            xh.append(np.ascontiguousarray(xcores[c - 1][NT - 4:NT]))
    return xh


def kernel(**inputs):
    inp = {k: np.ascontiguousarray(np.asarray(v, dtype=np.float32)) for k, v in inputs.items()}
    x = inp["x"].reshape(2 * SEQ, D_MODEL)
    params = host_params(inp)
    consts = [host_consts(c) for c in range(NCORES)]
    xcores = [np.ascontiguousarray(x[c * NT:(c + 1) * NT]) for c in range(NCORES)]
    zg = [np.zeros((4, 128, 392), np.float32) for _ in range(NCORES)]
    xh = _halo(xcores)
    nc, _ = build([0, 1], False, True, use_cc=True)
    res = run_bass_kernel_spmd(nc, _in_maps(inp, consts, params, xcores, xh, zg), core_ids=list(range(NCORES)))
    out = np.concatenate([np.asarray(r["y"], np.float32) for r in res.results], axis=0)
    return out.reshape(2, SEQ, D_MODEL).astype(np.float32)
```

```python
import numpy as np
import concourse.bass as bass
import concourse.mybir as mybir
from concourse.bass_utils import run_bass_kernel_spmd

F32 = mybir.dt.float32
BF16 = mybir.dt.bfloat16
AF = mybir.ActivationFunctionType
ALU = mybir.AluOpType

D_MODEL = 1024
SEQ = 8192
NCORES = 8
NT = 2048
G = 512
NG = NT // G
D_FF = 2816
NJ = D_FF // 128
IN_COLS = 3352
EPS = 1e-6
SB_BASE = 16512
SB_END = 229312
GRAN = 256

C_AU, C_AV = 0, 256
C_BB, C_BC, C_BH = 512, 768, 1024
C_CQ, C_CK, C_CV, C_CA, C_CB, C_CZ = 1280, 1536, 1792, 2048, 2052, 2056
C_DQ, C_DK, C_DV, C_DG, C_DZ = 2312, 2568, 2824, 3080, 3096


DBG = {}
ATTACH_WAITS = True
ATTACH_ENGS = ("act", "dve", "pe")


class V:
    __slots__ = ("ap", "keys")

    def __init__(self, ap, keys):
        self.ap = ap
        self.keys = keys

    def b(self, shape):
        return V(self.ap.broadcast_to(list(shape)), self.keys)

    def __getitem__(self, idx):
        return V(self.ap[idx], self.keys)


class T:
    def __init__(self, space, name, ap, off, shape, esz):
        self.space, self.name, self.ap, self.off, self.shape, self.esz = space, name, ap, off, shape, esz
        st = [1] * len(shape)
        for i in range(len(shape) - 2, 0, -1):
            st[i] = st[i + 1] * shape[i + 1]
        self.st = st

    def __getitem__(self, idx):
        if not isinstance(idx, tuple):
            idx = (idx,)
        lo = 0
        hi = 0
        full = list(idx) + [slice(None)] * (len(self.shape) - len(idx))
        for d in range(1, len(self.shape)):
            ix = full[d]
            if isinstance(ix, slice):
                a = 0 if ix.start is None else ix.start
                b_ = self.shape[d] if ix.stop is None else ix.stop
            else:
                a, b_ = ix, ix + 1
            lo += a * self.st[d]
            hi += (b_ - 1) * self.st[d]
        b0 = self.off + lo * self.esz
        b1 = self.off + (hi + 1) * self.esz
        if self.space == "ps":
            keys = (("ps", self.off // 2048),)
        else:
            keys = tuple((self.space, g) for g in range(b0 // GRAN, (b1 - 1) // GRAN + 1))
        return V(self.ap[idx], keys)

    @property
    def all(self):
        return self[(slice(None),)]


class Op:
    __slots__ = ("eng", "fn", "reads", "writes", "dma", "eidx", "deps", "sig", "count", "sem", "waits", "cc", "raw")

    def __init__(self, eng, fn, reads, writes, dma):
        self.eng, self.fn, self.reads, self.writes, self.dma = eng, fn, reads, writes, dma
        self.deps = []
        self.sig = False
        self.count = 0
        self.sem = None
        self.waits = []
        self.cc = False


ENG_ATTR = {"pe": "tensor", "act": "scalar", "dve": "vector", "pool": "gpsimd", "sp": "sync"}
OUT_NAMES = ("out", "accum_out", "ap")


class Prog:
    def __init__(self, nc):
        self.nc = nc
        self.ops = []
        self.sb_off = SB_BASE
        self.ps_banks = 0
        self.marks = []

    def sb(self, name, shape, dt, at=None):
        esz = 2 if dt == BF16 else 4
        n = 1
        for s in shape[1:]:
            n *= s
        nbytes = ((n * esz + 63) // 64) * 64
        if at is None:
            off = self.sb_off
            self.sb_off += nbytes
        else:
            off = at
        assert off + nbytes <= SB_END, ("SBUF overflow", name, off, nbytes)
        h = self.nc.alloc_sbuf_tensor_at(name, list(shape), dt, offset=off)
        return T("sb", name, h.ap(), off, list(shape), esz)

    def ps(self, name, shape, dt):
        h = self.nc.alloc_psum_tensor(name, list(shape), dt)
        esz = 2 if dt == BF16 else 4
        off = self.ps_banks * 2048
        self.ps_banks += 1
        return T("ps", name, h.ap(), off, list(shape), esz)

    def I(self, eng, method, **kw):
        reads, writes, args = [], [], {}
        for k, v in kw.items():
            if isinstance(v, V):
                args[k] = v.ap
                if k in OUT_NAMES:
                    writes.extend(v.keys)
                else:
                    reads.extend(v.keys)
                    writes.extend(kk for kk in v.keys if kk[0] == "ps")
            else:
                args[k] = v
        extra_r = args.pop("_r", None)
        dma = method == "dma_start"
        fn = (method, args)
        op = Op(eng, fn, tuple(reads), tuple(writes), dma)
        self.ops.append(op)
        return op

    def custom(self, eng, fn, reads=(), writes=(), dma=False, cc=False):
        op = Op(eng, ("custom", fn), tuple(reads), tuple(writes), dma)
        op.cc = cc
        self.ops.append(op)
        return op

    def finalize(self, nsem_dma=10):
        nc = self.nc
        last_w = {}
        readers = {}
        per_eng = {e: [] for e in ENG_ATTR}
        for i, op in enumerate(self.ops):
            op.eidx = len(per_eng[op.eng])
            per_eng[op.eng].append(op)
            deps = set()
            for k in op.reads:
                w = last_w.get(k)
                if w is not None:
                    deps.add(w)
            for k in op.writes:
                w = last_w.get(k)
                if w is not None:
                    deps.add(w)
                for r in readers.get(k, ()):
                    deps.add(r)
            deps.discard(op)
            op.deps = deps
            op.raw = set(last_w[k] for k in op.reads if last_w.get(k) is not None)
            for k in op.reads:
                readers.setdefault(k, []).append(op)
            for k in op.writes:
                last_w[k] = op
                readers[k] = []
        for op in self.ops:
            need = []
            for d in op.deps:
                if d.dma:
                    need.append(d)
                    continue
                if d.eng == op.eng and not op.dma:
                    if op.eng == "pe":
                        continue
                    if op.eng in ("act", "dve") and op.eidx - d.eidx > 3:
                        continue
                    if op.eng in ("act", "dve") and d not in op.raw:
                        continue
                need.append(d)
                d.sig = True
            op.deps = need
        with_sems = {}
        eng_sem = {}
        dma_sems = {}
        names = []
        for e in ENG_ATTR:
            eng_sem[e] = nc.alloc_semaphore("s_" + e)
            dma_sems[e] = [nc.alloc_semaphore("d_%s_%d" % (e, i)) for i in range(nsem_dma)] if e in ("sp", "pool", "act") else []
        for e, lst in per_eng.items():
            cnt = 0
            ndma = 0
            for op in lst:
                if op.cc:
                    op.sem = nc.alloc_semaphore("cc_%d" % op.eidx)
                    op.count = 1
                elif op.dma:
                    op.sem = dma_sems[e][ndma % nsem_dma]
                    prev = 16 * (ndma // nsem_dma)
                    op.count = prev + 16
                    if prev > 0:
                        op.waits.append((op.sem, prev))
                    ndma += 1
                elif op.sig:
                    cnt += 1
                    op.count = cnt
                    op.sem = eng_sem[e]
        for e, lst in per_eng.items():
            waited = {}
            for op in lst:
                ws = {}
                for (s, c) in op.waits:
                    ws[s] = max(ws.get(s, 0), c)
                for d in op.deps:
                    ws[d.sem] = max(ws.get(d.sem, 0), d.count)
                out = []
                for s, c in ws.items():
                    if waited.get(s, 0) >= c:
                        continue
                    waited[s] = c
                    out.append((s, c))
                op.waits = out
        self.nwaits = sum(len(op.waits) for op in self.ops)
        with nc.Block() as block:
            for e, lst in per_eng.items():
                if not lst:
                    continue

                def body(engobj, lst=lst):
                    for op in lst:
                        kind, payload = op.fn
                        attach = None
                        waits = op.waits
                        if ATTACH_WAITS and waits and op.eng in ATTACH_ENGS and kind not in ("custom", "nop") and not op.dma:
                            attach = waits[-1]
                            waits = waits[:-1]
                        for (s, c) in waits:
                            engobj.wait_ge(s, c)
                        if kind == "custom":
                            ins = payload(engobj)
                        elif kind == "nop":
                            ins = None
                        else:
                            ins = getattr(engobj, kind)(**payload)
                        if attach is not None:
                            ins._wait_ge(attach[0], attach[1])
                        if ins is not None:
                            if op.cc:
                                ins.then_inc(op.sem, 1)
                            elif op.dma:
                                ins.then_inc(op.sem, 16)
                            elif op.sig:
                                ins.then_inc(op.sem, 1)

                getattr(block, ENG_ATTR[e])(body)


def build(layers, local_only, final_norm, use_cc=False, dbg=None, skip=()):
    nc = bass.Bass("TRN2", target_bir_lowering=False, num_devices=NCORES) if use_cc else bass.Bass("TRN2", target_bir_lowering=False)
    P = Prog(nc)

    def din(name, shape):
        return nc.dram_tensor(name, list(shape), F32, kind="ExternalInput").ap()

    x_d = din("x", [NT, D_MODEL])
    xh_d = din("xh", [4, D_MODEL])
    w_in_d = din("w_in", [2, D_MODEL, IN_COLS])
    w_out_d = din("w_out", [2, D_MODEL, D_MODEL])
    w_gu_d = din("w_gate_up", [2, D_MODEL, 2 * D_FF])
    w_dn_d = din("w_down", [2, D_FF, D_MODEL])
    nrm_d = din("nrm", [5, D_MODEL])
    pcol_d = din("pcol", [2, 128, 40])
    p4_d = din("p4", [2, 4, 2])
    lnwb_d = din("lnwb", [2, 2, 256])
    wsp_d = din("wsp", [2, 4, 128, 128])
    bsp_d = din("bsp", [2, 4, 128])
    wg2_d = din("wg2", [2, 16, 256])
    cst_d = din("cst", [128, 11, 128])
    ehp_d = din("ehp", [4, 2, 128])
    rmask_d = din("rmask", [128, 512])
    gath_d = din("gath", [4, 128, 392])
    if local_only:
        pack_o = nc.dram_tensor("pack", [128, 392], F32, kind="ExternalOutput").ap()
    else:
        y_o = nc.dram_tensor("y", [NT, D_MODEL], F32, kind="ExternalOutput").ap()
    packd = nc.dram_tensor("packd", [128, 392], F32, kind="Internal").ap()
    gathd = nc.dram_tensor("gathd", [512, 392], F32, kind="Internal").ap()
    xsend = nc.dram_tensor("xsend", [4, D_MODEL], F32, kind="Internal").ap()
    xgath = nc.dram_tensor("xgath", [16, D_MODEL], F32, kind="Internal").ap()
    dfo_d = nc.dram_tensor("dfo", [NG, 128, 2, 2, G], F32, kind="Internal").ap()
    dfb_d = nc.dram_tensor("dfb", [NG, 128, 4, 2, G], BF16, kind="Internal").ap()
    DK = lambda n: (("dram", n),)

    def dv(ap, name=None):
        return V(ap, DK(name) if name else ())

    xs = P.sb("xs", [128, 16, 1024], F32)
    cst = P.sb("cst", [128, 11, 128], F32)
    identb = P.sb("identb", [128, 128], BF16)
    bonesb = P.sb("bonesb", [128, 128], BF16)
    ehp = P.sb("ehp", [4, 2, 128], F32)
    rmask = P.sb("rmask", [128, 512], F32)
    nwb = P.sb("nwb", [128, 1024], F32)
    pcol = P.sb("pcol", [128, 40], F32)
    p4 = P.sb("p4", [4, 4], F32)
    ss = P.sb("ss", [128, 16], F32)
    rstd = P.sb("rstd", [128, 16], F32)
    hT = P.sb("hT", [128, 8, G], BF16)
    xn = [P.sb("xn%d" % i, [128, 1024], BF16) for i in range(2)]
    junk = P.sb("junk", [128, 1024], BF16)
    wo = P.sb("wo", [128, 8, 1024], BF16)
    SinB = P.sb("SinB", [128, 2, 2, 64], BF16)
    SCR = P.sb_off
    ident = cst[:, 0, :]
    maskL = cst[:, 1, :]
    BDUi = cst[:, 2, :]
    BDUs = cst[:, 3, :]
    BDLs = cst[:, 4, :]
    ones_f = cst[:, 5, :]
    eye2 = cst[:, 7, 0:64]
    sel = cst[:, 8, 0:4]
    nBDUs = cst[:, 9, :]
    nBDLs = cst[:, 10, :]

    pb = [P.ps("pb%d" % i, [128, 512], F32) for i in range(7)]
    pT = P.ps("pT", [128, 8, 128], BF16)
    rot = {"i": 0}

    def rbank():
        b = pb[2 + rot["i"] % 4]
        rot["i"] += 1
        return b

    wide = {"i": 0}

    def wbank():
        b = pb[wide["i"] % 6]
        wide["i"] += 1
        return b

    mmi = {"i": 0}

    def mbank():
        b = pb[mmi["i"] % 2]
        mmi["i"] += 1
        return b

    P.I("sp", "dma_start", out=cst.all, in_=dv(cst_d))
    P.I("sp", "dma_start", out=ehp.all, in_=dv(ehp_d))
    P.I("sp", "dma_start", out=rmask.all, in_=dv(rmask_d))
    P.I("dve", "tensor_copy", out=identb.all, in_=ident)
    P.I("dve", "tensor_copy", out=bonesb.all, in_=cst[:, 6, :])
    for t in range(16):
        P.I("sp", "dma_start", out=xs[:, t, :], in_=dv(x_d[t * 128:(t + 1) * 128, :]))

    def norm_group(g, nidx_loaded):
        for tt in range(4):
            Tt = g * 4 + tt
            P.I("act", "activation", out=junk.all, in_=xs[:, Tt, :], func=AF.Square, accum_out=ss[:, Tt:Tt + 1])
            P.I("act", "activation", out=rstd[:, Tt:Tt + 1], in_=ss[:, Tt:Tt + 1], func=AF.Sqrt, bias=EPS, scale=1.0 / D_MODEL)
            P.I("dve", "reciprocal", out=rstd[:, Tt:Tt + 1], in_=rstd[:, Tt:Tt + 1])
            xb = xn[tt % 2]
            P.I("dve", "scalar_tensor_tensor", out=xb.all, in0=xs[:, Tt, :], scalar=rstd[:, Tt:Tt + 1], in1=nwb.all,
                op0=ALU.mult, op1=ALU.mult)
            for fc in range(8):
                P.I("pe", "transpose", out=pT[:, fc, :], in_=xb[:, fc * 128:(fc + 1) * 128], identity=identb.all)
            P.I("act", "activation", out=hT[:, :, tt * 128:(tt + 1) * 128], in_=pT.all, func=AF.Copy)

    def load_nw(idx):
        P.I("sp", "dma_start", out=nwb.all, in_=dv(nrm_d[idx:idx + 1, :].partition_broadcast(128)[:, 0, :]))

    def ffn_group(l, g, S):
        gT, wg, wd, sg = S["gT"], S["wg"], S["wd"], S["sg"]
        nblk = (NJ + 1) // 2
        for jb in range(nblk):
            j0 = jb * 2
            nj = min(2, NJ - j0)
            w = wg[jb % 3]
            P.I("pool", "dma_start", out=w[:, :, 0, 0:nj * 128],
                in_=dv(w_gu_d[l][:, j0 * 128:(j0 + nj) * 128].rearrange("(kc p) n -> p kc n", p=128)))
            P.I("pool", "dma_start", out=w[:, :, 1, 0:nj * 128],
                in_=dv(w_gu_d[l][:, D_FF + j0 * 128:D_FF + (j0 + nj) * 128].rearrange("(kc p) n -> p kc n", p=128)))
            for jj in range(nj):
                j = j0 + jj
                pa, pu = wbank(), wbank()
                for kc in range(8):
                    P.I("pe", "matmul", out=pa.all, lhsT=w[:, kc, 0, jj * 128:(jj + 1) * 128], rhs=hT[:, kc, :],
                        start=(kc == 0), stop=(kc == 7))
                for kc in range(8):
                    P.I("pe", "matmul", out=pu.all, lhsT=w[:, kc, 1, jj * 128:(jj + 1) * 128], rhs=hT[:, kc, :],
                        start=(kc == 0), stop=(kc == 7))
                s_ = sg[j % 2]
                P.I("act", "activation", out=s_.all, in_=pa.all, func=AF.Silu)
                P.I("dve", "tensor_tensor", out=gT[:, j, :], in0=s_.all, in1=pu.all, op=ALU.mult)
        for half in range(2):
            w = wd[half]
            P.I("pool", "dma_start", out=w.all,
                in_=dv(w_dn_d[l][:, half * 512:(half + 1) * 512].rearrange("(j p) n -> p j n", p=128)))
            for tt in range(4):
                Tt = g * 4 + tt
                pa = wbank()
                for j in range(NJ):
                    P.I("pe", "matmul", out=pa.all, lhsT=gT[:, j, tt * 128:(tt + 1) * 128], rhs=w[:, j, :],
                        start=(j == 0), stop=(j == NJ - 1))
                P.I("dve", "tensor_tensor", out=xs[:, Tt, half * 512:(half + 1) * 512],
                    in0=xs[:, Tt, half * 512:(half + 1) * 512], in1=pa.all, op=ALU.add)

    def alloc_ffn_at(_unused=None):
        o = {"i": SCR}

        def a(name, shape, dt):
            t = P.sb(name, shape, dt, at=o["i"])
            esz = 2 if dt == BF16 else 4
            n = 1
            for s in shape[1:]:
                n *= s
            o["i"] += ((n * esz + 63) // 64) * 64
            return t
        S = {}
        S["gT"] = a("gT", [128, NJ, G], BF16)
        S["wg"] = [a("wg%d" % i, [128, 8, 2, 256], BF16) for i in range(3)]
        S["wd"] = [a("wd%d" % i, [128, NJ, 512], BF16) for i in range(2)]
        S["sg"] = [a("sg%d" % i, [128, G], F32) for i in range(2)]
        return S


    class Arena:
        def __init__(self, base):
            self.o = base

        def a(self, name, shape, dt):
            t = P.sb(name, shape, dt, at=self.o)
            esz = 2 if dt == BF16 else 4
            n = 1
            for s_ in shape[1:]:
                n *= s_
            self.o += ((n * esz + 63) // 64) * 64
            return t

    AR = Arena(SCR)
    wi = [AR.a("wi%d" % i, [128, 8, 528], BF16) for i in range(2)]
    mixAB = AR.a("mixAB", [128, 4, G], BF16)
    hTh = AR.a("hTh", [128, 8, 4], BF16)
    WsT = AR.a("WsT", [128, 4, 128], BF16)
    bsb = AR.a("bsb", [128, 2, 128], F32)
    lnw = AR.a("lnw", [128, 2, 256], F32)
    wg2b = AR.a("wg2b", [16, 256], BF16)
    haloB = AR.a("haloB", [128, 2, 4], F32)
    haloC = AR.a("haloC", [128, 6, 4], F32)
    XC = AR.a("XC", [128, 2, 128], F32)
    SD = AR.a("SD", [128, 2, 64], F32)
    DcD = AR.a("DcD", [128, 2, 33], F32)
    packS = AR.a("packS", [128, 392], F32)
    R1 = AR.o
    S1 = Arena(R1)
    xht = S1.a("xht", [4, 1024], F32)
    xhn = S1.a("xhn", [4, 1024], BF16)
    gxh = S1.a("gxh", [16, 1024], F32)
    wsp = S1.a("wsp", [128, 4, 128], F32)
    A1 = Arena(R1)
    uA = A1.a("uA", [128, 2, G], F32)
    vgA = [A1.a("vgA%d" % i, [128, 256], F32) for i in range(2)]
    vlnA = [A1.a("vlnA%d" % i, [128, 256], BF16) for i in range(2)]
    stA = A1.a("stA", [128, 8], F32)
    tmpA = A1.a("tmpA", [128, 2, 128], F32)
    xgA = A1.a("xgA", [128, G], F32)
    tgA = A1.a("tgA", [128, G], F32)
    B1 = Arena(R1)
    gbB = B1.a("gbB", [128, 2, G], F32)
    gcB = B1.a("gcB", [128, G], F32)
    tB = B1.a("tB", [128, 2, G + 2], F32)
    accB = B1.a("accB", [128, G], F32)
    D1 = Arena(R1)
    qfD = D1.a("qfD", [128, 2, G], F32)
    kfD = D1.a("kfD", [128, 2, G], F32)
    vtD = D1.a("vtD", [128, 4, 256], BF16)
    glrD = D1.a("glrD", [16, G], BF16)
    zsD = D1.a("zsD", [128, 2, G], BF16)
    efD = D1.a("efD", [128, G], F32)
    csD = D1.a("csD", [128, 2, G], F32)
    d1D = D1.a("d1D", [128, G], F32)
    aeD = D1.a("aeD", [128, G], F32)
    a3D = D1.a("a3D", [128, 2, G], F32)
    qaD = D1.a("qaD", [128, 2, G], BF16)
    kaD = D1.a("kaD", [128, 2, G], BF16)
    qgD = D1.a("qgD", [128, 2, G], BF16)
    klD = D1.a("klD", [128, 2, G], BF16)
    qgDD = D1.a("qgDD", [128, 2, G], BF16)
    kltD = D1.a("kltD", [128, 4, 256], BF16)
    atD = [D1.a("atD%d" % i, [128, 4, 128], BF16) for i in range(2)]
    SbD = [D1.a("SbD%d" % i, [128, 2, 2, 64], BF16) for i in range(2)]
    ostD = D1.a("ostD", [128, 2, G], F32)
    R2 = R1
    C1 = Arena(R2)
    cbC = C1.a("cbC", [128, G + 3], F32)
    accC = C1.a("accC", [128, G], F32)
    qsC = accC
    sqC = C1.a("sqC", [128, G], BF16)
    rtC = C1.a("rtC", [128, G], F32)
    qTb = C1.a("qTb", [128, 2, G], BF16)
    kTb = C1.a("kTb", [128, 2, G], BF16)
    kbTb = C1.a("kbTb", [128, 2, G], BF16)
    qgTb = C1.a("qgTb", [128, 2, G], BF16)
    vTb = C1.a("vTb", [128, 2, G], BF16)
    zsC = C1.a("zsC", [128, 2, G], BF16)
    t1C = C1.a("t1C", [4, G], F32)
    gcC = C1.a("gcC", [4, G], F32)
    egC = C1.a("egC", [4, G], F32)
    btC = C1.a("btC", [4, G], F32)
    tokq = C1.a("tokq", [128, 4, 4, 4], F32)
    tokb = C1.a("tokb", [128, 4, 4], F32)
    decb = C1.a("decb", [128, 2, 8], F32)
    vbt = [C1.a("vbt%d" % i, [128, 4, 64], BF16) for i in range(2)]
    kbgt = [C1.a("kbgt%d" % i, [128, 4, 64], BF16) for i in range(2)]
    kdt = [C1.a("kdt%d" % i, [128, 4, 64], BF16) for i in range(2)]
    bufA = [C1.a("bufA%d" % i, [128, 4, 128], F32) for i in range(2)]
    bufB = [C1.a("bufB%d" % i, [128, 4, 128], F32) for i in range(2)]
    bufC = [C1.a("bufC%d" % i, [128, 4, 128], F32) for i in range(2)]
    Nb = [[C1.a("Nb%d_%d" % (j, i), [128, 4, 128], BF16) for i in range(2)] for j in range(2)]
    NTb = [[C1.a("NTb%d_%d" % (j, i), [128, 4, 128], BF16) for i in range(2)] for j in range(2)]
    atC = [C1.a("atC%d" % i, [128, 4, 128], BF16) for i in range(2)]
    TTb = [[C1.a("TTb%d_%d" % (j, i), [128, 4, 128], BF16) for i in range(2)] for j in range(2)]
    uaug = [C1.a("uaug%d" % i, [128, 4, 128], F32) for i in range(2)]
    wTb = [C1.a("wTb%d" % i, [128, 2, 128], BF16) for i in range(2)]
    _vnb = C1.a("vnb", [128, 4, 128], BF16)
    vnb = [_vnb, _vnb]
    _XbC = C1.a("XbC", [128, 2, 2, 128], BF16)
    XbC = [_XbC, _XbC]
    ostC = C1.a("ostC", [128, 2, G], F32)
    PTC = C1.a("PTC", [128, 2, G], BF16)
    END_LOCAL = C1.o
    Q1 = Arena(R1)
    gat = Q1.a("gat", [128, 4, 392], F32)
    curC = Q1.a("curC", [128, 2, 64], F32)
    curD = Q1.a("curD", [128, 2, 64], F32)
    accS = Q1.a("accS", [128, 2, 2, 64], F32)
    MTf = Q1.a("MTf", [64, 2, 128], F32)
    curF = Q1.a("curF", [64, 2, 64], F32)
    ost = Q1.a("ost", [128, 2, 2, G], F32)
    dbf = Q1.a("dbf", [128, 4, 2, G], BF16)
    ofQ = Q1.a("ofQ", [128, G], F32)
    sqQ = Q1.a("sqQ", [128, G], BF16)
    rtQ = Q1.a("rtQ", [128, G], F32)
    t1Q = Q1.a("t1Q", [128, G], F32)
    mixCD = Q1.a("mixCD", [128, 4, G], BF16)
    print("SBUF plan: SCR=%d R1=%d R2=%d endlocal=%d endpost=%d" % (SCR, R1, R2, END_LOCAL, Q1.o))
    SF = alloc_ffn_at(max(Q1.o, 0) if False else None)

    def rows(h):
        return slice((h % 2) * 64, (h % 2) * 64 + 64)

    def pc(col):
        return pcol[:, col:col + 1]

    def layer_setup(l):
        P.I("sp", "dma_start", out=pcol.all, in_=dv(pcol_d[l]))
        P.I("sp", "dma_start", out=p4[:, 0:2], in_=dv(p4_d[l]))
        P.I("dve", "tensor_scalar", out=pcol[:, 34:36], in0=pcol[:, 30:32], scalar1=-1.0, scalar2=None, op0=ALU.mult)
        P.I("act", "activation", out=p4[:, 2:3], in_=p4[:, 0:1], func=AF.Exp)
        P.I("dve", "tensor_scalar", out=p4[:, 2:3], in0=p4[:, 2:3], scalar1=-1.0, scalar2=None, op0=ALU.mult)
        for hf in range(2):
            P.I("pool", "dma_start", out=wo[:, hf * 4:(hf + 1) * 4, :],
                in_=dv(w_out_d[l][hf * 512:(hf + 1) * 512, :].rearrange("(kc p) n -> p kc n", p=128)))
        P.I("sp", "dma_start", out=wsp.all, in_=dv(wsp_d[l].rearrange("h t s -> t h s")))
        P.I("dve", "tensor_tensor", out=wsp.all, in0=wsp.all, in1=maskL[:, None, :].b([128, 4, 128]), op=ALU.mult)
        bk = rbank()
        for h in range(4):
            P.I("pe", "transpose", out=bk[:, h * 128:(h + 1) * 128], in_=wsp[:, h, :], identity=ident)
        P.I("act", "activation", out=WsT.all, in_=bk.all, func=AF.Copy)
        for h in range(4):
            P.I("sp", "dma_start", out=bsb[rows(h), h // 2, :], in_=dv(bsp_d[l][h:h + 1, :].partition_broadcast(64)[:, 0, :]))
        for i in range(2):
            P.I("sp", "dma_start", out=lnw[:, i, :], in_=dv(lnwb_d[l][i:i + 1, :].partition_broadcast(128)[:, 0, :]))
        P.I("pool", "dma_start", out=wg2b.all, in_=dv(wg2_d[l]))
        P.I("dve", "memset", ap=XC.all, constant=0.0)
        for hp in range(2):
            P.I("dve", "tensor_copy", out=XC[:, hp, 64:128], in_=eye2)
        P.I("dve", "memset", ap=SD.all, constant=0.0)
        P.I("dve", "memset", ap=DcD.all, constant=1.0)
        for i in range(2):
            P.I("dve", "memset", ap=uaug[i].all, constant=0.0)
        P.I("dve", "memset", ap=packS.all, constant=0.0)

    def halo_prep(l, from_x_dram):
        if from_x_dram is not None:
            P.I("sp", "dma_start", out=xht.all, in_=from_x_dram)
        else:
            P.I("sp", "dma_start", out=gxh.all, in_=dv(xgath, "xgath"))
            for half in range(2):
                bk = rbank()
                P.I("pe", "matmul", out=bk[0:4, :], lhsT=cst[0:16, 8, 4:8], rhs=gxh[:, half * 512:(half + 1) * 512], start=True, stop=True)
                P.I("act", "activation", out=xht[:, half * 512:(half + 1) * 512], in_=bk[0:4, :], func=AF.Copy)
        P.I("act", "activation", out=xhn.all, in_=xht.all, func=AF.Square, accum_out=p4[:, 3:4])
        P.I("act", "activation", out=p4[:, 3:4], in_=p4[:, 3:4], func=AF.Sqrt, bias=EPS, scale=1.0 / D_MODEL)
        P.I("dve", "reciprocal", out=p4[:, 3:4], in_=p4[:, 3:4])
        P.I("dve", "scalar_tensor_tensor", out=xhn.all, in0=xht.all, scalar=p4[:, 3:4], in1=nwb[0:4, :],
            op0=ALU.mult, op1=ALU.mult)
        for fc in range(8):
            P.I("pe", "transpose", out=pT[:, fc, 0:4], in_=xhn[:, fc * 128:(fc + 1) * 128], identity=identb[0:4, 0:4])
        P.I("act", "activation", out=hTh.all, in_=pT[:, :, 0:4], func=AF.Copy)

    def ip_fm(w, c0, n, rhs_of_kc=None, N=G):
        bk = mbank()
        for kc in range(8):
            P.I("pe", "matmul", out=bk[0:n, 0:N], lhsT=w[:, kc, c0:c0 + n],
                rhs=(hT[:, kc, :] if rhs_of_kc is None else rhs_of_kc(kc)), start=(kc == 0), stop=(kc == 7))
        return bk

    def ip_tm(w, c0, n, tt):
        bk = mbank()
        for kc in range(8):
            P.I("pe", "matmul", out=bk[:, 0:n], lhsT=hT[:, kc, tt * 128:(tt + 1) * 128], rhs=w[:, kc, c0:c0 + n],
                start=(kc == 0), stop=(kc == 7))
        return bk

    wslot = {"i": 0}

    def load_wi(l, c0, n):
        w = wi[wslot["i"] % 2]
        wslot["i"] += 1
        P.I("pool", "dma_start", out=w[:, :, 0:n], in_=dv(w_in_d[l][:, c0:c0 + n].rearrange("(kc p) n -> p kc n", p=128)))
        return w

    def gelu_tanh(out_v, ps_v, xg, tg):
        P.I("act", "activation", out=xg, in_=ps_v, func=AF.Copy)
        P.I("dve", "tensor_tensor", out=tg, in0=xg, in1=xg, op=ALU.mult)
        P.I("dve", "tensor_scalar", out=tg, in0=tg, scalar1=0.044715, scalar2=1.0, op0=ALU.mult, op1=ALU.add)
        P.I("dve", "tensor_tensor", out=tg, in0=tg, in1=xg, op=ALU.mult)
        P.I("act", "activation", out=tg, in_=tg, func=AF.Sigmoid, scale=1.5957691216057308)
        P.I("dve", "tensor_tensor", out=out_v, in0=xg, in1=tg, op=ALU.mult)

    def mixer_A(l, g, first):
        w = load_wi(l, 0, 512)
        for c in range(2):
            bk = ip_fm(w, C_AU + c * 128, 128)
            gelu_tanh(uA[:, c, :], bk.all, xgA.all, tgA.all)
        tmpA2 = V(tmpA.ap.rearrange("p c t -> p (c t)"), tmpA.all.keys)
        for tt in range(4):
            bk = ip_tm(w, C_AV, 256, tt)
            vg, vl = vgA[tt % 2], vlnA[tt % 2]
            gelu_tanh(vg.all, bk[:, 0:256], xgA[:, 0:256], tgA[:, 0:256])
            P.I("dve", "tensor_reduce", out=stA[:, 0:1], in_=vg.all, axis=mybir.AxisListType.X, op=ALU.add)
            P.I("act", "activation", out=stA[:, 4:5], in_=stA[:, 0:1], func=AF.Copy, scale=-1.0 / 256)
            P.I("dve", "tensor_scalar", out=vg.all, in0=vg.all, scalar1=stA[:, 4:5], scalar2=None, op0=ALU.add)
            P.I("act", "activation", out=tmpA2, in_=vg.all, func=AF.Square, accum_out=stA[:, 1:2])
            P.I("act", "activation", out=stA[:, 2:3], in_=stA[:, 1:2], func=AF.Sqrt, bias=EPS, scale=1.0 / 256)
            P.I("dve", "reciprocal", out=stA[:, 2:3], in_=stA[:, 2:3])
            P.I("dve", "scalar_tensor_tensor", out=vg.all, in0=vg.all, scalar=stA[:, 2:3], in1=lnw[:, 0, :], op0=ALU.mult, op1=ALU.mult)
            P.I("dve", "tensor_tensor", out=vl.all, in0=vg.all, in1=lnw[:, 1, :], op=ALU.add)
            bm = rbank()
            for h in range(4):
                P.I("pe", "matmul", out=bm[rows(h), (h // 2) * 128:(h // 2) * 128 + 128], lhsT=vl[:, h * 64:(h + 1) * 64],
                    rhs=WsT[:, h, :], start=True, stop=True)
            P.I("dve", "tensor_tensor", out=tmpA.all, in0=V(bm.ap[:, 0:256].rearrange("p (c t) -> p c t", t=128), bm[:, 0:256].keys), in1=bsb.all, op=ALU.add)
            P.I("dve", "tensor_tensor", out=mixAB[:, 0:2, tt * 128:(tt + 1) * 128], in0=tmpA.all,
                in1=uA[:, :, tt * 128:(tt + 1) * 128], op=ALU.mult)

    def mixer_B(l, g, first):
        w = load_wi(l, 512, 512)
        w2 = load_wi(l, 1024, 512)
        for c in range(2):
            bk = ip_fm(w, c * 128, 128)
            P.I("act", "activation", out=gbB[:, c, :], in_=bk.all, func=AF.Copy)
        for c in range(2):
            bk = ip_fm(w, 256 + c * 128, 128)
            P.I("act", "activation", out=gcB.all, in_=bk.all, func=AF.Copy)
            if first:
                b3 = ip_fm(w, 256 + c * 128, 128, rhs_of_kc=lambda kc: hTh[:, kc, :], N=4)
                P.I("act", "activation", out=haloB[:, c, :], in_=b3[:, 0:4], func=AF.Copy)
                b4 = ip_fm(w2, c * 128, 128, rhs_of_kc=lambda kc: hTh[:, kc, :], N=4)
                P.I("dve", "tensor_tensor", out=haloB[:, c, :], in0=haloB[:, c, :], in1=b4[:, 0:4], op=ALU.mult)
            P.I("dve", "tensor_copy", out=tB[:, c, 0:2], in_=haloB[:, c, 2:4])
            bk2 = ip_fm(w2, c * 128, 128)
            P.I("dve", "tensor_tensor", out=tB[:, c, 2:G + 2], in0=gcB.all, in1=bk2.all, op=ALU.mult)
            k0 = c * 3
            P.I("dve", "tensor_scalar", out=accB.all, in0=tB[:, c, 2:G + 2], scalar1=pc(k0 + 2), scalar2=None, op0=ALU.mult)
            P.I("dve", "scalar_tensor_tensor", out=accB.all, in0=tB[:, c, 1:G + 1], scalar=pc(k0 + 1), in1=accB.all, op0=ALU.mult, op1=ALU.add)
            P.I("dve", "scalar_tensor_tensor", out=accB.all, in0=tB[:, c, 0:G], scalar=pc(k0 + 0), in1=accB.all, op0=ALU.mult, op1=ALU.add)
            P.I("dve", "tensor_tensor", out=mixAB[:, 2 + c, :], in0=accB.all, in1=gbB[:, c, :], op=ALU.mult)
            P.I("dve", "tensor_copy", out=haloB[:, c, 2:4], in_=tB[:, c, G:G + 2])
        return w2

    def wout_part(g, mixbuf, kc0):
        for tt in range(4):
            Tt = g * 4 + tt
            for half in range(2):
                bk = wbank()
                for k in range(4):
                    P.I("pe", "matmul", out=bk.all, lhsT=mixbuf[:, k, tt * 128:(tt + 1) * 128],
                        rhs=wo[:, kc0 + k, half * 512:(half + 1) * 512], start=(k == 0), stop=(k == 3))
                P.I("dve", "tensor_tensor", out=xs[:, Tt, half * 512:(half + 1) * 512],
                    in0=xs[:, Tt, half * 512:(half + 1) * 512], in1=bk.all, op=ALU.add)

    def mixer_D(l, g, first):
        w = load_wi(l, C_DQ, 512)
        for c in range(2):
            bk = ip_fm(w, c * 128, 128)
            P.I("act", "activation", out=qfD[:, c, :], in_=bk.all, func=AF.Copy)
        for c in range(2):
            bk = ip_fm(w, 256 + c * 128, 128)
            P.I("act", "activation", out=kfD[:, c, :], in_=bk.all, func=AF.Copy)
        w = load_wi(l, C_DV, 528)
        for tt in range(4):
            bk = ip_tm(w, 0, 256, tt)
            P.I("act", "activation", out=vtD[:, tt, :], in_=bk[:, 0:256], func=AF.Copy)
        bk = ip_fm(w, 256, 16)
        P.I("act", "activation", out=glrD.all, in_=bk[0:16, :], func=AF.Copy)
        for c in range(2):
            bk = ip_fm(w, 272 + c * 128, 128)
            P.I("act", "activation", out=zsD[:, c, :], in_=bk.all, func=AF.Silu)
        if DBG.get("dcut", 99) <= 1:
            return
        for c in range(2):
            bk = mbank()
            P.I("pe", "matmul", out=bk.all, lhsT=wg2b[:, c * 128:(c + 1) * 128], rhs=glrD.all, start=True, stop=True)
            P.I("act", "activation", out=efD.all, in_=bk.all, func=AF.Exp, bias=pc(34 + c), scale=-1.0)
            P.I("act", "activation", out=efD.all, in_=efD.all, func=AF.Ln, bias=1.0, scale=1.0)
            P.I("dve", "tensor_tensor_scan", out=csD[:, c, :], data0=rmask.all, data1=efD.all, initial=0.0, op0=ALU.mult, op1=ALU.add)
            cs3 = V(csD.ap[:, c, :].rearrange("p (n t) -> p n t", t=64), csD[:, c, :].keys)
            d13 = V(d1D.ap.rearrange("p (n t) -> p n t", t=64), d1D.all.keys)
            P.I("dve", "tensor_tensor", out=d13, in0=cs3, in1=cs3[:, :, 32:33].b([128, 8, 64]), op=ALU.subtract)
            P.I("act", "activation", out=aeD.all, in_=d1D.all, func=AF.Exp, scale=-1.0 / 16)
            P.I("dve", "scalar_tensor_tensor", out=qaD[:, c, :], in0=qfD[:, c, :], scalar=0.125, in1=aeD.all, op0=ALU.mult, op1=ALU.mult)
            P.I("act", "activation", out=aeD.all, in_=d1D.all, func=AF.Exp, scale=1.0 / 16)
            P.I("dve", "tensor_tensor", out=kaD[:, c, :], in0=kfD[:, c, :], in1=aeD.all, op=ALU.mult)
            P.I("act", "activation", out=a3D[:, c, :], in_=csD[:, c, :], func=AF.Exp, scale=-1.0 / 16)
            P.I("dve", "scalar_tensor_tensor", out=qgD[:, c, :], in0=qfD[:, c, :], scalar=0.125, in1=a3D[:, c, :], op0=ALU.mult, op1=ALU.mult)
            P.I("dve", "tensor_tensor", out=d13, in0=cs3, in1=cs3[:, :, 63:64].b([128, 8, 64]), op=ALU.subtract)
            P.I("act", "activation", out=aeD.all, in_=d1D.all, func=AF.Exp, scale=1.0 / 16)
            P.I("dve", "tensor_tensor", out=klD[:, c, :], in0=kfD[:, c, :], in1=aeD.all, op=ALU.mult)
        if DBG.get("dcut", 99) <= 2:
            return
        a34 = V(a3D.ap.rearrange("p c (n t) -> p c n t", t=64), a3D.all.keys)
        for n in range(8):
            ng = g * 8 + n
            P.I("dve", "tensor_tensor", out=DcD[:, :, ng + 1], in0=DcD[:, :, ng], in1=a34[:, :, n, 63], op=ALU.mult)
        for c in range(2):
            q3 = V(qgD.ap[:, c, :].rearrange("p (n t) -> p n t", t=64), qgD[:, c, :].keys)
            o3 = V(qgDD.ap[:, c, :].rearrange("p (n t) -> p n t", t=64), qgDD[:, c, :].keys)
            P.I("dve", "tensor_tensor", out=o3, in0=q3, in1=DcD[:, c, g * 8:g * 8 + 8][:, :, None].b([128, 8, 64]), op=ALU.mult)
        if DBG.get("dcut", 99) <= 3:
            return
        for tt in range(4):
            for c in range(2):
                P.I("pe", "transpose", out=pT[:, c, :], in_=klD[:, c, tt * 128:(tt + 1) * 128], identity=identb.all)
            P.I("act", "activation", out=kltD[:, tt, :], in_=pT[:, 0:2, :], func=AF.Copy)
        if DBG.get("dcut", 99) <= 4:
            return
        for tt in range(4):
            tok = slice(tt * 128, (tt + 1) * 128)
            at = atD[tt % 2]
            bkp = [rbank(), rbank()]
            for h in range(4):
                P.I("pe", "matmul", out=bkp[h % 2][:, (h // 2) * 128:(h // 2) * 128 + 128], lhsT=kaD[rows(h), h // 2, tok], rhs=qaD[rows(h), h // 2, tok],
                    start=True, stop=True)
            at4 = V(at.ap.rearrange("p (c r) s -> p c r s", r=2), at.all.keys)
            for par in range(2):
                P.I("dve", "tensor_tensor", out=at4[:, :, par, :], in0=V(bkp[par].ap[:, 0:256].rearrange("p (c s) -> p c s", s=128), bkp[par][:, 0:256].keys),
                    in1=BDUi[:, None, :].b([128, 2, 128]), op=ALU.mult)
            if DBG.get("dsub", 9) <= 1:
                continue
            bsn = [rbank(), rbank()]
            for n2 in range(2):
                rn = slice(n2 * 64, n2 * 64 + 64)
                for h in range(4):
                    P.I("pe", "matmul", out=bsn[n2][rows(h), (h // 2) * 64:(h // 2) * 64 + 64],
                        lhsT=kltD[rn, tt, h * 64:(h + 1) * 64], rhs=vtD[rn, tt, h * 64:(h + 1) * 64], start=True, stop=True)
            Sb = SbD[tt % 2]
            for n2 in range(2):
                n = tt * 2 + n2
                P.I("act", "activation", out=Sb[:, n2, :, :], in_=SD.all, func=AF.Copy)
                for hp in range(2):
                    P.I("dve", "scalar_tensor_tensor", out=SD[:, hp, :], in0=SD[:, hp, :], scalar=a34[:, hp, n, 63:64],
                        in1=bsn[n2][:, hp * 64:hp * 64 + 64], op0=ALU.mult, op1=ALU.add)
            if DBG.get("dsub", 9) <= 2:
                continue
            pop = [pb[6], rbank()]
            pox = [rbank(), rbank()]
            for h in range(4):
                hp = h // 2
                P.I("pe", "matmul", out=pop[h % 2][rows(h), hp * 128:(hp + 1) * 128], lhsT=vtD[:, tt, h * 64:(h + 1) * 64], rhs=at[:, h, :],
                    start=True, stop=True)
                for n2 in range(2):
                    P.I("pe", "matmul", out=pox[h % 2][rows(h), hp * 128 + n2 * 64:hp * 128 + n2 * 64 + 64], lhsT=Sb[rows(h), n2, hp, :],
                        rhs=qgD[rows(h), hp, tt * 128 + n2 * 64:tt * 128 + n2 * 64 + 64], start=True, stop=True)
            for par in range(2):
                pr = slice(par * 64, par * 64 + 64)
                P.I("act", "activation", out=ostD[pr, :, tok], in_=V(pop[par].ap[pr, 0:256].rearrange("p (c t) -> p c t", t=128), pop[par][:, 0:256].keys), func=AF.Copy)
                P.I("dve", "tensor_tensor", out=ostD[pr, :, tok], in0=ostD[pr, :, tok],
                    in1=V(pox[par].ap[pr, 0:256].rearrange("p (c t) -> p c t", t=128), pox[par][:, 0:256].keys), op=ALU.add)
        if DBG.get("dcut", 99) <= 5:
            return
        P.I("sp", "dma_start", out=dv(dfo_d[g][:, 1], "dfo%d" % g), in_=ostD.all)
        P.I("sp", "dma_start", out=dv(dfb_d[g][:, 1], "dfb%d" % g), in_=qgDD.all)
        P.I("sp", "dma_start", out=dv(dfb_d[g][:, 3], "dfb%d" % g), in_=zsD.all)

    def mixer_C(l, g, first, w2):
        wk = load_wi(l, C_CK, 512)
        srcs = [(w2, 256), (w2, 384), (wk, 0), (wk, 128), (wk, 256), (wk, 384)]
        for ci, (w, c0) in enumerate(srcs):
            bk = ip_fm(w, c0, 128)
            P.I("act", "activation", out=cbC[:, 3:G + 3], in_=bk.all, func=AF.Copy)
            if first:
                b3 = ip_fm(w, c0, 128, rhs_of_kc=lambda kc: hTh[:, kc, :], N=4)
                P.I("act", "activation", out=haloC[:, ci, :], in_=b3[:, 0:4], func=AF.Copy)
            P.I("dve", "tensor_copy", out=cbC[:, 0:3], in_=haloC[:, ci, 1:4])
            k0 = 6 + ci * 4
            P.I("act", "activation", out=accC.all, in_=cbC[:, 3:G + 3], func=AF.Copy, scale=pc(k0 + 3))
            for k in range(3):
                P.I("dve", "scalar_tensor_tensor", out=accC.all, in0=cbC[:, k:G + k], scalar=pc(k0 + k), in1=accC.all, op0=ALU.mult, op1=ALU.add)
            P.I("dve", "tensor_copy", out=haloC[:, ci, 1:4], in_=cbC[:, G:G + 3])
            if ci < 4:
                P.I("act", "activation", out=qsC.all, in_=accC.all, func=AF.Silu)
                P.I("act", "activation", out=sqC.all, in_=qsC.all, func=AF.Square)
                bq = rbank()
                P.I("pe", "matmul", out=bq.all, lhsT=bonesb.all, rhs=sqC.all, start=True, stop=True)
                P.I("act", "activation", out=rtC.all, in_=bq.all, func=AF.Sqrt, bias=EPS, scale=1.0)
                P.I("dve", "reciprocal", out=rtC.all, in_=rtC.all)
                if ci < 2:
                    P.I("dve", "scalar_tensor_tensor", out=qTb[:, ci, :], in0=qsC.all, scalar=0.125, in1=rtC.all, op0=ALU.mult, op1=ALU.mult)
                else:
                    P.I("dve", "tensor_tensor", out=kTb[:, ci - 2, :], in0=qsC.all, in1=rtC.all, op=ALU.mult)
            else:
                P.I("act", "activation", out=vTb[:, ci - 4, :], in_=accC.all, func=AF.Silu)
        if DBG.get("ccut", 99) <= 1:
            return
        wz = load_wi(l, C_CA, 264)
        bka = ip_fm(wz, 0, 4)
        P.I("act", "activation", out=t1C.all, in_=bka[0:4, :], func=AF.Exp, bias=p4[:, 1:2], scale=1.0)
        P.I("act", "activation", out=t1C.all, in_=t1C.all, func=AF.Ln, bias=1.0, scale=1.0)
        P.I("dve", "tensor_scalar", out=t1C.all, in0=t1C.all, scalar1=p4[:, 2:3], scalar2=None, op0=ALU.mult)
        P.I("dve", "tensor_tensor_scan", out=gcC.all, data0=rmask[0:4, :], data1=t1C.all, initial=0.0, op0=ALU.mult, op1=ALU.add)
        bkb = ip_fm(wz, 4, 4)
        P.I("act", "activation", out=btC.all, in_=bkb[0:4, :], func=AF.Exp, scale=-1.0)
        P.I("dve", "tensor_scalar", out=btC.all, in0=btC.all, scalar1=1.0, scalar2=None, op0=ALU.add)
        P.I("dve", "reciprocal", out=btC.all, in_=btC.all)
        for c in range(2):
            bk = ip_fm(wz, 8 + c * 128, 128)
            P.I("act", "activation", out=zsC[:, c, :], in_=bk.all, func=AF.Silu)
        gc3 = V(gcC.ap.rearrange("p (n t) -> p n t", t=64), gcC.all.keys)
        t13 = V(t1C.ap.rearrange("p (n t) -> p n t", t=64), t1C.all.keys)
        P.I("dve", "tensor_tensor", out=t13, in0=gc3[:, :, 63:64].b([4, 8, 64]), in1=gc3, op=ALU.subtract)
        P.I("act", "activation", out=t1C.all, in_=t1C.all, func=AF.Exp)
        P.I("act", "activation", out=egC.all, in_=gcC.all, func=AF.Exp)
        if DBG.get("ccut", 99) <= 2:
            return
        for hp in range(2):
            bb = rbank()
            P.I("pe", "matmul", out=bb.all, lhsT=ehp[:, hp, :], rhs=btC.all, start=True, stop=True)
            P.I("dve", "tensor_tensor", out=kbTb[:, hp, :], in0=kTb[:, hp, :], in1=bb.all, op=ALU.mult)
            be = rbank()
            P.I("pe", "matmul", out=be.all, lhsT=ehp[:, hp, :], rhs=egC.all, start=True, stop=True)
            P.I("dve", "tensor_tensor", out=qgTb[:, hp, :], in0=qTb[:, hp, :], in1=be.all, op=ALU.mult)
            bd = rbank()
            P.I("pe", "matmul", out=bd[:, 0:8], lhsT=ehp[:, hp, :], rhs=V(egC.ap.rearrange("p (n t) -> p n t", t=64)[:, :, 63], egC.all.keys),
                start=True, stop=True)
            P.I("act", "activation", out=decb[:, hp, :], in_=bd[:, 0:8], func=AF.Copy)
        if DBG.get("ccut", 99) <= 3:
            return
        for tt in range(4):
            tok = slice(tt * 128, (tt + 1) * 128)
            bt_ = rbank()
            for qi, src in enumerate((gcC, btC, t1C, egC)):
                P.I("pe", "transpose", out=bt_[:, qi * 4:qi * 4 + 4], in_=src[:, tok], identity=ident[0:4, 0:4])
            P.I("act", "activation", out=tokq[:, tt, :, :], in_=V(bt_.ap[:, 0:16].rearrange("p (q h) -> p q h", h=4), bt_[:, 0:16].keys), func=AF.Copy)
            P.I("dve", "tensor_tensor", out=tokb[:, tt, :], in0=tokq[:, tt, 1, :], in1=tokq[:, tt, 3, :], op=ALU.mult)
        if DBG.get("ccut", 99) <= 4:
            return
        for i in range(2):
            P.I("dve", "memset", ap=uaug[i][:, :, 64:128], constant=0.0)
        f4 = lambda t_: V(t_.ap.rearrange("p h s -> p (h s)"), t_.all.keys)
        v4 = lambda t_: V(t_.ap.rearrange("p (c r) s -> p c r s", r=2), t_.all.keys)

        def c_prep(tt, sl):
            tok = slice(tt * 128, (tt + 1) * 128)
            bA, bB, bC = bufA[sl], bufB[sl], bufC[sl]
            N_, NT_, TT_ = Nb[sl], NTb[sl], TTb[sl]
            for c in range(2):
                P.I("pe", "transpose", out=pT[:, c, :], in_=kTb[:, c, tok], identity=identb.all)
                P.I("pe", "transpose", out=pT[:, 2 + c, :], in_=vTb[:, c, tok], identity=identb.all)
            kt4 = V(pT.ap[:, 0:2, :].rearrange("p c (h d) -> p (c h) d", d=64), pT[:, 0:2, :].keys)
            vt4 = V(pT.ap[:, 2:4, :].rearrange("p c (h d) -> p (c h) d", d=64), pT[:, 2:4, :].keys)
            P.I("dve", "tensor_tensor", out=vbt[sl].all, in0=vt4, in1=tokq[:, tt, 1, :][:, :, None].b([128, 4, 64]), op=ALU.mult)
            P.I("dve", "tensor_tensor", out=kbgt[sl].all, in0=kt4, in1=tokb[:, tt, :][:, :, None].b([128, 4, 64]), op=ALU.mult)
            P.I("dve", "tensor_tensor", out=kdt[sl].all, in0=kt4, in1=tokq[:, tt, 2, :][:, :, None].b([128, 4, 64]), op=ALU.mult)
            yield
            if DBG.get("pcut", 99) <= 1:
                return
            P.I("dve", "tensor_tensor", out=bA.all, in0=tokq[:, tt, 0, :][:, :, None].b([128, 4, 128]), in1=ident[:, None, :].b([128, 4, 128]), op=ALU.mult)
            bg = rbank()
            P.I("pe", "matmul", out=bg.all, lhsT=ones_f, rhs=f4(bA), start=True, stop=True)
            bg3 = V(bg.ap.rearrange("p (h s) -> p h s", s=128), bg.all.keys)
            P.I("dve", "tensor_tensor", out=bB.all, in0=bg3, in1=tokq[:, tt, 0, :][:, :, None].b([128, 4, 128]), op=ALU.subtract)
            yield
            P.I("dve", "tensor_scalar", out=bC.all, in0=bB.all, scalar1=3.0e38, scalar2=0.0, op0=ALU.min, op1=ALU.max)
            P.I("act", "activation", out=bC.all, in_=bC.all, func=AF.Exp, scale=-1.0)
            P.I("dve", "tensor_scalar", out=bA.all, in0=bB.all, scalar1=0.0, scalar2=-3.0e38, op0=ALU.min, op1=ALU.max)
            P.I("act", "activation", out=bA.all, in_=bA.all, func=AF.Exp, scale=1.0)
            yield
            P.I("dve", "tensor_tensor", out=bC.all, in0=bC.all, in1=nBDLs[:, None, :].b([128, 4, 128]), op=ALU.mult)
            P.I("dve", "tensor_tensor", out=bB.all, in0=bA.all, in1=nBDUs[:, None, :].b([128, 4, 128]), op=ALU.mult)
            P.I("dve", "tensor_tensor", out=bA.all, in0=bA.all, in1=BDUi[:, None, :].b([128, 4, 128]), op=ALU.mult)
            if DBG.get("pcut", 99) <= 2:
                return
            for (lt, rt_, dst, msk) in ((kbTb, kTb, N_[0], bC), (kTb, kbTb, NT_[0], bB), (kTb, qTb, atC[sl], bA)):
                bkp = [rbank(), rbank()]
                for h in range(4):
                    hp = h // 2
                    P.I("pe", "matmul", out=bkp[h % 2][:, hp * 128:(hp + 1) * 128], lhsT=lt[rows(h), hp, tok], rhs=rt_[rows(h), hp, tok], start=True, stop=True)
                for par in range(2):
                    P.I("dve", "tensor_tensor", out=v4(dst)[:, :, par, :], in0=V(bkp[par].ap[:, 0:256].rearrange("p (c s) -> p c s", s=128), bkp[par][:, 0:256].keys),
                        in1=v4(msk)[:, :, par, :], op=ALU.mult)
                yield
            if DBG.get("pcut", 99) <= 3:
                return
            P.I("dve", "tensor_tensor", out=TT_[0].all, in0=NT_[0].all, in1=ident[:, None, :].b([128, 4, 128]), op=ALU.add)
            cur = 0
            for k in range(1, 6):
                bn = rbank()
                for h in range(4):
                    P.I("pe", "matmul", out=bn[:, h * 128:(h + 1) * 128], lhsT=NT_[cur][:, h, :], rhs=N_[cur][:, h, :], start=True, stop=True)
                P.I("act", "activation", out=f4(N_[1 - cur]), in_=bn.all, func=AF.Copy)
                if k < 5:
                    bnt = rbank()
                    for h in range(4):
                        P.I("pe", "matmul", out=bnt[:, h * 128:(h + 1) * 128], lhsT=N_[cur][:, h, :], rhs=NT_[cur][:, h, :], start=True, stop=True)
                    P.I("act", "activation", out=f4(NT_[1 - cur]), in_=bnt.all, func=AF.Copy)
                cur = 1 - cur
                yield
                bt2 = rbank()
                tcur = (k - 1) % 2
                for h in range(4):
                    P.I("pe", "matmul", out=bt2[:, h * 128:(h + 1) * 128], lhsT=N_[cur][:, h, :], rhs=TT_[tcur][:, h, :], start=True, stop=True)
                P.I("dve", "tensor_tensor", out=f4(TT_[1 - tcur]), in0=bt2.all, in1=f4(TT_[tcur]), op=ALU.add)
                yield
            TT = TT_[1]
            if DBG.get("pcut", 99) <= 4:
                return
            bu = rbank()
            for h in range(4):
                P.I("pe", "matmul", out=bu[:, h * 64:(h + 1) * 64], lhsT=TT[:, h, :], rhs=vbt[sl][:, h, :], start=True, stop=True)
            P.I("act", "activation", out=uaug[sl][:, :, 0:64], in_=V(bu.ap[:, 0:256].rearrange("p (h e) -> p h e", e=64), bu[:, 0:256].keys), func=AF.Copy)
            bw = rbank()
            for h in range(4):
                P.I("pe", "matmul", out=bw[rows(h), (h // 2) * 128:(h // 2) * 128 + 128], lhsT=kbgt[sl][:, h, :], rhs=TT[:, h, :], start=True, stop=True)
            P.I("act", "activation", out=wTb[sl].all, in_=V(bw.ap[:, 0:256].rearrange("p (c t) -> p c t", t=128), bw[:, 0:256].keys), func=AF.Copy)
            yield

        def c_scan_out(tt, sl):
            tok = slice(tt * 128, (tt + 1) * 128)
            Xb = XbC[0]
            vn = vnb[0]
            for n2 in range(2):
                n = tt * 2 + n2
                rn = slice(n2 * 64, n2 * 64 + 64)
                P.I("act", "activation", out=Xb[:, n2, :, :], in_=XC.all, func=AF.Copy)
                bvp = [pb[6], rbank()]
                for h in range(4):
                    P.I("pe", "matmul", out=bvp[h % 2][rn, (h // 2) * 128:(h // 2) * 128 + 128], lhsT=wTb[sl][rows(h), h // 2, n2 * 64:n2 * 64 + 64],
                        rhs=Xb[rows(h), n2, h // 2, :], start=True, stop=True)
                vn4 = V(vn.ap.rearrange("p (c r) f -> p c r f", r=2), vn.all.keys)
                ua4 = V(uaug[sl].ap.rearrange("p (c r) f -> p c r f", r=2), uaug[sl].all.keys)
                for par in range(2):
                    P.I("dve", "tensor_tensor", out=vn4[rn, :, par, :], in0=ua4[rn, :, par, :],
                        in1=V(bvp[par].ap[rn, 0:256].rearrange("p (c f) -> p c f", f=128), bvp[par][:, 0:256].keys), op=ALU.subtract)
                bd2 = rbank()
                for h in range(4):
                    P.I("pe", "matmul", out=bd2[rows(h), (h // 2) * 128:(h // 2) * 128 + 128], lhsT=kdt[sl][rn, h, :],
                        rhs=vn[rn, h, :], start=True, stop=True)
                for hp in range(2):
                    P.I("dve", "scalar_tensor_tensor", out=XC[:, hp, :], in0=XC[:, hp, :], scalar=decb[:, hp, n:n + 1],
                        in1=bd2[:, hp * 128:(hp + 1) * 128], op0=ALU.mult, op1=ALU.add)
            pop = [pb[6], rbank()]
            pox = [rbank(), rbank()]
            for part in range(2):
                for h in range(4):
                    hp = h // 2
                    base = part * 256 + hp * 128
                    P.I("pe", "matmul", out=pop[h % 2][rows(h), base:base + 128], lhsT=vn[:, h, part * 64:part * 64 + 64], rhs=atC[sl][:, h, :],
                        start=True, stop=True)
                    for n2 in range(2):
                        P.I("pe", "matmul", out=pox[h % 2][rows(h), base + n2 * 64:base + n2 * 64 + 64],
                            lhsT=Xb[rows(h), n2, hp, part * 64:part * 64 + 64],
                            rhs=qgTb[rows(h), hp, tt * 128 + n2 * 64:tt * 128 + n2 * 64 + 64], start=True, stop=True)
            for par in range(2):
                pr = slice(par * 64, par * 64 + 64)
                P.I("act", "activation", out=ostC[pr, :, tok], in_=V(pop[par].ap[pr, 0:256].rearrange("p (c t) -> p c t", t=128), pop[par][:, 0:256].keys), func=AF.Copy)
                P.I("dve", "tensor_tensor", out=ostC[pr, :, tok], in0=ostC[pr, :, tok],
                    in1=V(pox[par].ap[pr, 0:256].rearrange("p (c t) -> p c t", t=128), pox[par][:, 0:256].keys), op=ALU.add)
                P.I("act", "activation", out=accC[pr, 0:256], in_=pop[par][pr, 256:512], func=AF.Copy)
                P.I("dve", "tensor_tensor", out=PTC[pr, :, tok], in0=V(accC.ap[pr, 0:256].rearrange("p (c t) -> p c t", t=128), accC[:, 0:256].keys),
                    in1=V(pox[par].ap[pr, 256:512].rearrange("p (c t) -> p c t", t=128), pox[par][:, 256:512].keys), op=ALU.add)

        for pair in range(2):
            gens = [c_prep(pair * 2 + i, i) for i in range(2)]
            alive = [True, True]
            while any(alive):
                for i in range(2):
                    if alive[i]:
                        try:
                            next(gens[i])
                        except StopIteration:
                            alive[i] = False
            for i in range(2):
                if DBG.get("pcut", 99) > 5:
                    c_scan_out(pair * 2 + i, i)
        P.I("sp", "dma_start", out=dv(dfo_d[g][:, 0], "dfo%d" % g), in_=ostC.all)
        P.I("sp", "dma_start", out=dv(dfb_d[g][:, 0], "dfb%d" % g), in_=PTC.all)
        P.I("sp", "dma_start", out=dv(dfb_d[g][:, 2], "dfb%d" % g), in_=zsC.all)

    def exchange(l):
        P.I("dve", "tensor_copy", out=packS[:, 0:256], in_=V(XC.ap.rearrange("p c f -> p (c f)"), XC.all.keys))
        P.I("dve", "tensor_copy", out=packS[:, 256:384], in_=V(SD.ap.rearrange("p c f -> p (c f)"), SD.all.keys))
        P.I("dve", "tensor_copy", out=packS[:, 384:386], in_=DcD[:, :, 32])
        if local_only and l == layers[-1]:
            P.I("sp", "dma_start", out=dv(pack_o, "pack"), in_=packS.all)
            return False
        if use_cc:
            P.I("sp", "dma_start", out=dv(packd, "packd"), in_=packS.all)
            P.custom("pool", lambda e: e.collective_compute("AllGather", op=ALU.bypass, replica_groups=[[0, 1, 2, 3], [4, 5, 6, 7]],
                                                            ins=[packd.opt()], outs=[gathd.opt()]),
                     reads=DK("packd"), writes=DK("gathd"), dma=True, cc=True)
            P.I("sp", "dma_start", out=gat.all, in_=dv(gathd.rearrange("(r p) f -> p r f", p=128), "gathd"))
        else:
            P.I("sp", "dma_start", out=gat.all, in_=dv(gath_d.rearrange("r p f -> p r f")))
        P.I("dve", "memset", ap=accS.all, constant=0.0)
        for j in range(1, 4):
            gC = V(gat.ap[:, j - 1, 0:256].rearrange("p (c f) -> p c f", f=128), gat[:, j - 1, 0:256].keys)
            gD = V(gat.ap[:, j - 1, 256:384].rearrange("p (c f) -> p c f", f=64), gat[:, j - 1, 256:384].keys)
            if j == 1:
                P.I("dve", "tensor_copy", out=curC.all, in_=gC[:, :, 0:64])
                P.I("dve", "tensor_copy", out=curD.all, in_=gD)
            else:
                bt_ = rbank()
                for hp in range(2):
                    P.I("pe", "transpose", out=bt_[0:64, hp * 128:(hp + 1) * 128], in_=gC[:, hp, 64:128], identity=ident)
                P.I("act", "activation", out=MTf.all, in_=V(bt_.ap[0:64, 0:256].rearrange("p (c f) -> p c f", f=128), bt_[:, 0:256].keys), func=AF.Copy)
                bs_ = rbank()
                for hp in range(2):
                    P.I("pe", "matmul", out=bs_[0:64, hp * 64:(hp + 1) * 64], lhsT=ident[:, 64:128], rhs=curC[:, hp, :], start=True, stop=True)
                P.I("act", "activation", out=curF.all, in_=V(bs_.ap[0:64, 0:128].rearrange("p (c f) -> p c f", f=64), bs_[:, 0:128].keys), func=AF.Copy)
                bm = rbank()
                for h in range(4):
                    hp = h // 2
                    rhs_ = curC[0:64, hp, :] if h % 2 == 0 else curF[:, hp, :]
                    P.I("pe", "matmul", out=bm[rows(h), hp * 64:(hp + 1) * 64], lhsT=MTf[:, hp, (h % 2) * 64:(h % 2) * 64 + 64], rhs=rhs_, start=True, stop=True)
                P.I("dve", "tensor_tensor", out=curC.all, in0=gC[:, :, 0:64], in1=V(bm.ap[:, 0:128].rearrange("p (c f) -> p c f", f=64), bm[:, 0:128].keys), op=ALU.add)
                for hp in range(2):
                    P.I("dve", "scalar_tensor_tensor", out=curD[:, hp, :], in0=curD[:, hp, :], scalar=gat[:, j - 1, 384 + hp:385 + hp],
                        in1=gD[:, hp, :], op0=ALU.mult, op1=ALU.add)
            P.I("dve", "scalar_tensor_tensor", out=accS[:, 0], in0=curC.all, scalar=sel[:, j:j + 1], in1=accS[:, 0], op0=ALU.mult, op1=ALU.add)
            P.I("dve", "scalar_tensor_tensor", out=accS[:, 1], in0=curD.all, scalar=sel[:, j:j + 1], in1=accS[:, 1], op0=ALU.mult, op1=ALU.add)
        P.I("act", "activation", out=SinB.all, in_=accS.all, func=AF.Copy)
        return True

    def post_group(l, g):
        P.I("sp", "dma_start", out=ost.all, in_=dv(dfo_d[g], "dfo%d" % g))
        P.I("sp", "dma_start", out=dbf.all, in_=dv(dfb_d[g], "dfb%d" % g))
        for m in range(2):
            if ("C" in skip and m == 0) or ("D" in skip and m == 1):
                P.I("dve", "memset", ap=mixCD[:, m * 2:m * 2 + 2, :], constant=0.0)
                continue
            for hp in range(2):
                for hh in range(2):
                    h = hp * 2 + hh
                    bk = rbank()
                    P.I("pe", "matmul", out=bk[rows(h), :], lhsT=SinB[rows(h), m, hp, :], rhs=dbf[rows(h), m, hp, :], start=True, stop=True)
                    P.I("dve", "tensor_tensor", out=ofQ[rows(h), :], in0=ost[rows(h), m, hp, :], in1=bk[rows(h), :], op=ALU.add)
                P.I("act", "activation", out=sqQ.all, in_=ofQ.all, func=AF.Square)
                bq = rbank()
                P.I("pe", "matmul", out=bq.all, lhsT=bonesb.all, rhs=sqQ.all, start=True, stop=True)
                P.I("act", "activation", out=rtQ.all, in_=bq.all, func=AF.Sqrt, bias=EPS, scale=1.0 / 64)
                P.I("dve", "reciprocal", out=rtQ.all, in_=rtQ.all)
                P.I("dve", "scalar_tensor_tensor", out=t1Q.all, in0=ofQ.all, scalar=pc(32 + m), in1=rtQ.all, op0=ALU.mult, op1=ALU.mult)
                P.I("dve", "tensor_tensor", out=mixCD[:, m * 2 + hp, :], in0=t1Q.all, in1=dbf[:, 2 + m, hp, :], op=ALU.mult)
        wout_part(g, mixCD, 4)

    nw_idx = {0: (0, 1), 1: (2, 3)}
    for li, l in enumerate(layers):
        load_nw(nw_idx[l][0])
        layer_setup(l)
        halo_prep(l, dv(xh_d) if (li == 0 or not use_cc) else None)
        for g in range(NG):
            norm_group(g, None)
            if "A" in skip:
                P.I("dve", "memset", ap=mixAB[:, 0:2, :], constant=0.0)
            else:
                mixer_A(l, g, g == 0)
            w2 = mixer_B(l, g, g == 0)
            wout_part(g, mixAB, 0)
            if "C" not in skip:
                mixer_C(l, g, g == 0, w2)
            if "D" not in skip:
                mixer_D(l, g, g == 0)
        if "X" not in skip:
            cont = exchange(l)
            if not cont:
                break
        load_nw(nw_idx[l][1])
        for g in range(NG):
            if "X" not in skip:
                post_group(l, g)
            norm_group(g, None)
            ffn_group(l, g, SF)
        if use_cc and li + 1 < len(layers):
            P.I("sp", "dma_start", out=dv(xsend, "xsend"), in_=xs[124:128, 15, :])
            P.custom("pool", lambda e: e.collective_compute("AllGather", op=ALU.bypass, replica_groups=[[0, 1, 2, 3], [4, 5, 6, 7]],
                                                            ins=[xsend.opt()], outs=[xgath.opt()]),
                     reads=DK("xsend"), writes=DK("xgath"), dma=True, cc=True)
    if local_only:
        allk = DK("pack")
        for g in range(NG):
            allk = allk + DK("dfo%d" % g) + DK("dfb%d" % g)
        P.ops.append(Op("sp", ("nop", None), allk, (), False))
        P.finalize()
        return nc, P
    if final_norm:
        load_nw(4)
        for Tt in range(16):
            P.I("act", "activation", out=junk.all, in_=xs[:, Tt, :], func=AF.Square, accum_out=ss[:, Tt:Tt + 1])
            P.I("act", "activation", out=rstd[:, Tt:Tt + 1], in_=ss[:, Tt:Tt + 1], func=AF.Sqrt, bias=EPS, scale=1.0 / D_MODEL)
            P.I("dve", "reciprocal", out=rstd[:, Tt:Tt + 1], in_=rstd[:, Tt:Tt + 1])
            P.I("dve", "scalar_tensor_tensor", out=xs[:, Tt, :], in0=xs[:, Tt, :], scalar=rstd[:, Tt:Tt + 1], in1=nwb.all,
                op0=ALU.mult, op1=ALU.mult)
    for Tt in range(16):
        P.I("sp", "dma_start", out=dv(y_o[Tt * 128:(Tt + 1) * 128, :], "y"), in_=xs[:, Tt, :])
    P.ops.append(Op("sp", ("nop", None), DK("y"), (), False))
    P.finalize()
    return nc, P


def host_consts(core):
    seg = core % 4
    c = np.zeros((128, 11, 128), np.float32)
    i = np.arange(128)
    c[:, 0, :] = np.eye(128)
    c[:, 1, :] = (i[None, :] <= i[:, None])
    same = (i[:, None] // 64) == (i[None, :] // 64)
    c[:, 2, :] = same & (i[None, :] >= i[:, None])
    c[:, 3, :] = same & (i[None, :] > i[:, None])
    c[:, 4, :] = same & (i[None, :] < i[:, None])
    c[:, 5, :] = 1.0
    c[:, 6, :] = same
    c[:, 7, 0:64] = (i[:, None] % 64) == np.arange(64)[None, :]
    c[:, 8, seg] = 1.0
    c[:, 9, :] = -1.0 * (same & (i[None, :] > i[:, None]))
    c[:, 10, :] = -1.0 * (same & (i[None, :] < i[:, None]))
    for t in range(4):
        if seg > 0:
            c[(seg - 1) * 4 + t, 8, 4 + t] = 1.0
    ehp = np.zeros((4, 2, 128), np.float32)
    for h in range(4):
        ehp[h, h // 2, (h % 2) * 64:(h % 2) * 64 + 64] = 1.0
    rmask = np.ones((128, 512), np.float32)
    rmask[:, ::64] = 0.0
    return c, ehp, rmask


def host_params(inp):
    pcol = np.zeros((2, 128, 40), np.float32)
    p = np.arange(128)
    for l in range(2):
        scw = inp["sc_conv_w"][l]
        for c in range(2):
            for k in range(3):
                pcol[l, :, c * 3 + k] = scw[k, c * 128 + p]
        dnw = inp["dn_conv_w"][l]
        for c in range(6):
            for k in range(4):
                pcol[l, :, 6 + c * 4 + k] = dnw[k, c * 128 + p]
        for c in range(2):
            pcol[l, :, 30 + c] = inp["gla_gate_bias"][l][c * 128 + p]
        pcol[l, :, 32] = inp["dn_norm_w"][l][p % 64]
        pcol[l, :, 33] = inp["gla_norm_w"][l][p % 64]
    p4 = np.stack([inp["dn_a_log"], inp["dn_dt_bias"]], axis=-1).astype(np.float32)
    lnwb = np.stack([inp["sgu_ln_w"], inp["sgu_ln_b"]], axis=1).astype(np.float32)
    nrm = np.stack([inp["norm1_w"][0], inp["norm2_w"][0], inp["norm1_w"][1], inp["norm2_w"][1],
                    inp["final_norm_w"]], axis=0).astype(np.float32)
    return pcol, p4, lnwb, nrm


def _in_maps(inp, consts, params, xcores, xh, gath):
    pcol, p4, lnwb, nrm = params
    maps = []
    for c in range(NCORES):
        cst, ehp, rmask = consts[c]
        maps.append({
            "x": xcores[c], "xh": xh[c], "w_in": inp["w_in"], "w_out": inp["w_out"],
            "w_gate_up": inp["w_gate_up"], "w_down": inp["w_down"], "nrm": nrm, "pcol": pcol, "p4": p4,
            "lnwb": lnwb, "wsp": inp["sgu_w_spatial"], "bsp": inp["sgu_b_spatial"], "wg2": inp["gla_w_gate2"],
            "cst": cst, "ehp": ehp, "rmask": rmask, "gath": gath[c],
        })
    return maps


def _halo(xcores):
    xh = []
    for c in range(NCORES):
        if c % 4 == 0:
            xh.append(np.zeros((4, D_MODEL), np.float32))
        else:
            xh.append(np.ascontiguousarray(xcores[c - 1][NT - 4:NT]))
    return xh


def kernel(**inputs):
    inp = {k: np.ascontiguousarray(np.asarray(v, dtype=np.float32)) for k, v in inputs.items()}
    x = inp["x"].reshape(2 * SEQ, D_MODEL)
    params = host_params(inp)
    consts = [host_consts(c) for c in range(NCORES)]
    xcores = [np.ascontiguousarray(x[c * NT:(c + 1) * NT]) for c in range(NCORES)]
    zg = [np.zeros((4, 128, 392), np.float32) for _ in range(NCORES)]
    xh = _halo(xcores)
    nc, _ = build([0, 1], False, True, use_cc=True)
    res = run_bass_kernel_spmd(nc, _in_maps(inp, consts, params, xcores, xh, zg), core_ids=list(range(NCORES)))
    out = np.concatenate([np.asarray(r["y"], np.float32) for r in res.results], axis=0)
    return out.reshape(2, SEQ, D_MODEL).astype(np.float32)
```
